# Optimizing a Trainium2 kernel written in Bass

```python
import math
import jax, jax.numpy as jnp
from jax import lax
import numpy as np

D_MODEL = 1024
BATCH = 2
SEQ = 8192
DEPTH = 2

MIX_WIDTH = D_MODEL
MLA_WIDTH = D_MODEL // 2
SGU_WIDTH = D_MODEL // 4
POOL_WIDTH = D_MODEL // 4

MLA_HEADS = 4
V_HEAD = MLA_WIDTH // MLA_HEADS
QK_NOPE = 64
QK_ROPE = 32
QK_HEAD = QK_NOPE + QK_ROPE
Q_LORA = D_MODEL // 4
KV_LORA = D_MODEL // 8
ROPE_THETA = 10000.0
Q_BLOCK = 128

SGU_HEADS = 4
SGU_HEAD_DIM = SGU_WIDTH // SGU_HEADS
CHUNK = 128

POOL_WINDOWS = (2, 4, 8, 16)
POOL_GROUPS = len(POOL_WINDOWS)
POOL_GROUP_DIM = POOL_WIDTH // POOL_GROUPS

IN_WIDTH = Q_LORA + KV_LORA + QK_ROPE + 2 * SGU_WIDTH + POOL_WIDTH
FFN_HIDDEN = -(-8 * D_MODEL // (3 * 256)) * 256
EPS = 1e-6

kernel_name = "hybrid_mla_sgu_pool_block"


def rms_norm(x, g):
    xf = x.astype(jnp.float32)
    y = xf * lax.rsqrt(jnp.mean(xf * xf, axis=-1, keepdims=True) + EPS)
    return (y * g.astype(jnp.float32)).astype(x.dtype)


def apply_rope(x, positions):
    half = x.shape[-1] // 2
    inv_freq = 1.0 / (ROPE_THETA ** (jnp.arange(half, dtype=jnp.float32) / half))
    ang = positions.astype(jnp.float32)[:, :, None, None] * inv_freq
    cos, sin = jnp.cos(ang), jnp.sin(ang)
    xf = x.astype(jnp.float32)
    x1, x2 = xf[..., :half], xf[..., half:]
    return jnp.concatenate([x1 * cos - x2 * sin, x2 * cos + x1 * sin], axis=-1).astype(x.dtype)


def mla_mixer(q_lat, kv_lat, k_rope, positions, g_q_lat, w_q_up, g_kv_lat, w_kv_up, g_q_head, g_k_head):
    B, S, _ = q_lat.shape
    q = (rms_norm(q_lat, g_q_lat) @ w_q_up).reshape(B, S, MLA_HEADS, QK_HEAD)
    kv = (rms_norm(kv_lat, g_kv_lat) @ w_kv_up).reshape(B, S, MLA_HEADS, QK_NOPE + V_HEAD)
    k_nope, v = kv[..., :QK_NOPE], kv[..., QK_NOPE:]
    k_pe = jnp.broadcast_to(k_rope[:, :, None, :], (B, S, MLA_HEADS, QK_ROPE))
    k = jnp.concatenate([k_nope, k_pe], axis=-1)
    q = rms_norm(q, g_q_head)
    k = rms_norm(k, g_k_head)
    q = jnp.concatenate([q[..., :QK_NOPE], apply_rope(q[..., QK_NOPE:], positions)], axis=-1)
    k = jnp.concatenate([k[..., :QK_NOPE], apply_rope(k[..., QK_NOPE:], positions)], axis=-1)

    n_blocks = S // Q_BLOCK
    scale = 1.0 / math.sqrt(QK_HEAD)
    qb = q.reshape(B, n_blocks, Q_BLOCK, MLA_HEADS, QK_HEAD).transpose(1, 0, 2, 3, 4)
    kpos = jnp.arange(S)

    def attend_block(args):
        qi, bi = args
        s = jnp.einsum('bqhd,bkhd->bhqk', qi, k).astype(jnp.float32) * scale
        qpos = bi * Q_BLOCK + jnp.arange(Q_BLOCK)
        causal = kpos[None, :] <= qpos[:, None]
        s = jnp.where(causal[None, None], s, jnp.finfo(jnp.float32).min)
        p = jax.nn.softmax(s, axis=-1)
        return jnp.einsum('bhqk,bkhd->bqhd', p.astype(v.dtype), v)

    o = lax.map(attend_block, (qb, jnp.arange(n_blocks)))
    return o.transpose(1, 0, 2, 3, 4).reshape(B, S, MLA_WIDTH)


def sgu_mixer(uv, g_v, w_spatial, b_spatial):
    B, S, _ = uv.shape
    u, v = uv[..., :SGU_WIDTH], uv[..., SGU_WIDTH:]
    v = rms_norm(v, g_v)
    vc = v.reshape(B, S // CHUNK, CHUNK, SGU_HEADS, SGU_HEAD_DIM)
    w = w_spatial * jnp.tril(jnp.ones((CHUNK, CHUNK), dtype=w_spatial.dtype))
    zc = jnp.einsum('hts,bcshd->bcthd', w, vc) + b_spatial.T[None, None, :, :, None]
    return u * zc.reshape(B, S, SGU_WIDTH)


def pool_mixer(p, w_pool, pool_scale):
    B, S, _ = p.shape
    pf = p.astype(jnp.float32).reshape(B, S, POOL_GROUPS, POOL_GROUP_DIM)
    t1 = jnp.arange(1, S + 1, dtype=jnp.float32)
    outs = []
    for g, win in enumerate(POOL_WINDOWS):
        xg = pf[:, :, g]
        cs = jnp.cumsum(xg, axis=1)
        cs_shift = jnp.pad(cs, ((0, 0), (win, 0), (0, 0)))[:, :S]
        count = jnp.minimum(t1, float(win))[None, :, None]
        outs.append((cs - cs_shift) / count - xg)
    m = jnp.stack(outs, axis=2).astype(p.dtype)
    y = jnp.einsum('bsgc,gcd->bsgd', m, w_pool).reshape(B, S, POOL_WIDTH)
    return y * pool_scale


def setup_inputs(seed: int = 0) -> dict:
    key = jax.random.key(seed)
    ks = jax.random.split(key, 24)
    f32 = jnp.float32

    def dense(k, shape, fan_in):
        return jax.random.normal(k, shape, f32) * fan_in ** -0.5

    def gain(k, shape):
        return 1.0 + 0.01 * jax.random.normal(k, shape, f32)

    x = jax.random.normal(ks[0], (BATCH, SEQ, D_MODEL), f32)
    start = jax.random.randint(ks[1], (BATCH, 1), 0, 4096, dtype=jnp.int32)
    positions = start + jnp.arange(SEQ, dtype=jnp.int32)[None, :]
    return {
        "x": x,
        "positions": positions,
        "g_mix_norm": gain(ks[2], (DEPTH, D_MODEL)),
        "w_in": dense(ks[3], (DEPTH, D_MODEL, IN_WIDTH), D_MODEL),
        "g_q_lat": gain(ks[4], (DEPTH, Q_LORA)),
        "w_q_up": dense(ks[5], (DEPTH, Q_LORA, MLA_HEADS * QK_HEAD), Q_LORA),
        "g_kv_lat": gain(ks[6], (DEPTH, KV_LORA)),
        "w_kv_up": dense(ks[7], (DEPTH, KV_LORA, MLA_HEADS * (QK_NOPE + V_HEAD)), KV_LORA),
        "g_q_head": gain(ks[8], (DEPTH, QK_HEAD)),
        "g_k_head": gain(ks[9], (DEPTH, QK_HEAD)),
        "g_sgu_v": gain(ks[10], (DEPTH, SGU_WIDTH)),
        "w_spatial": dense(ks[11], (DEPTH, SGU_HEADS, CHUNK, CHUNK), CHUNK),
        "b_spatial": 1.0 + 0.01 * jax.random.normal(ks[12], (DEPTH, SGU_HEADS, CHUNK), f32),
        "w_pool": dense(ks[13], (DEPTH, POOL_GROUPS, POOL_GROUP_DIM, POOL_GROUP_DIM), POOL_GROUP_DIM),
        "pool_scale": 1.0 + 0.1 * jax.random.normal(ks[14], (DEPTH, POOL_WIDTH), f32),
        "g_out_mla": gain(ks[15], (DEPTH, MLA_WIDTH)),
        "g_out_sgu": gain(ks[16], (DEPTH, SGU_WIDTH)),
        "g_out_pool": gain(ks[17], (DEPTH, POOL_WIDTH)),
        "w_out": dense(ks[18], (DEPTH, MIX_WIDTH, D_MODEL), MIX_WIDTH),
        "g_ffn_norm": gain(ks[19], (DEPTH, D_MODEL)),
        "w_gate": dense(ks[20], (DEPTH, D_MODEL, FFN_HIDDEN), D_MODEL),
        "w_up": dense(ks[21], (DEPTH, D_MODEL, FFN_HIDDEN), D_MODEL),
        "w_down": dense(ks[22], (DEPTH, FFN_HIDDEN, D_MODEL), FFN_HIDDEN),
    }


def reference(x, positions, g_mix_norm, w_in, g_q_lat, w_q_up, g_kv_lat, w_kv_up, g_q_head, g_k_head,
              g_sgu_v, w_spatial, b_spatial, w_pool, pool_scale, g_out_mla, g_out_sgu, g_out_pool,
              w_out, g_ffn_norm, w_gate, w_up, w_down):
    o1 = Q_LORA
    o2 = o1 + KV_LORA
    o3 = o2 + QK_ROPE
    o4 = o3 + 2 * SGU_WIDTH
    for l in range(DEPTH):
        h = rms_norm(x, g_mix_norm[l])
        z = h @ w_in[l]
        q_lat, kv_lat, k_rope = z[..., :o1], z[..., o1:o2], z[..., o2:o3]
        uv, pin = z[..., o3:o4], z[..., o4:]
        a = mla_mixer(q_lat, kv_lat, k_rope, positions, g_q_lat[l], w_q_up[l], g_kv_lat[l],
                      w_kv_up[l], g_q_head[l], g_k_head[l])
        gm = sgu_mixer(uv, g_sgu_v[l], w_spatial[l], b_spatial[l])
        po = pool_mixer(pin, w_pool[l], pool_scale[l])
        mix = jnp.concatenate([rms_norm(a, g_out_mla[l]), rms_norm(gm, g_out_sgu[l]),
                               rms_norm(po, g_out_pool[l])], axis=-1)
        x = x + mix @ w_out[l]
        h = rms_norm(x, g_ffn_norm[l])
        x = x + (jax.nn.silu(h @ w_gate[l]) * (h @ w_up[l])) @ w_down[l]
    return x
```

```python
import math
import numpy as np
import ml_dtypes
import concourse.bass as bass
import concourse.mybir as mybir
from concourse.bass_utils import run_bass_kernel_spmd

F32 = mybir.dt.float32
BF16 = mybir.dt.bfloat16
I32 = mybir.dt.int32
AF = mybir.ActivationFunctionType
ALU = mybir.AluOpType

D = 1024
NTOK = 2048
NBLK = 16
NGRP = 4
INW = 1184
FFH = 2816
NHT = 22
EPS = 1e-6
PAY_KT = 0
PAY_V = 384
PAY_H = 896
PAY_ROWS = 928
PIECE_ROWS = (224, 224, 224, 224, 32)
NV = 40
FF_CHUNKS = (6, 6, 5, 5)
import os as _os
_SKIP = _os.environ.get('KSKIP', '').split(',')

TWO_PI = 2.0 * math.pi
C1 = 6.28125
C2 = TWO_PI - C1
MAGIC = 12582912.0
PI_SAFE = 3.1415925


class _Rec:
    def __init__(self):
        self.call = None

    def __getattr__(self, name):
        def f(*a, **k):
            self.call = (name, a, k)
            return None
        return f


def _record(fn):
    r = _Rec()
    fn(r)
    assert r.call is not None
    return r.call


class Sched:
    ENG = ('pe', 'act', 'dve', 'pool', 'sp')

    def __init__(self, ndma=16):
        self.q = {e: [] for e in self.ENG}
        self.cnt = {e: 0 for e in self.ENG}
        self.known = {e: {} for e in self.ENG}
        self.w = {}
        self.r = {}
        self.ndma = ndma
        self.dma_val = [0] * ndma
        self.dma_next = 0
        self.extra_val = {}

    def _need(self, e, toks):
        for (s, v) in toks:
            if s == e and e == 'pe':
                continue
            if self.known[e].get(s, 0) < v:
                self.known[e][s] = v
                self.q[e].append(('wait', s, v))

    def _deps(self, reads, writes):
        toks = []
        for r_ in reads:
            if r_ in self.w:
                toks.append(self.w[r_])
        for w_ in writes:
            if w_ in self.w:
                toks.append(self.w[w_])
            toks.extend(self.r.get(w_, {}).items())
        return toks

    def _commit(self, tok, reads, writes):
        for w_ in writes:
            self.w[w_] = tok
            self.r[w_] = {}
        for r_ in reads:
            d = self.r.setdefault(r_, {})
            if d.get(tok[0], 0) < tok[1]:
                d[tok[0]] = tok[1]

    def op(self, e, fn, reads=(), writes=()):
        fn = _record(fn)
        self._need(e, self._deps(reads, writes))
        self.cnt[e] += 1
        tok = (e, self.cnt[e])
        self.q[e].append(('op', fn, e))
        self._commit(tok, reads, writes)

    def dma(self, fn, reads=(), writes=(), q='sp'):
        fn = _record(fn)
        k = self.dma_next
        self.dma_next = (k + 1) % self.ndma
        s = 'dma%d' % k
        toks = self._deps(reads, writes)
        if self.dma_val[k] > 0:
            toks.append((s, self.dma_val[k]))
        self._need(q, toks)
        self.dma_val[k] += 16
        tok = (s, self.dma_val[k])
        self.q[q].append(('dma', fn, s))
        self._commit(tok, reads, writes)

    def special(self, q, semkey, fn, reads=(), writes=()):
        fn = _record(fn)
        toks = self._deps(reads, writes)
        v = self.extra_val.get(semkey, 0)
        if v > 0:
            toks.append((semkey, v))
        self._need(q, toks)
        v += 1
        self.extra_val[semkey] = v
        self.q[q].append(('op', fn, semkey))
        self._commit((semkey, v), reads, writes)

    def barrier(self):
        allt = [(f, self.cnt[f]) for f in self.ENG if self.cnt[f] > 0]
        allt += [('dma%d' % k, v) for k, v in enumerate(self.dma_val) if v > 0]
        for e in self.ENG:
            self._need(e, [t for t in allt if t[0] != e])

    def finish(self):
        allt = [('dma%d' % k, v) for k, v in enumerate(self.dma_val) if v > 0]
        allt += [(f, self.cnt[f]) for f in self.ENG if self.cnt[f] > 0 and f != 'sp']
        allt += list(self.extra_val.items())
        self._need('sp', allt)

    def replay(self, e, eng, sems):
        for item in self.q[e]:
            if item[0] == 'wait':
                eng.wait_ge(sems[item[1]], item[2])
            elif item[0] == 'op':
                getattr(eng, item[1][0])(*item[1][1], **item[1][2]).then_inc(sems[item[2]], 1)
            else:
                getattr(eng, item[1][0])(*item[1][1], **item[1][2]).then_inc(sems[item[2]], 16)


def build(stages, fused=False, dbg=False):
    nc = bass.Bass("TRN2", target_bir_lowering=False)
    S = Sched()

    def din(name, shape, dt=F32):
        return nc.dram_tensor(name, list(shape), dt, kind="ExternalInput").ap()

    def dout(name, shape, dt=F32):
        return nc.dram_tensor(name, list(shape), dt, kind="ExternalOutput").ap()

    xin = din("xin", [D, NTOK])
    pos = din("pos", [1, NTOK], I32)
    cst = din("cst", [128, 2048])
    cst16 = din("cst16", [64, 512])
    vecs = din("vecs", [2, 128, NV])
    w_in = din("w_in", [2, D, INW])
    w_q_up = din("w_q_up", [2, 256, 384])
    w_kv_up = din("w_kv_up", [2, 128, 768])
    g_sgu_v = din("g_sgu_v", [2, 256])
    w_spatial = din("w_spatial", [2, 4, 128, 128])
    b_spatial = din("b_spatial", [2, 512])
    w_pool = din("w_pool", [2, 4, 64, 64])
    w_out = din("w_out", [2, D, D])
    w_gate = din("w_gate", [2, D, FFH])
    w_up = din("w_up", [2, D, FFH])
    w_down = din("w_down", [2, FFH, D])

    layers = sorted({int(s[1]) for s in stages if s[0] in 'ABCX'})
    need_gath = {}
    pay = {}
    for l in layers:
        if fused:
            pay[l] = [nc.dram_tensor("pay%d_%d" % (l, i), [PIECE_ROWS[i], 2048], BF16, kind="Internal").ap() for i in range(5)]
            need_gath[l] = [nc.dram_tensor("gath%d_%d" % (l, i), [4 * PIECE_ROWS[i], 2048], BF16, kind="Internal").ap() for i in range(5)]
        else:
            if ("A%d" % l) in stages and not (("B%d" % l) in stages):
                pay[l] = [dout("pay%d_%d" % (l, i), [PIECE_ROWS[i], 2048], BF16) for i in range(5)]
            if ("B%d" % l) in stages:
                need_gath[l] = [din("gath%d_%d" % (l, i), [4 * PIECE_ROWS[i], 2048], BF16) for i in range(5)]
    xout = dout("xout", [D, NTOK]) if any(s[0] == 'C' for s in stages) else None
    dbgt = {}
    if dbg:
        dbgt['qT'] = dout("dbg_qT", [96, 4 * NTOK], BF16)
        dbgt['gn'] = dout("dbg_gn", [64, 4 * NTOK], BF16)
        dbgt['aT'] = dout("dbg_aT", [128, 4 * NTOK], BF16)
        dbgt['x'] = dout("dbg_x", [D, NTOK])

    from contextlib import ExitStack
    with ExitStack() as es:
        arena_cols = 212000 // 4
        arena = es.enter_context(nc.sbuf_tensor("arena", [128, arena_cols], F32))
        psb = [es.enter_context(nc.psum_tensor("ps%d" % i, [128, 512], F32)) for i in range(8)]
        sems = {}
        for e in Sched.ENG:
            sems[e] = es.enter_context(nc.semaphore("sem_" + e))
        for k in range(S.ndma):
            sems['dma%d' % k] = es.enter_context(nc.semaphore("sem_dma%d" % k))
        for i in range(5):
            sems['cc%d' % i] = es.enter_context(nc.semaphore("sem_cc%d" % i))

        class Bump:
            def __init__(self, lo, hi):
                self.lo, self.hi, self.p = lo, hi, lo

            def take(self, shape, dt, parts=128):
                esz = 4 if dt in (F32, I32) else 2
                n = 1
                for s_ in shape:
                    n *= s_
                nbytes = (n * esz + 63) // 64 * 64
                off = self.p
                self.p += nbytes
                assert self.p <= self.hi, ("SBUF overflow", self.p, self.hi)
                assert (n * esz) % 4 == 0
                ap = arena[0:parts, off // 4:(off + n * esz) // 4]
                if dt != F32:
                    ap = ap.bitcast(dt)
                if len(shape) == 2:
                    ap = ap.rearrange("p (a b) -> p a b", a=shape[0])
                elif len(shape) == 3:
                    ap = ap.rearrange("p (a b c) -> p a b c", a=shape[0], b=shape[1])
                return ap

        TOT = arena_cols * 4
        pers = Bump(0, TOT)
        xT = pers.take([8, NTOK], F32)
        ones_bf = pers.take([128], BF16)
        ident32 = pers.take([128], F32)
        triu32 = pers.take([128], F32)
        maskT = pers.take([4, 128], BF16)
        apool = pers.take([2, 4, 128], BF16)
        ahalo = pers.take([4, 128], BF16, parts=64)
        selpe = pers.take([2, 96], BF16, parts=64)
        ones32 = pers.take([64], F32, parts=1)
        invf = pers.take([1], F32, parts=96)
        vec_sb = pers.take([NV], F32)
        gv_bc = pers.take([256], F32)
        brow32 = pers.take([512], F32, parts=1)
        epsb = pers.take([1], F32)
        negI = pers.take([128], BF16)
        notmask = pers.take([4, 128], BF16)
        Ctab = pers.take([NTOK], F32, parts=96)
        Stab = pers.take([NTOK], F32, parts=96)
        W1_LO = pers.p
        W1_SZ = 26624
        MIX_LO = W1_LO + W1_SZ
        MIX_SZ = 16384 + 4096 + 16384 + 8192
        T_LO = MIX_LO + MIX_SZ
        assert T_LO < TOT
        wA = Bump(W1_LO, W1_LO + W1_SZ)
        w_in_sb = wA.take([8, 1216], BF16)
        wq_sb = wA.take([2, 384], BF16)
        wqR_sb = wA.take([2, 384], BF16)
        wkv_sb = wA.take([768], BF16)
        wk1_sb = wA.take([4, 96], BF16)
        WsT_sb = wA.take([4, 128], BF16)
        wB = Bump(W1_LO, W1_LO + W1_SZ)
        wpool_sb = wB.take([4, 64], BF16, parts=64)
        wo_a = wB.take([4, D], BF16)
        wo_g = wB.take([4, D], BF16, parts=64)
        wo_p = wB.take([4, D], BF16, parts=64)
        mx = Bump(MIX_LO, MIX_LO + MIX_SZ)
        qT = mx.take([4, NTOK], BF16, parts=96)
        aT0 = mx.take([NTOK], BF16)
        gn = mx.take([4, NTOK], BF16, parts=64)
        pin = mx.take([NBLK, 256], BF16)
        qT_off = MIX_LO

        def aT_ap(h):
            if h == 0:
                return aT0
            off = qT_off + (h - 1) * NTOK * 2
            return arena[0:128, off // 4:(off + NTOK * 2) // 4].bitcast(BF16)

        def aT_res(h, g):
            return "aT%d_%d" % (h, g) if h == 0 else "qT%d_%d" % (h - 1, g)

        PSN = [0]

        def ps_next():
            i = PSN[0]
            PSN[0] = (i + 1) % 8
            return psb[i], "ps%d" % i

        inv_sqrt = {}

        def rstd_from(ps_ap, pres, n, P, sd_ap, sdres, out_ap, ores, cols):
            S.op('act', lambda e: e.activation(out=out_ap, in_=ps_ap, func=AF.Ln, bias=epsb[0:P, 0:1], scale=1.0 / n),
                 reads=[pres, "epsb"], writes=[ores])
            S.op('act', lambda e: e.activation(out=out_ap, in_=out_ap, func=AF.Exp, scale=-0.5), reads=[ores], writes=[ores])

        tb = Bump(MIX_LO, MIX_LO + MIX_SZ)
        st_c = tb.take([2048], F32)
        st_c16 = tb.take([512], F32, parts=64)
        S.dma(lambda e: e.dma_start(out=st_c, in_=cst), writes=["st_c"])
        S.dma(lambda e: e.dma_start(out=st_c16, in_=cst16), writes=["st_c16"])
        S.dma(lambda e: e.dma_start(out=xT, in_=xin.rearrange("(c p) t -> p c t", p=128)),
              writes=["x%d_%d" % (c, g) for c in range(8) for g in range(NGRP)])
        S.op('pool', lambda e: e.memset(ones_bf, 1.0), writes=["ones_bf"])
        S.op('pool', lambda e: e.memset(ones32, 1.0), writes=["ones32"])
        S.op('pool', lambda e: e.memset(epsb, float(EPS)), writes=["epsb"])
        S.op('dve', lambda e: e.tensor_scalar(out=negI, in0=st_c[:, 0:128], scalar1=-30000.0, scalar2=None, op0=ALU.mult),
             reads=["st_c"], writes=["negI"])
        S.op('dve', lambda e: e.tensor_scalar(out=notmask, in0=st_c[:, 256:768].rearrange("p (a b) -> p a b", a=4),
                                              scalar1=-1.0, scalar2=1.0, op0=ALU.mult, op1=ALU.add),
             reads=["st_c"], writes=["notmask"])
        S.op('dve', lambda e: e.tensor_copy(out=ident32, in_=st_c[:, 0:128]), reads=["st_c"], writes=["ident32"])
        S.op('dve', lambda e: e.tensor_copy(out=triu32, in_=st_c[:, 128:256]), reads=["st_c"], writes=["triu32"])
        S.op('dve', lambda e: e.tensor_copy(out=maskT, in_=st_c[:, 256:768].rearrange("p (a b) -> p a b", a=4)),
             reads=["st_c"], writes=["maskT"])
        S.op('dve', lambda e: e.tensor_copy(out=apool, in_=st_c[:, 768:1792].rearrange("p (a b c) -> p a b c", a=2, b=4)),
             reads=["st_c"], writes=["apool"])
        S.op('dve', lambda e: e.tensor_copy(out=selpe, in_=st_c[0:64, 1792:1984].rearrange("p (a b) -> p a b", a=2)),
             reads=["st_c"], writes=["selpe"])
        S.op('dve', lambda e: e.tensor_copy(out=invf, in_=st_c[0:96, 1984:1985]), reads=["st_c"], writes=["invf"])
        S.op('dve', lambda e: e.tensor_copy(out=ahalo, in_=st_c16.rearrange("p (a b) -> p a b", a=4)),
             reads=["st_c16"], writes=["ahalo"])
        posi = tb.take([NTOK], I32, parts=96)
        ang = tb.take([NTOK], F32, parts=96)
        tq = tb.take([NTOK], F32, parts=96)
        kq = tb.take([NTOK], F32, parts=96)
        S.dma(lambda e: e.dma_start(out=posi, in_=pos[0].partition_broadcast(96)), writes=["posi"])
        S.op('dve', lambda e: e.tensor_copy(out=ang, in_=posi), reads=["posi"], writes=["ang"])
        S.op('dve', lambda e: e.tensor_scalar(out=ang, in0=ang, scalar1=invf[:, 0:1], scalar2=None, op0=ALU.mult),
             reads=["ang", "invf"], writes=["ang"])
        for (tab, tres, shift, post) in ((Stab, "Stab", 0.0, 0.0), (Ctab, "Ctab", 0.25, math.pi / 2)):
            S.op('dve', lambda e, shift=shift: e.tensor_scalar(out=tq, in0=ang, scalar1=1.0 / TWO_PI, scalar2=shift,
                                                               op0=ALU.mult, op1=ALU.add),
                 reads=["ang"], writes=["tq"])
            S.op('dve', lambda e: e.tensor_scalar(out=kq, in0=tq, scalar1=MAGIC, scalar2=None, op0=ALU.add),
                 reads=["tq"], writes=["kq"])
            S.op('dve', lambda e: e.tensor_scalar(out=kq, in0=kq, scalar1=MAGIC, scalar2=None, op0=ALU.subtract),
                 reads=["kq"], writes=["kq"])
            S.op('dve', lambda e: e.scalar_tensor_tensor(out=tq, in0=kq, scalar=-C1, in1=ang, op0=ALU.mult, op1=ALU.add),
                 reads=["kq", "ang"], writes=["tq"])
            S.op('dve', lambda e, post=post: e.scalar_tensor_tensor(out=tq, in0=kq, scalar=-C2, in1=tq, op0=ALU.mult, op1=ALU.add),
                 reads=["kq", "tq"], writes=["tq"])
            if post != 0.0:
                S.op('dve', lambda e, post=post: e.tensor_scalar(out=tq, in0=tq, scalar1=post, scalar2=None, op0=ALU.add),
                     reads=["tq"], writes=["tq"])
            S.op('dve', lambda e: e.tensor_scalar(out=tq, in0=tq, scalar1=-PI_SAFE, scalar2=PI_SAFE, op0=ALU.max, op1=ALU.min),
                 reads=["tq"], writes=["tq"])
            S.op('act', lambda e, tab=tab: e.activation(out=tab, in_=tq, func=AF.Sin), reads=["tq"], writes=[tres])

        def load_vecs(l):
            S.dma(lambda e: e.dma_start(out=vec_sb, in_=vecs[l]), writes=["vec"])
            S.dma(lambda e: e.dma_start(out=gv_bc, in_=g_sgu_v[l].partition_broadcast(128)), writes=["gv_bc"])
            S.dma(lambda e: e.dma_start(out=brow32, in_=b_spatial[l:l + 1, :]), writes=["brow32"])

        VC = dict(gmix=0, gffn=8, gql=16, gkv=18, gq=19, gqp=20, gk=21, gkp=22, goa=23, gog=27, gop=31, psc=35)

        def vcol(name, i=0, P=128):
            c = VC[name] + i
            return vec_sb[0:P, c:c + 1]

        def issue_cc(l, pieces):
            for i in pieces:
                if i < 4:
                    rd = ["pay_kt%d_%d" % (h, i) for h in range(4)] + ["pay_v%d_%d" % (h, m) for h in range(4) for m in range(4 * i, 4 * i + 4)]
                else:
                    rd = ["pay_h%d" % m for m in range(NBLK)]
                S.special('pool', 'cc%d' % i, lambda e: e.collective_compute(
                    "AllGather", ALU.bypass, replica_groups=[[0, 1, 2, 3], [4, 5, 6, 7]],
                    ins=[pay[l][i].opt()], outs=[need_gath[l][i].opt()]), reads=rd, writes=["gath%d" % i])

        def stage_A(l, write_pay):
            tb = Bump(T_LO, TOT)
            stgs = [tb.take([INW], F32) for _ in range(4)]
            stg = stgs[0]
            for c in range(8):
                sg_, sgr = stgs[c % 4], ("stg" if c % 4 == 0 else "stg_%d" % (c % 4))
                S.dma(lambda e: e.dma_start(out=sg_, in_=w_in[l, c * 128:(c + 1) * 128, :]), writes=[sgr])
                S.op('dve', lambda e: e.tensor_copy(out=w_in_sb[:, c, 0:416], in_=sg_[:, 0:416]),
                     reads=[sgr], writes=["w_in%d" % c])
                S.op('pool', lambda e: e.tensor_scalar(out=w_in_sb[:, c, 416:432], in0=sg_[:, 400:416], scalar1=-1.0,
                                                       scalar2=None, op0=ALU.mult),
                     reads=[sgr], writes=["w_in%d" % c])
                S.op('pool', lambda e: e.tensor_copy(out=w_in_sb[:, c, 432:448], in_=sg_[:, 384:400]),
                     reads=[sgr], writes=["w_in%d" % c])
                S.op('act', lambda e: e.copy(out=w_in_sb[:, c, 448:1216], in_=sg_[:, 416:1184]),
                     reads=[sgr], writes=["w_in%d" % c])
            stq = stg[:, 0:768].rearrange("p (a b) -> p a b", a=2)
            S.dma(lambda e: e.dma_start(out=stq, in_=w_q_up[l].rearrange("(c p) n -> p c n", p=128)), writes=["stg"])
            S.op('pool', lambda e: e.tensor_copy(out=wq_sb, in_=stq), reads=["stg"], writes=["wq"])
            S.op('pool', lambda e: e.memset(wqR_sb, 0.0), writes=["wqR"])
            stq4 = stg[:, 0:768].rearrange("p (a b) -> p a b", a=8)
            wqR4 = wqR_sb.rearrange("p c (h d) -> p (c h) d", h=4)
            S.op('pool', lambda e: e.tensor_scalar(out=wqR4[:, :, 64:80], in0=stq4[:, :, 80:96], scalar1=-1.0, scalar2=None,
                                                   op0=ALU.mult), reads=["stg"], writes=["wqR"])
            S.op('pool', lambda e: e.tensor_copy(out=wqR4[:, :, 80:96], in_=stq4[:, :, 64:80]), reads=["stg"], writes=["wqR"])
            S.dma(lambda e: e.dma_start(out=stg[:, 0:768], in_=w_kv_up[l]), writes=["stg"])
            S.op('pool', lambda e: e.tensor_copy(out=wkv_sb, in_=stg[:, 0:768]), reads=["stg"], writes=["wkv"])
            S.op('pool', lambda e: e.memset(wk1_sb, 0.0), writes=["wk1"])
            S.op('pool', lambda e: e.tensor_copy(out=wk1_sb[:, :, 0:64],
                                                 in_=stg[:, 0:768].rearrange("p (h d) -> p h d", h=4)[:, :, 0:64]),
                 reads=["stg"], writes=["wk1"])
            for h in range(4):
                sgh, sghr = stgs[h], ("stg" if h == 0 else "stg_%d" % h)
                S.dma(lambda e: e.dma_start(out=sgh[:, 0:128], in_=w_spatial[l, h]), writes=[sghr])
                pt, pr = ps_next()
                S.op('pe', lambda e: e.transpose(out=pt[:, 0:128], in_=sgh[:, 0:128], identity=ident32),
                     reads=[sghr, "ident32"], writes=[pr])
                S.op('dve', lambda e, pt=pt, h=h: e.tensor_tensor(out=WsT_sb[:, h, :], in0=pt[:, 0:128], in1=triu32, op=ALU.mult),
                     reads=[pr, "triu32"], writes=["WsT"])
            S.barrier()
            tb = Bump(T_LO, TOT)
            sqb = [tb.take([512], BF16) for _ in range(2)]
            hT = tb.take([8, 512], BF16)
            rst = [tb.take([512], F32) for _ in range(2)]
            sd = tb.take([16], F32)
            qlat_n = tb.take([2, 512], BF16)
            kvlat_n = tb.take([512], BF16)
            kpe_sb = tb.take([512], BF16, parts=64)
            uT = tb.take([4, 512], BF16, parts=64)
            gm = tb.take([4, 512], F32, parts=64)
            v_n = tb.take([4, 256], BF16)
            t1 = tb.take([512], F32, parts=96)
            t2 = tb.take([512], F32, parts=96)
            rxb = [tb.take([512], F32) for _ in range(2)]
            t2k = t2
            KTg = tb.take([4, 512], BF16, parts=96)
            Vb = tb.take([512], BF16)
            ssv = tb.take([12], F32)
            SQI = [0]
            RSI = [0]

            def sq_next():
                i = SQI[0]
                SQI[0] = (i + 1) % 2
                return sqb[i], "sqb%d" % i

            def rst_next():
                i = RSI[0]
                RSI[0] = (i + 1) % 2
                return rst[i], "rst%d" % i

            for g in range(NGRP):
                gs = slice(g * 512, (g + 1) * 512)
                def x_stats(g2):
                    gs2 = slice(g2 * 512, (g2 + 1) * 512)
                    pss, pssr = ps_next()
                    for c in range(8):
                        sq, sqr = sq_next()
                        S.op('act', lambda e: e.activation(out=sq, in_=xT[:, c, gs2], func=AF.Square),
                             reads=["x%d_%d" % (c, g2)], writes=[sqr])
                        S.op('pe', lambda e: e.matmul(pss[:, :], ones_bf, sq, start=(c == 0), stop=(c == 7)),
                             reads=[sqr, "ones_bf"], writes=[pssr])
                    rstd_from(pss[:, :], pssr, 1024.0, 128, sd, "sd", rxb[g2 % 2], "rxb%d" % (g2 % 2), 512)

                if g == 0:
                    x_stats(0)
                rx, rxr = rxb[g % 2], "rxb%d" % (g % 2)
                for c in range(8):
                    S.op('dve', lambda e: e.scalar_tensor_tensor(out=hT[:, c, :], in0=xT[:, c, gs], scalar=vcol('gmix', c),
                                                                 in1=rx, op0=ALU.mult, op1=ALU.mult),
                         reads=["x%d_%d" % (c, g), rxr, "vec"], writes=["hT%d" % c])
                hres = ["hT%d" % c for c in range(8)]
                wres = ["w_in%d" % c for c in range(8)]

                def proj_ii(col0, M, pt, pr):
                    for c in range(8):
                        S.op('pe', lambda e, c=c: e.matmul(pt[0:M, :], w_in_sb[:, c, col0:col0 + M], hT[:, c, :],
                                                           start=(c == 0), stop=(c == 7)),
                             reads=hres + wres, writes=[pr])

                pq = [ps_next(), ps_next()]
                for mt in range(2):
                    proj_ii(mt * 128, 128, pq[mt][0], pq[mt][1])
                pss, pssr = ps_next()
                for mt in range(2):
                    sq, sqr = sq_next()
                    S.op('act', lambda e, sq=sq, mt=mt: e.activation(out=sq, in_=pq[mt][0][:, :], func=AF.Square),
                         reads=[pq[mt][1]], writes=[sqr])
                    S.op('pe', lambda e, sq=sq, mt=mt, pss=pss: e.matmul(pss[:, :], ones_bf, sq, start=(mt == 0), stop=(mt == 1)),
                         reads=[sqr, "ones_bf"], writes=[pssr])
                rq, rqr = rst_next()
                rstd_from(pss[:, :], pssr, 256.0, 128, sd, "sd", rq, rqr, 512)
                for mt in range(2):
                    S.op('dve', lambda e, mt=mt, rq=rq: e.scalar_tensor_tensor(out=qlat_n[:, mt, :], in0=pq[mt][0][:, :],
                                                                               scalar=vcol('gql', mt), in1=rq,
                                                                               op0=ALU.mult, op1=ALU.mult),
                         reads=[pq[mt][1], rqr, "vec"], writes=["qlat_n"])
                pk, pkr = ps_next()
                proj_ii(256, 128, pk, pkr)
                sq, sqr = sq_next()
                S.op('act', lambda e, sq=sq, pk=pk: e.activation(out=sq, in_=pk[:, :], func=AF.Square), reads=[pkr], writes=[sqr])
                pss, pssr = ps_next()
                S.op('pe', lambda e, sq=sq, pss=pss: e.matmul(pss[:, :], ones_bf, sq, start=True, stop=True),
                     reads=[sqr, "ones_bf"], writes=[pssr])
                rk, rkr = rst_next()
                rstd_from(pss[:, :], pssr, 128.0, 128, sd, "sd", rk, rkr, 512)
                S.op('dve', lambda e, pk=pk, rk=rk: e.scalar_tensor_tensor(out=kvlat_n, in0=pk[:, :], scalar=vcol('gkv'), in1=rk,
                                                                           op0=ALU.mult, op1=ALU.mult),
                     reads=[pkr, rkr, "vec"], writes=["kvlat_n"])
                pp, ppr = ps_next()
                proj_ii(384, 64, pp, ppr)
                S.op('act', lambda e, pp=pp: e.copy(out=kpe_sb, in_=pp[0:64, :]), reads=[ppr], writes=["kpe"])
                for h in range(4):
                    pu, pur = ps_next()
                    proj_ii(448 + h * 64, 64, pu, pur)
                    S.op('act', lambda e, pu=pu, h=h: e.copy(out=uT[:, h, :], in_=pu[0:64, :]), reads=[pur], writes=["uT"])
                pvs = []
                for b in range(4):
                    m = 4 * g + b
                    pv, pvr = ps_next()
                    pvs.append((pv, pvr))
                    for c in range(8):
                        S.op('pe', lambda e: e.matmul(pv[:, :], hT[:, c, b * 128:(b + 1) * 128], w_in_sb[:, c, 704:1216],
                                                      start=(c == 0), stop=(c == 7)),
                             reads=hres + wres, writes=[pvr])
                    sq, sqr = sq_next()
                    S.op('act', lambda e: e.activation(out=sq[:, 0:256], in_=pv[:, 0:256], func=AF.Square, accum_out=ssv[:, b:b + 1]),
                         reads=[pvr], writes=[sqr, "ssv"])
                    S.op('act', lambda e: e.copy(out=pin[:, m, :], in_=pv[:, 256:512]), reads=[pvr], writes=["pin%d" % m])
                if g + 1 < NGRP:
                    x_stats(g + 1)
                S.op('act', lambda e: e.activation(out=ssv[:, 4:8], in_=ssv[:, 0:4], func=AF.Ln, bias=epsb[:, 0:1], scale=1.0 / 256),
                     reads=["ssv", "epsb"], writes=["ssv1"])
                S.op('act', lambda e: e.activation(out=ssv[:, 8:12], in_=ssv[:, 4:8], func=AF.Exp, scale=-0.5), reads=["ssv1"], writes=["ssv2"])
                for b in range(4):
                    pv, pvr = pvs[b]
                    S.op('dve', lambda e: e.scalar_tensor_tensor(out=v_n[:, b, :], in0=pv[:, 0:256], scalar=ssv[:, 8 + b:9 + b], in1=gv_bc,
                                                                 op0=ALU.mult, op1=ALU.mult),
                         reads=[pvr, "ssv2", "gv_bc"], writes=["v_n%d" % b])
                for b in range(4):
                    pz, pzr = ps_next()
                    for h in range(4):
                        S.op('pe', lambda e: e.matmul(pz[0:64, h * 128:(h + 1) * 128], v_n[:, b, h * 64:(h + 1) * 64],
                                                      WsT_sb[:, h, :], start=True, stop=False),
                             reads=["v_n%d" % b, "WsT"], writes=[pzr])
                        S.op('pe', lambda e: e.matmul(pz[0:64, h * 128:(h + 1) * 128], ones32[0:1, 0:64],
                                                      brow32[0:1, h * 128:(h + 1) * 128], start=False, stop=True),
                             reads=["ones32", "brow32"], writes=[pzr])
                    S.op('dve', lambda e: e.tensor_tensor(out=gm[:, :, b * 128:(b + 1) * 128],
                                                          in0=pz[0:64, :].rearrange("p (h t) -> p h t", h=4),
                                                          in1=uT[:, :, b * 128:(b + 1) * 128], op=ALU.mult),
                         reads=[pzr, "uT"], writes=["gm"])
                pss, pssr = ps_next()
                for h in range(4):
                    sq, sqr = sq_next()
                    S.op('act', lambda e: e.activation(out=sq[0:64, :], in_=gm[:, h, :], func=AF.Square),
                         reads=["gm"], writes=[sqr])
                    S.op('pe', lambda e: e.matmul(pss[0:64, :], ones_bf[0:64, 0:64], sq[0:64, :], start=(h == 0), stop=(h == 3)),
                         reads=[sqr, "ones_bf"], writes=[pssr])
                rg, rgr = rst_next()
                rstd_from(pss[0:64, :], pssr, 256.0, 64, sd[0:64, :], "sd", rg[0:64, :], rgr, 512)
                for h in range(4):
                    S.op('dve', lambda e: e.scalar_tensor_tensor(out=gn[:, h, gs], in0=gm[:, h, :], scalar=vcol('gog', h, 64),
                                                                 in1=rg[0:64, :], op0=ALU.mult, op1=ALU.mult),
                         reads=["gm", rgr, "vec"], writes=["gn%d" % (4 * g + b) for b in range(4)])
                def head_norm_rope(praw, prawr, pR, pRr, gname, gpname, out_ap, ores, t2_pre=None):
                    sq, sqr = sq_next()
                    S.op('act', lambda e: e.activation(out=sq[0:96, :], in_=praw[0:96, :], func=AF.Square),
                         reads=[prawr], writes=[sqr])
                    pss, pssr = ps_next()
                    S.op('pe', lambda e: e.matmul(pss[0:96, :], ones_bf[0:96, 0:96], sq[0:96, :], start=True, stop=True),
                         reads=[sqr, "ones_bf"], writes=[pssr])
                    rr, rrr = rst_next()
                    rstd_from(pss[0:96, :], pssr, 96.0, 96, sd[0:96, :], "sd", rr[0:96, :], rrr, 512)
                    S.op('dve', lambda e: e.scalar_tensor_tensor(out=t1, in0=praw[0:96, :], scalar=vcol(gname, 0, 96),
                                                                 in1=Ctab[:, gs], op0=ALU.mult, op1=ALU.mult),
                         reads=[prawr, "Ctab", "vec"], writes=["t1"])
                    if t2_pre is None:
                        S.op('dve', lambda e: e.scalar_tensor_tensor(out=t2, in0=pR[0:96, :], scalar=vcol(gpname, 0, 96),
                                                                     in1=Stab[:, gs], op0=ALU.mult, op1=ALU.mult),
                             reads=[pRr, "Stab", "vec"], writes=["t2"])
                        t2u, t2r = t2, "t2"
                    else:
                        t2u, t2r = t2_pre
                    S.op('dve', lambda e: e.tensor_tensor(out=t1, in0=t1, in1=t2u, op=ALU.add), reads=["t1", t2r], writes=["t1"])
                    S.op('dve', lambda e: e.tensor_tensor(out=out_ap, in0=t1, in1=rr[0:96, :], op=ALU.mult),
                         reads=["t1", rrr], writes=[ores])

                for h in range(4):
                    pqa, pqar = ps_next()
                    pqb, pqbr = ps_next()
                    for c in range(2):
                        S.op('pe', lambda e, c=c, h=h, pqa=pqa: e.matmul(pqa[0:96, :], wq_sb[:, c, h * 96:(h + 1) * 96], qlat_n[:, c, :],
                                                                         start=(c == 0), stop=(c == 1)),
                             reads=["wq", "qlat_n"], writes=[pqar])
                    for c in range(2):
                        S.op('pe', lambda e, c=c, h=h, pqb=pqb: e.matmul(pqb[0:96, :], wqR_sb[:, c, h * 96:(h + 1) * 96], qlat_n[:, c, :],
                                                                         start=(c == 0), stop=(c == 1)),
                             reads=["wqR", "qlat_n"], writes=[pqbr])
                    head_norm_rope(pqa, pqar, pqb, pqbr, 'gq', 'gqp', qT[:, h, gs], "qT%d_%d" % (h, g))
                pkR, pkRr = ps_next()
                S.op('pe', lambda e: e.matmul(pkR[0:96, :], selpe[:, 1, :], kpe_sb, start=True, stop=True),
                     reads=["selpe", "kpe"], writes=[pkRr])
                S.op('dve', lambda e: e.scalar_tensor_tensor(out=t2k, in0=pkR[0:96, :], scalar=vcol('gkp', 0, 96),
                                                             in1=Stab[:, gs], op0=ALU.mult, op1=ALU.mult),
                     reads=[pkRr, "Stab", "vec"], writes=["t2"])
                for h in range(4):
                    pka, pkar = ps_next()
                    S.op('pe', lambda e, h=h, pka=pka: e.matmul(pka[0:96, :], wk1_sb[:, h, :], kvlat_n, start=True, stop=False),
                         reads=["wk1", "kvlat_n"], writes=[pkar])
                    S.op('pe', lambda e, h=h, pka=pka: e.matmul(pka[0:96, :], selpe[:, 0, :], kpe_sb, start=False, stop=True),
                         reads=["selpe", "kpe"], writes=[pkar])
                    head_norm_rope(pka, pkar, None, None, 'gk', 'gkp', KTg[:, h, :], "KTg", t2_pre=(t2k, "t2"))
                if write_pay:
                    for h in range(4):
                        kdst = pay[l][g][h * 56:h * 56 + 24, :].rearrange("r (c t) -> (r c) t", c=4)
                        S.dma(lambda e: e.dma_start(out=kdst, in_=KTg[:, h, :]),
                              reads=["KTg"], writes=["pay_kt%d_%d" % (h, g)])
                for b in range(4):
                    m = 4 * g + b
                    pv, pvr = ps_next()
                    S.op('pe', lambda e, pv=pv, b=b: e.matmul(
                        pv[:, :].rearrange("p (h d) -> p h d", h=4), kvlat_n[:, b * 128:(b + 1) * 128],
                        wkv_sb.rearrange("p (h d) -> p h d", h=4)[:, :, 64:192], start=True, stop=True),
                         reads=["kvlat_n", "wkv"], writes=[pvr])
                    S.op('act', lambda e, pv=pv: e.copy(out=Vb, in_=pv[:, :]), reads=[pvr], writes=["Vb"])
                    if write_pay:
                        for h in range(4):
                            dst = pay[l][g][h * 56 + 24 + b * 8:h * 56 + 32 + b * 8, :].rearrange("r (q d) -> (r q) d", q=16)
                            S.dma(lambda e: e.dma_start(out=dst, in_=Vb[:, h * 128:(h + 1) * 128]),
                                  reads=["Vb"], writes=["pay_v%d_%d" % (h, m)])
                        hd = pay[l][4].rearrange("(m r) (q d) -> m (r q) d", m=NBLK, q=8)[m]
                        S.dma(lambda e: e.dma_start(out=hd, in_=pin[112:128, m, :]),
                              reads=["pin%d" % m], writes=["pay_h%d" % m])
                if fused and write_pay:
                    issue_cc(l, [g])
            if fused and write_pay:
                issue_cc(l, [4])
            if dbg:
                S.dma(lambda e: e.dma_start(out=dbgt['qT'], in_=qT.rearrange("p h t -> p (h t)")),
                      reads=["qT%d_%d" % (h, g) for h in range(4) for g in range(4)], writes=["dbg_qT"])
                S.dma(lambda e: e.dma_start(out=dbgt['gn'], in_=gn.rearrange("p h t -> p (h t)")),
                      reads=["gn%d" % m for m in range(NBLK)], writes=["dbg_gn"])
            S.barrier()

        def stage_X(l):
            return

        def stage_B(l):
            gath = need_gath[l]
            tb = Bump(T_LO, TOT)
            KTh = tb.take([4 * NTOK], BF16, parts=96)
            Vh = tb.take([64, 128], BF16)
            NPT = 6
            pT = [tb.take([512], BF16) for _ in range(NPT)]
            rden = tb.take([512], F32)
            stg1_ = tb.take([D], F32)
            stgs_ = [stg1_, stg1_]
            scale = 1.0 / math.sqrt(96.0)
            LA = 3
            SBK = (0, 1, 2, 7)
            phases = [(h, (0, 1, 2)) for h in range(4)] + [(h, (3,)) for h in range(4)]
            items = []
            unit = 0
            for pi, (h, qgs) in enumerate(phases):
                for g in qgs:
                    sl = [(r, mk) for r in range(4) for mk in range(4 * g + 4)]
                    for si, (r, mk) in enumerate(sl):
                        items.append(dict(h=h, g=g, r=r, mk=mk, first=(si == 0), last=(si == len(sl) - 1), hg=unit, ph=pi,
                                          lastg=(g == qgs[-1])))
                    unit += 1

            def load_kv(pi, r):
                h_, qgs_ = phases[pi]
                for g_ in range(qgs_[-1] + 1):
                    ksrc = gath[g_][r * 224 + h_ * 56:r * 224 + h_ * 56 + 24, :].rearrange("r (c t) -> (r c) t", c=4)
                    S.dma(lambda e: e.dma_start(out=KTh[:, r * NTOK + g_ * 512:r * NTOK + (g_ + 1) * 512], in_=ksrc),
                          reads=["gath%d" % g_], writes=["KTh%d_%d" % (r, g_)])
                    vsrc = gath[g_][r * 224 + h_ * 56 + 24:r * 224 + h_ * 56 + 56, :].rearrange("(m r2) (q d) -> (r2 q) m d", m=4, q=16)
                    S.dma(lambda e: e.dma_start(out=Vh[:, r * 16 + 4 * g_:r * 16 + 4 * g_ + 4, :], in_=vsrc),
                          reads=["gath%d" % g_], writes=["Vh%d_%d" % (r, g_)])

            for r in range(4):
                load_kv(0, r)
            for c in range(4):
                S.dma(lambda e, c=c: e.dma_start(out=stgs_[c % 2], in_=w_out[l, c * 128:(c + 1) * 128, :]), writes=["stgB0"])
                S.op('pool', lambda e, c=c: e.tensor_copy(out=wo_a[:, c, :], in_=stgs_[c % 2]), reads=["stgB0"], writes=["wo_a"])
            for (wsb, base, nm) in ((wo_g, 512, "wo_g"), (wo_p, 768, "wo_p")):
                for c in range(4):
                    S.dma(lambda e, c=c, base=base: e.dma_start(out=stgs_[c % 2][0:64, :], in_=w_out[l, base + c * 64:base + (c + 1) * 64, :]),
                          writes=["stgB0"])
                    S.op('pool', lambda e, c=c, wsb=wsb: e.tensor_copy(out=wsb[:, c, :], in_=stgs_[c % 2][0:64, :]), reads=["stgB0"], writes=[nm])
            S.dma(lambda e: e.dma_start(out=stgs_[0][0:64, 0:256].rearrange("p (g d) -> p g d", g=4),
                                        in_=w_pool[l].rearrange("g c d -> c g d")), writes=["stgB0"])
            S.op('pool', lambda e: e.tensor_copy(out=wpool_sb, in_=stgs_[0][0:64, 0:256].rearrange("p (g d) -> p g d", g=4)),
                 reads=["stgB0"], writes=["wpool"])
            NI = len(items)
            for idx in range(NI + LA):
                if idx < NI:
                    it_ = items[idx]
                    h, g, r, mk = it_['h'], it_['g'], it_['r'], it_['mk']
                    slot = r * 16 + mk
                    mlo = max(0, mk - 4 * g)
                    cs = slice(mlo * 128, 512)
                    pS, pSr = psb[SBK[idx % 4]], "ps%d" % SBK[idx % 4]
                    pt_, ptr = pT[idx % NPT], "pT%d" % (idx % NPT)
                    diag = mk >= 4 * g
                    S.op('pe', lambda e: e.matmul(pS[:, cs], KTh[:, slot * 128:(slot + 1) * 128],
                                                  qT[:, h, g * 512 + mlo * 128:(g + 1) * 512], start=True, stop=not diag),
                         reads=["KTh%d_%d" % (r, mk // 4), "qT%d_%d" % (h, g)], writes=[pSr])
                    if diag:
                        mb = mk - 4 * g
                        S.op('pe', lambda e: e.matmul(pS[:, mb * 128:(mb + 1) * 128], negI, notmask[:, r, :], start=False, stop=True),
                             reads=["negI", "notmask"], writes=[pSr])
                    S.op('act', lambda e: e.activation(out=pt_[:, cs], in_=pS[:, cs], func=AF.Exp, scale=scale),
                         reads=[pSr], writes=[ptr])
                if idx >= LA:
                    j_ = idx - LA
                    it_ = items[j_]
                    h, g, r, mk = it_['h'], it_['g'], it_['r'], it_['mk']
                    slot = r * 16 + mk
                    mlo = max(0, mk - 4 * g)
                    cs = slice(mlo * 128, 512)
                    pt_, ptr = pT[j_ % NPT], "pT%d" % (j_ % NPT)
                    po, por = psb[3 + (it_['hg'] % 2)], "ps%d" % (3 + (it_['hg'] % 2))
                    pd, pdr = psb[5 + (it_['hg'] % 2)], "ps%d" % (5 + (it_['hg'] % 2))
                    S.op('pe', lambda e: e.matmul(po[:, cs], Vh[:, slot, :], pt_[:, cs], start=it_['first'], stop=it_['last']),
                         reads=[ptr, "Vh%d_%d" % (r, mk // 4)], writes=[por])
                    S.op('pe', lambda e: e.matmul(pd[:, cs], ones_bf, pt_[:, cs], start=it_['first'], stop=it_['last']),
                         reads=[ptr, "ones_bf"], writes=[pdr])
                    if it_['lastg'] and mk == 4 * g + 3 and it_['ph'] + 1 < len(phases):
                        load_kv(it_['ph'] + 1, r)
                    if it_['last']:
                        S.op('dve', lambda e: e.reciprocal(out=rden, in_=pd[:, :]), reads=[pdr], writes=["rden"])
                        S.op('dve', lambda e: e.tensor_tensor(out=aT_ap(h)[:, g * 512:(g + 1) * 512], in0=po[:, :], in1=rden, op=ALU.mult),
                             reads=[por, "rden"], writes=[aT_res(h, g)])
            S.barrier()
            if dbg:
                for h in range(4):
                    S.dma(lambda e, h=h: e.dma_start(out=dbgt['aT'][:, h * NTOK:(h + 1) * NTOK], in_=aT_ap(h)), writes=["dbg_aT%d" % h])
                S.barrier()
            tb = Bump(T_LO, TOT)
            sqb = [tb.take([512], BF16) for _ in range(2)]
            rst = [tb.take([512], F32) for _ in range(3)]
            sd = tb.take([16], F32)
            an2 = [tb.take([4, 512], BF16) for _ in range(2)]
            halo2 = [tb.take([4, 256], BF16, parts=64) for _ in range(2)]
            pm2 = [tb.take([4, 128], BF16, parts=64) for _ in range(2)]
            py2 = [tb.take([4, 512], F32, parts=64) for _ in range(2)]
            pn2 = [tb.take([4, 512], BF16, parts=64) for _ in range(2)]
            for i_ in range(2):
                S.op('pool', lambda e: e.memset(halo2[i_], 0.0), writes=["halo%d_%d" % (i_, r) for r in range(4)])
            SQI = [0]
            RSI = [0]
            PMI = [0]

            def sq_next():
                i = SQI[0]
                SQI[0] = (i + 1) % 2
                return sqb[i], "sqb%d" % i

            def rst_next():
                i = RSI[0]
                RSI[0] = (i + 1) % 3
                return rst[i], "rst%d" % i

            def chain(g):
                par = g % 2
                an, halo, py, pn = an2[par], halo2[par], py2[par], pn2[par]
                anr, pyr, pnr = "an%d" % par, "py%d" % par, "pn%d" % par
                gs = slice(g * 512, (g + 1) * 512)
                pss, pssr = ps_next()
                for h in range(4):
                    sq, sqr = sq_next()
                    S.op('act', lambda e: e.activation(out=sq, in_=aT_ap(h)[:, gs], func=AF.Square),
                         reads=[aT_res(h, g)], writes=[sqr])
                    S.op('pe', lambda e: e.matmul(pss[:, :], ones_bf, sq, start=(h == 0), stop=(h == 3)),
                         reads=[sqr, "ones_bf"], writes=[pssr])
                yield
                ra, rar = rst_next()
                rstd_from(pss[:, :], pssr, 512.0, 128, sd, "sd", ra, rar, 512)
                for h in range(4):
                    S.op('dve', lambda e: e.scalar_tensor_tensor(out=an[:, h, :], in0=aT_ap(h)[:, gs], scalar=vcol('goa', h),
                                                                 in1=ra, op0=ALU.mult, op1=ALU.mult),
                         reads=[aT_res(h, g), rar, "vec"], writes=[anr])
                for r in range(4):
                    hsrc = gath[4][r * 32:(r + 1) * 32, :].rearrange("(m r2) (q d) -> (r2 q) m d", m=NBLK, q=8)
                    m0 = 4 * g if r < 3 else 4 * g - 1
                    b0_ = 0
                    if m0 < 0:
                        m0, b0_ = 0, 1
                    nb = 4 - b0_
                    S.dma(lambda e: e.dma_start(out=halo[16 * r:16 * r + 16, b0_:4, :], in_=hsrc[:, m0:m0 + nb, :]),
                          reads=["gath4"], writes=["halo%d_%d" % (par, r)])
                for b in range(4):
                    m = 4 * g + b
                    var = 0 if m == 0 else 1
                    ppm, ppmr = ps_next()
                    for gi in range(4):
                        S.op('pe', lambda e: e.matmul(ppm[0:64, gi * 128:(gi + 1) * 128], pin[:, m, gi * 64:(gi + 1) * 64],
                                                      apool[:, var, gi, :], start=True, stop=False),
                             reads=["pin%d" % m, "apool"], writes=[ppmr])
                        S.op('pe', lambda e: e.matmul(ppm[0:64, gi * 128:(gi + 1) * 128], halo[:, b, gi * 64:(gi + 1) * 64],
                                                      ahalo[:, gi, :], start=False, stop=True),
                             reads=["halo%d_%d" % (par, r) for r in range(4)] + ["ahalo"], writes=[ppmr])
                    yield
                    pmi = PMI[0] % 2
                    PMI[0] += 1
                    pm, pmr = pm2[pmi], "pm%d" % pmi
                    S.op('act', lambda e: e.copy(out=pm, in_=ppm[0:64, :].rearrange("p (g t) -> p g t", g=4)),
                         reads=[ppmr], writes=[pmr])
                    ppy, ppyr = ps_next()
                    for gi in range(4):
                        S.op('pe', lambda e: e.matmul(ppy[0:64, gi * 128:(gi + 1) * 128], wpool_sb[:, gi, :], pm[:, gi, :],
                                                      start=True, stop=True),
                             reads=["wpool", pmr], writes=[ppyr])
                    for gi in range(4):
                        S.op('dve', lambda e: e.tensor_scalar(out=py[:, gi, b * 128:(b + 1) * 128], in0=ppy[0:64, gi * 128:(gi + 1) * 128],
                                                              scalar1=vcol('psc', gi, 64), scalar2=None, op0=ALU.mult),
                             reads=[ppyr, "vec"], writes=[pyr])
                    yield
                pss, pssr = ps_next()
                for gi in range(4):
                    sq, sqr = sq_next()
                    S.op('act', lambda e: e.activation(out=sq[0:64, :], in_=py[:, gi, :], func=AF.Square),
                         reads=[pyr], writes=[sqr])
                    S.op('pe', lambda e: e.matmul(pss[0:64, :], ones_bf[0:64, 0:64], sq[0:64, :], start=(gi == 0), stop=(gi == 3)),
                         reads=[sqr, "ones_bf"], writes=[pssr])
                yield
                rp, rpr = rst_next()
                rstd_from(pss[0:64, :], pssr, 256.0, 64, sd[0:64, :], "sd", rp[0:64, :], rpr, 512)
                for gi in range(4):
                    S.op('dve', lambda e: e.scalar_tensor_tensor(out=pn[:, gi, :], in0=py[:, gi, :], scalar=vcol('gop', gi, 64),
                                                                 in1=rp[0:64, :], op0=ALU.mult, op1=ALU.mult),
                         reads=[pyr, rpr, "vec"], writes=[pnr])

            def wout(g):
                par = g % 2
                an, pn = an2[par], pn2[par]
                anr, pnr = "an%d" % par, "pn%d" % par
                gs = slice(g * 512, (g + 1) * 512)
                for dt_ in range(8):
                    pyo, pyor = ps_next()
                    ds = slice(dt_ * 128, (dt_ + 1) * 128)
                    for c in range(4):
                        S.op('pe', lambda e: e.matmul(pyo[:, :], wo_a[:, c, ds], an[:, c, :], start=(c == 0), stop=False),
                             reads=["wo_a", anr], writes=[pyor])
                    for c in range(4):
                        S.op('pe', lambda e: e.matmul(pyo[:, :], wo_g[:, c, ds], gn[:, c, gs], start=False, stop=False),
                             reads=["wo_g"] + ["gn%d" % (4 * g + b) for b in range(4)], writes=[pyor])
                    for c in range(4):
                        S.op('pe', lambda e: e.matmul(pyo[:, :], wo_p[:, c, ds], pn[:, c, :], start=False, stop=(c == 3)),
                             reads=["wo_p", pnr], writes=[pyor])
                    S.op('dve', lambda e: e.tensor_tensor(out=xT[:, dt_, gs], in0=pyo[:, :], in1=xT[:, dt_, gs], op=ALU.add),
                         reads=[pyor, "x%d_%d" % (dt_, g)], writes=["x%d_%d" % (dt_, g)])
                    yield

            for _ in chain(0):
                pass
            for g in range(NGRP):
                gens = [wout(g)]
                if g + 1 < NGRP:
                    gens.append(chain(g + 1))
                while gens:
                    for gen in list(gens):
                        try:
                            next(gen)
                        except StopIteration:
                            gens.remove(gen)
            S.barrier()

        def stage_C(l, stream_out=False):
            tb = Bump(W1_LO, TOT)
            hF = tb.take([8, NTOK], BF16)
            aF = tb.take([6, NTOK], BF16)
            wd = tb.take([6, D], BF16)
            wg = [tb.take([8, 256], BF16) for _ in range(2)]
            wu = [tb.take([8, 256], BF16) for _ in range(2)]
            stgA = [tb.take([8, 256], F32) for _ in range(2)]
            stgB = tb.take([D], F32)
            sqb = [tb.take([512], BF16) for _ in range(2)]
            rx = tb.take([512], F32)
            rx2 = tb.take([512], F32)
            sd = tb.take([16], F32)
            sg = [tb.take([512], F32) for _ in range(2)]
            pairs = []
            ht0 = 0
            for ch, nt in enumerate(FF_CHUNKS):
                j = 0
                while j < nt:
                    npair = min(2, nt - j)
                    pairs.append((ch, j, ht0 + j, npair))
                    j += npair
                ht0 += nt
            SA = [0]

            def load_pair(pi):
                ch, j, ht, npair = pairs[pi]
                ncol = npair * 128
                b_ = pi % 2
                for (wsrc, wdst, wr) in ((w_gate, wg[b_], "wg%d" % b_), (w_up, wu[b_], "wu%d" % b_)):
                    si = SA[0] % 2
                    SA[0] += 1
                    S.dma(lambda e: e.dma_start(out=stgA[si][:, :, 0:ncol],
                                                in_=wsrc[l, :, ht * 128:ht * 128 + ncol].rearrange("(c p) n -> p c n", p=128)),
                          writes=["stgA%d" % si])
                    S.op('act', lambda e: e.copy(out=wdst[:, :, 0:ncol], in_=stgA[si][:, :, 0:ncol]),
                         reads=["stgA%d" % si], writes=[wr])

            def load_wd(ch):
                base = sum(FF_CHUNKS[:ch])
                for j in range(FF_CHUNKS[ch]):
                    ht = base + j
                    S.dma(lambda e: e.dma_start(out=stgB, in_=w_down[l, ht * 128:(ht + 1) * 128, :]), writes=["stgB"])
                    S.op('pool', lambda e: e.tensor_copy(out=wd[:, j, :], in_=stgB), reads=["stgB"], writes=["wd"])

            load_pair(0)
            load_pair(1)
            load_wd(0)
            def ffn_norm(g):
                gs = slice(g * 512, (g + 1) * 512)
                pss, pssr = ps_next()
                for c in range(8):
                    sq, sqr = sqb[c % 2], "sqb%d" % (c % 2)
                    S.op('act', lambda e: e.activation(out=sq, in_=xT[:, c, gs], func=AF.Square),
                         reads=["x%d_%d" % (c, g)], writes=[sqr])
                    S.op('pe', lambda e: e.matmul(pss[:, :], ones_bf, sq, start=(c == 0), stop=(c == 7)),
                         reads=[sqr, "ones_bf"], writes=[pssr])
                rxg, rxgr = (rx, "rx") if g % 2 == 0 else (rx2, "rx2")
                rstd_from(pss[:, :], pssr, 1024.0, 128, sd, "sd", rxg, rxgr, 512)
                for c in range(8):
                    S.op('dve', lambda e: e.scalar_tensor_tensor(out=hF[:, c, gs], in0=xT[:, c, gs], scalar=vcol('gffn', c),
                                                                 in1=rxg, op0=ALU.mult, op1=ALU.mult),
                         reads=["x%d_%d" % (c, g), rxgr, "vec"], writes=["hF%d" % g])
            SG = [0]
            for pi, (ch, j, ht, npair) in enumerate(pairs):
                b_ = pi % 2
                wgb, wub = wg[b_], wu[b_]
                wgr, wur = "wg%d" % b_, "wu%d" % b_
                for jj in range(npair):
                    for g in range(NGRP):
                        if pi == 0 and jj == 0:
                            if g == 0:
                                ffn_norm(0)
                                ffn_norm(1)
                            elif g + 1 < NGRP:
                                ffn_norm(g + 1)
                        gs = slice(g * 512, (g + 1) * 512)
                        pg, pgr = ps_next()
                        pu, pur = ps_next()
                        for c in range(8):
                            S.op('pe', lambda e: e.matmul(pg[:, :], wgb[:, c, jj * 128:(jj + 1) * 128], hF[:, c, gs],
                                                          start=(c == 0), stop=(c == 7)),
                                 reads=[wgr, "hF%d" % g], writes=[pgr])
                        for c in range(8):
                            S.op('pe', lambda e: e.matmul(pu[:, :], wub[:, c, jj * 128:(jj + 1) * 128], hF[:, c, gs],
                                                          start=(c == 0), stop=(c == 7)),
                                 reads=[wur, "hF%d" % g], writes=[pur])
                        sgi = SG[0] % 2
                        SG[0] += 1
                        S.op('act', lambda e: e.activation(out=sg[sgi], in_=pg[:, :], func=AF.Silu), reads=[pgr], writes=["sg%d" % sgi])
                        S.op('dve', lambda e: e.tensor_tensor(out=aF[:, j + jj, gs], in0=pu[:, :], in1=sg[sgi], op=ALU.mult),
                             reads=[pur, "sg%d" % sgi], writes=["aF%d" % g])
                if pi + 2 < len(pairs):
                    load_pair(pi + 2)
                last_in_chunk = (pi + 1 == len(pairs)) or (pairs[pi + 1][0] != ch)
                if last_in_chunk:
                    nt = FF_CHUNKS[ch]
                    for g in range(NGRP):
                        gs = slice(g * 512, (g + 1) * 512)
                        for dt_ in range(8):
                            pyo, pyor = ps_next()
                            for jd in range(nt):
                                S.op('pe', lambda e: e.matmul(pyo[:, :], wd[:, jd, dt_ * 128:(dt_ + 1) * 128], aF[:, jd, gs],
                                                              start=(jd == 0), stop=(jd == nt - 1)),
                                     reads=["wd", "aF%d" % g], writes=[pyor])
                            S.op('dve', lambda e: e.tensor_tensor(out=xT[:, dt_, gs], in0=pyo[:, :], in1=xT[:, dt_, gs], op=ALU.add),
                                 reads=[pyor, "x%d_%d" % (dt_, g)], writes=["x%d_%d" % (dt_, g)])
                            if stream_out and ch + 1 == len(FF_CHUNKS):
                                S.dma(lambda e: e.dma_start(out=xout[dt_ * 128:(dt_ + 1) * 128, gs], in_=xT[:, dt_, gs]),
                                      reads=["x%d_%d" % (dt_, g)], writes=["xout_%d_%d" % (dt_, g)])
                    if ch + 1 < len(FF_CHUNKS):
                        load_wd(ch + 1)
            S.barrier()

        first_A = True
        cur_l = None
        streamed = False
        for st in stages:
            if st == 'nopay':
                continue
            kind, l = st[0], int(st[1])
            if cur_l != l:
                load_vecs(l)
                cur_l = l
            if kind == 'A':
                wp = (l in pay) and not (first_A and 'nopay' in stages)
                stage_A(l, wp)
                first_A = False
            elif kind == 'X':
                stage_X(l)
            elif kind == 'B':
                stage_B(l)
            elif kind == 'C':
                is_last = (st == [s for s in stages if s != 'nopay'][-1])
                stage_C(l, stream_out=is_last)
                streamed = is_last
        if xout is not None and not streamed:
            S.dma(lambda e: e.dma_start(out=xout.rearrange("(c p) t -> p c t", p=128), in_=xT),
                  reads=["x%d_%d" % (c, g) for c in range(8) for g in range(NGRP)], writes=["xout"])
        if dbg and 'x' in dbgt and xout is None:
            S.dma(lambda e: e.dma_start(out=dbgt['x'].rearrange("(c p) t -> p c t", p=128), in_=xT),
                  reads=["x%d_%d" % (c, g) for c in range(8) for g in range(NGRP)], writes=["dbgx"])
        S.finish()

        with nc.Block() as block:
            @block.sync
            def _(eng):
                S.replay('sp', eng, sems)

            @block.scalar
            def _(eng):
                S.replay('act', eng, sems)

            @block.vector
            def _(eng):
                S.replay('dve', eng, sems)

            @block.gpsimd
            def _(eng):
                S.replay('pool', eng, sems)

            @block.tensor
            def _(eng):
                S.replay('pe', eng, sems)
    return nc


def _consts(j):
    cst = np.zeros((128, 2048), np.float32)
    cst[:, 0:128] = np.eye(128, dtype=np.float32)
    k = np.arange(128)[:, None]
    q = np.arange(128)[None, :]
    triu = (k <= q).astype(np.float32)
    cst[:, 128:256] = triu
    mask = np.zeros((128, 4, 128), np.float32)
    for r in range(4):
        if r < j:
            mask[:, r, :] = 1.0
        elif r == j:
            mask[:, r, :] = triu
    cst[:, 256:768] = mask.reshape(128, 512)
    apool = np.zeros((128, 2, 4, 128), np.float32)
    ahalo = np.zeros((64, 4, 128), np.float32)
    own = (j - 1) % 4
    for gi, w in enumerate((2, 4, 8, 16)):
        for t in range(128):
            for var in range(2):
                first = (var == 0 and j == 0)
                cntv = float(min(t + 1, w)) if first else float(w)
                for s_ in range(max(0, t - w + 1), t + 1):
                    apool[s_, var, gi, t] += 1.0 / cntv
                apool[t, var, gi, t] -= 1.0
            for sp in range(16):
                tt = sp - 16
                if tt > t - w:
                    ahalo[own * 16 + sp, gi, t] = 1.0 / float(w)
    cst[:, 768:1792] = apool.reshape(128, 1024)
    sel = np.zeros((64, 2, 96), np.float32)
    for i in range(32):
        sel[i, 0, 64 + i] = 1.0
        sel[32 + i, 1, 64 + i] = 1.0
    cst[0:64, 1792:1984] = sel.reshape(64, 192)
    half = 16
    inv_freq = (1.0 / (np.float32(10000.0) ** (np.arange(half, dtype=np.float32) / np.float32(half)))).astype(np.float32)
    invf = np.zeros(128, np.float32)
    invf[64:80] = inv_freq
    invf[80:96] = inv_freq
    cst[:, 1984] = invf
    return cst, ahalo.reshape(64, 512)


def _vecs(inp):
    v = np.zeros((2, 128, NV), np.float32)
    for l in range(2):
        v[l, :, 0:8] = inp["g_mix_norm"][l].reshape(8, 128).T
        v[l, :, 8:16] = inp["g_ffn_norm"][l].reshape(8, 128).T
        v[l, :, 16:18] = inp["g_q_lat"][l].reshape(2, 128).T
        v[l, :, 18] = inp["g_kv_lat"][l]
        for (name, c0) in (("g_q_head", 19), ("g_k_head", 21)):
            gq = inp[name][l]
            v[l, 0:96, c0] = gq
            v[l, 64:80, c0 + 1] = gq[80:96]
            v[l, 80:96, c0 + 1] = gq[64:80]
        v[l, :, 23:27] = inp["g_out_mla"][l].reshape(4, 128).T
        v[l, 0:64, 27:31] = inp["g_out_sgu"][l].reshape(4, 64).T
        v[l, 0:64, 31:35] = inp["g_out_pool"][l].reshape(4, 64).T
        v[l, 0:64, 35:39] = inp["pool_scale"][l].reshape(4, 64).T
    return v


def _tok_index(j):
    return (np.arange(NBLK)[:, None] * 4 + j) * 128 + np.arange(128)[None, :]


def _common_maps(inp):
    f = lambda a: np.ascontiguousarray(np.asarray(a, dtype=np.float32))
    com = {
        "vecs": _vecs(inp),
        "w_in": f(inp["w_in"]), "w_q_up": f(inp["w_q_up"]), "w_kv_up": f(inp["w_kv_up"]),
        "g_sgu_v": f(inp["g_sgu_v"]), "w_spatial": f(inp["w_spatial"]),
        "b_spatial": f(inp["b_spatial"]).reshape(2, 512), "w_pool": f(inp["w_pool"]),
        "w_out": f(inp["w_out"]), "w_gate": f(inp["w_gate"]), "w_up": f(inp["w_up"]), "w_down": f(inp["w_down"]),
    }
    return com


_NC_CACHE = {}


def _get_nc(key, stages, fused=False, dbg=False):
    if key not in _NC_CACHE:
        _NC_CACHE[key] = build(stages, fused=fused, dbg=dbg)
    return _NC_CACHE[key]


def _run(nc, maps):
    res = run_bass_kernel_spmd(nc, maps, core_ids=list(range(8)))
    return res.results


FUSED = True


def kernel(**inp):
    inp = {k: np.asarray(v) for k, v in inp.items()}
    x = inp["x"].astype(np.float32, copy=False)
    positions = inp["positions"].astype(np.int32, copy=False)
    com = _common_maps(inp)
    per = []
    for c in range(8):
        b, j = divmod(c, 4)
        idx = _tok_index(j).reshape(-1)
        cst, c16 = _consts(j)
        per.append({
            "xin": np.ascontiguousarray(x[b, idx, :].T),
            "pos": np.ascontiguousarray(positions[b, idx].reshape(1, NTOK)),
            "cst": cst, "cst16": c16,
        })

    def maps_for(extra):
        out = []
        for c in range(8):
            m = dict(com)
            m.update(per[c])
            m.update(extra[c])
            out.append(m)
        return out

    if FUSED:
        nc = _get_nc("fused", ["A0", "X0", "B0", "C0", "A1", "X1", "B1", "C1"], fused=True)
        res = _run(nc, maps_for([{} for _ in range(8)]))
        outs = [r["xout"] for r in res]
    else:
        def gather(res, l):
            out = []
            for c in range(8):
                b = c // 4
                out.append({"gath%d_%d" % (l, i): np.concatenate([np.asarray(res[4 * b + j]["pay%d_%d" % (l, i)]) for j in range(4)], axis=0)
                            for i in range(5)})
            return out
        nc1 = _get_nc("L1", ["A0"])
        r1 = _run(nc1, maps_for([{} for _ in range(8)]))
        nc2 = _get_nc("L2", ["nopay", "A0", "B0", "C0", "A1"])
        r2 = _run(nc2, maps_for(gather(r1, 0)))
        g1 = gather(r2, 1)
        for c in range(8):
            per[c]["xin"] = np.ascontiguousarray(np.asarray(r2[c]["xout"], dtype=np.float32))
        nc3 = _get_nc("L3", ["nopay", "A1", "B1", "C1"])
        r3 = _run(nc3, maps_for(g1))
        outs = [r["xout"] for r in r3]
    y = np.empty((2, 8192, D), np.float32)
    for c in range(8):
        b, j = divmod(c, 4)
        idx = _tok_index(j).reshape(-1)
        y[b, idx, :] = np.asarray(outs[c], dtype=np.float32).T
    return y
```

```python
import math
import numpy as np
import ml_dtypes
import concourse.bass as bass
import concourse.mybir as mybir
from concourse.bass_utils import run_bass_kernel_spmd

F32 = mybir.dt.float32
BF16 = mybir.dt.bfloat16
I32 = mybir.dt.int32
AF = mybir.ActivationFunctionType
ALU = mybir.AluOpType

D = 1024
NTOK = 2048
NBLK = 16
NGRP = 4
INW = 1184
FFH = 2816
NHT = 22
EPS = 1e-6
PAY_KT = 0
PAY_V = 384
PAY_H = 896
PAY_ROWS = 928
PIECE_ROWS = (224, 224, 224, 224, 32)
NV = 40
FF_CHUNKS = (6, 6, 5, 5)
import os as _os
_SKIP = _os.environ.get('KSKIP', '').split(',')

TWO_PI = 2.0 * math.pi
C1 = 6.28125
C2 = TWO_PI - C1
MAGIC = 12582912.0
PI_SAFE = 3.1415925


class _Rec:
    def __init__(self):
        self.call = None

    def __getattr__(self, name):
        def f(*a, **k):
            self.call = (name, a, k)
            return None
        return f


def _record(fn):
    r = _Rec()
    fn(r)
    assert r.call is not None
    return r.call


class Sched:
    ENG = ('pe', 'act', 'dve', 'pool', 'sp')

    def __init__(self, ndma=16):
        self.q = {e: [] for e in self.ENG}
        self.cnt = {e: 0 for e in self.ENG}
        self.known = {e: {} for e in self.ENG}
        self.w = {}
        self.r = {}
        self.ndma = ndma
        self.dma_val = [0] * ndma
        self.dma_next = 0
        self.extra_val = {}

    def _need(self, e, toks):
        for (s, v) in toks:
            if s == e and e == 'pe':
                continue
            if self.known[e].get(s, 0) < v:
                self.known[e][s] = v
                self.q[e].append(('wait', s, v))

    def _deps(self, reads, writes):
        toks = []
        for r_ in reads:
            if r_ in self.w:
                toks.append(self.w[r_])
        for w_ in writes:
            if w_ in self.w:
                toks.append(self.w[w_])
            toks.extend(self.r.get(w_, {}).items())
        return toks

    def _commit(self, tok, reads, writes):
        for w_ in writes:
            self.w[w_] = tok
            self.r[w_] = {}
        for r_ in reads:
            d = self.r.setdefault(r_, {})
            if d.get(tok[0], 0) < tok[1]:
                d[tok[0]] = tok[1]

    def op(self, e, fn, reads=(), writes=()):
        fn = _record(fn)
        self._need(e, self._deps(reads, writes))
        self.cnt[e] += 1
        tok = (e, self.cnt[e])
        self.q[e].append(('op', fn, e))
        self._commit(tok, reads, writes)

    def dma(self, fn, reads=(), writes=(), q='sp'):
        fn = _record(fn)
        k = self.dma_next
        self.dma_next = (k + 1) % self.ndma
        s = 'dma%d' % k
        toks = self._deps(reads, writes)
        if self.dma_val[k] > 0:
            toks.append((s, self.dma_val[k]))
        self._need(q, toks)
        self.dma_val[k] += 16
        tok = (s, self.dma_val[k])
        self.q[q].append(('dma', fn, s))
        self._commit(tok, reads, writes)

    def special(self, q, semkey, fn, reads=(), writes=()):
        fn = _record(fn)
        toks = self._deps(reads, writes)
        v = self.extra_val.get(semkey, 0)
        if v > 0:
            toks.append((semkey, v))
        self._need(q, toks)
        v += 1
        self.extra_val[semkey] = v
        self.q[q].append(('op', fn, semkey))
        self._commit((semkey, v), reads, writes)

    def barrier(self):
        allt = [(f, self.cnt[f]) for f in self.ENG if self.cnt[f] > 0]
        allt += [('dma%d' % k, v) for k, v in enumerate(self.dma_val) if v > 0]
        for e in self.ENG:
            self._need(e, [t for t in allt if t[0] != e])

    def finish(self):
        allt = [('dma%d' % k, v) for k, v in enumerate(self.dma_val) if v > 0]
        allt += [(f, self.cnt[f]) for f in self.ENG if self.cnt[f] > 0 and f != 'sp']
        allt += list(self.extra_val.items())
        self._need('sp', allt)

    def replay(self, e, eng, sems):
        for item in self.q[e]:
            if item[0] == 'wait':
                eng.wait_ge(sems[item[1]], item[2])
            elif item[0] == 'op':
                getattr(eng, item[1][0])(*item[1][1], **item[1][2]).then_inc(sems[item[2]], 1)
            else:
                getattr(eng, item[1][0])(*item[1][1], **item[1][2]).then_inc(sems[item[2]], 16)


def build(stages, fused=False, dbg=False):
    nc = bass.Bass("TRN2", target_bir_lowering=False)
    S = Sched()

    def din(name, shape, dt=F32):
        return nc.dram_tensor(name, list(shape), dt, kind="ExternalInput").ap()

    def dout(name, shape, dt=F32):
        return nc.dram_tensor(name, list(shape), dt, kind="ExternalOutput").ap()

    xin = din("xin", [D, NTOK])
    pos = din("pos", [1, NTOK], I32)
    cst = din("cst", [128, 2048])
    cst16 = din("cst16", [64, 512])
    vecs = din("vecs", [2, 128, NV])
    w_in = din("w_in", [2, D, INW])
    w_q_up = din("w_q_up", [2, 256, 384])
    w_kv_up = din("w_kv_up", [2, 128, 768])
    g_sgu_v = din("g_sgu_v", [2, 256])
    w_spatial = din("w_spatial", [2, 4, 128, 128])
    b_spatial = din("b_spatial", [2, 512])
    w_pool = din("w_pool", [2, 4, 64, 64])
    w_out = din("w_out", [2, D, D])
    w_gate = din("w_gate", [2, D, FFH])
    w_up = din("w_up", [2, D, FFH])
    w_down = din("w_down", [2, FFH, D])

    layers = sorted({int(s[1]) for s in stages if s[0] in 'ABCX'})
    need_gath = {}
    pay = {}
    for l in layers:
        if fused:
            pay[l] = [nc.dram_tensor("pay%d_%d" % (l, i), [PIECE_ROWS[i], 2048], BF16, kind="Internal").ap() for i in range(5)]
            need_gath[l] = [nc.dram_tensor("gath%d_%d" % (l, i), [4 * PIECE_ROWS[i], 2048], BF16, kind="Internal").ap() for i in range(5)]
        else:
            if ("A%d" % l) in stages and not (("B%d" % l) in stages):
                pay[l] = [dout("pay%d_%d" % (l, i), [PIECE_ROWS[i], 2048], BF16) for i in range(5)]
            if ("B%d" % l) in stages:
                need_gath[l] = [din("gath%d_%d" % (l, i), [4 * PIECE_ROWS[i], 2048], BF16) for i in range(5)]
    xout = dout("xout", [D, NTOK]) if any(s[0] == 'C' for s in stages) else None
    dbgt = {}
    if dbg:
        dbgt['qT'] = dout("dbg_qT", [96, 4 * NTOK], BF16)
        dbgt['gn'] = dout("dbg_gn", [64, 4 * NTOK], BF16)
        dbgt['aT'] = dout("dbg_aT", [128, 4 * NTOK], BF16)
        dbgt['x'] = dout("dbg_x", [D, NTOK])

    from contextlib import ExitStack
    with ExitStack() as es:
        arena_cols = 212000 // 4
        arena = es.enter_context(nc.sbuf_tensor("arena", [128, arena_cols], F32))
        psb = [es.enter_context(nc.psum_tensor("ps%d" % i, [128, 512], F32)) for i in range(8)]
        sems = {}
        for e in Sched.ENG:
            sems[e] = es.enter_context(nc.semaphore("sem_" + e))
        for k in range(S.ndma):
            sems['dma%d' % k] = es.enter_context(nc.semaphore("sem_dma%d" % k))
        for i in range(5):
            sems['cc%d' % i] = es.enter_context(nc.semaphore("sem_cc%d" % i))

        class Bump:
            def __init__(self, lo, hi):
                self.lo, self.hi, self.p = lo, hi, lo

            def take(self, shape, dt, parts=128):
                esz = 4 if dt in (F32, I32) else 2
                n = 1
                for s_ in shape:
                    n *= s_
                nbytes = (n * esz + 63) // 64 * 64
                off = self.p
                self.p += nbytes
                assert self.p <= self.hi, ("SBUF overflow", self.p, self.hi)
                assert (n * esz) % 4 == 0
                ap = arena[0:parts, off // 4:(off + n * esz) // 4]
                if dt != F32:
                    ap = ap.bitcast(dt)
                if len(shape) == 2:
                    ap = ap.rearrange("p (a b) -> p a b", a=shape[0])
                elif len(shape) == 3:
                    ap = ap.rearrange("p (a b c) -> p a b c", a=shape[0], b=shape[1])
                return ap

        TOT = arena_cols * 4
        pers = Bump(0, TOT)
        xT = pers.take([8, NTOK], F32)
        ones_bf = pers.take([128], BF16)
        ident32 = pers.take([128], F32)
        triu32 = pers.take([128], F32)
        maskT = pers.take([4, 128], BF16)
        apool = pers.take([2, 4, 128], BF16)
        ahalo = pers.take([4, 128], BF16, parts=64)
        selpe = pers.take([2, 96], BF16, parts=64)
        ones32 = pers.take([64], F32, parts=1)
        invf = pers.take([1], F32, parts=96)
        vec_sb = pers.take([NV], F32)
        gv_bc = pers.take([256], F32)
        brow32 = pers.take([512], F32, parts=1)
        epsb = pers.take([1], F32)
        negI = pers.take([128], BF16)
        notmask = pers.take([4, 128], BF16)
        Ctab = pers.take([NTOK], F32, parts=96)
        Stab = pers.take([NTOK], F32, parts=96)
        W1_LO = pers.p
        W1_SZ = 26624
        MIX_LO = W1_LO + W1_SZ
        MIX_SZ = 16384 + 4096 + 16384 + 8192
        T_LO = MIX_LO + MIX_SZ
        assert T_LO < TOT
        wA = Bump(W1_LO, W1_LO + W1_SZ)
        w_in_sb = wA.take([8, 1216], BF16)
        wq_sb = wA.take([2, 384], BF16)
        wqR_sb = wA.take([2, 384], BF16)
        wkv_sb = wA.take([768], BF16)
        wk1_sb = wA.take([4, 96], BF16)
        WsT_sb = wA.take([4, 128], BF16)
        wB = Bump(W1_LO, W1_LO + W1_SZ)
        wpool_sb = wB.take([4, 64], BF16, parts=64)
        wo_a = wB.take([4, D], BF16)
        wo_g = wB.take([4, D], BF16, parts=64)
        wo_p = wB.take([4, D], BF16, parts=64)
        mx = Bump(MIX_LO, MIX_LO + MIX_SZ)
        qT = mx.take([4, NTOK], BF16, parts=96)
        aT0 = mx.take([NTOK], BF16)
        gn = mx.take([4, NTOK], BF16, parts=64)
        pin = mx.take([NBLK, 256], BF16)
        qT_off = MIX_LO

        def aT_ap(h):
            if h == 0:
                return aT0
            off = qT_off + (h - 1) * NTOK * 2
            return arena[0:128, off // 4:(off + NTOK * 2) // 4].bitcast(BF16)

        def aT_res(h, g):
            return "aT%d_%d" % (h, g) if h == 0 else "qT%d_%d" % (h - 1, g)

        PSN = [0]

        def ps_next():
            i = PSN[0]
            PSN[0] = (i + 1) % 8
            return psb[i], "ps%d" % i

        inv_sqrt = {}

        def rstd_from(ps_ap, pres, n, P, sd_ap, sdres, out_ap, ores, cols):
            S.op('act', lambda e: e.activation(out=out_ap, in_=ps_ap, func=AF.Ln, bias=epsb[0:P, 0:1], scale=1.0 / n),
                 reads=[pres, "epsb"], writes=[ores])
            S.op('act', lambda e: e.activation(out=out_ap, in_=out_ap, func=AF.Exp, scale=-0.5), reads=[ores], writes=[ores])

        tb = Bump(MIX_LO, MIX_LO + MIX_SZ)
        st_c = tb.take([2048], F32)
        st_c16 = tb.take([512], F32, parts=64)
        S.dma(lambda e: e.dma_start(out=st_c, in_=cst), writes=["st_c"])
        S.dma(lambda e: e.dma_start(out=st_c16, in_=cst16), writes=["st_c16"])
        S.dma(lambda e: e.dma_start(out=xT, in_=xin.rearrange("(c p) t -> p c t", p=128)),
              writes=["x%d_%d" % (c, g) for c in range(8) for g in range(NGRP)])
        S.op('pool', lambda e: e.memset(ones_bf, 1.0), writes=["ones_bf"])
        S.op('pool', lambda e: e.memset(ones32, 1.0), writes=["ones32"])
        S.op('pool', lambda e: e.memset(epsb, float(EPS)), writes=["epsb"])
        S.op('dve', lambda e: e.tensor_scalar(out=negI, in0=st_c[:, 0:128], scalar1=-30000.0, scalar2=None, op0=ALU.mult),
             reads=["st_c"], writes=["negI"])
        S.op('dve', lambda e: e.tensor_scalar(out=notmask, in0=st_c[:, 256:768].rearrange("p (a b) -> p a b", a=4),
                                              scalar1=-1.0, scalar2=1.0, op0=ALU.mult, op1=ALU.add),
             reads=["st_c"], writes=["notmask"])
        S.op('dve', lambda e: e.tensor_copy(out=ident32, in_=st_c[:, 0:128]), reads=["st_c"], writes=["ident32"])
        S.op('dve', lambda e: e.tensor_copy(out=triu32, in_=st_c[:, 128:256]), reads=["st_c"], writes=["triu32"])
        S.op('dve', lambda e: e.tensor_copy(out=maskT, in_=st_c[:, 256:768].rearrange("p (a b) -> p a b", a=4)),
             reads=["st_c"], writes=["maskT"])
        S.op('dve', lambda e: e.tensor_copy(out=apool, in_=st_c[:, 768:1792].rearrange("p (a b c) -> p a b c", a=2, b=4)),
             reads=["st_c"], writes=["apool"])
        S.op('dve', lambda e: e.tensor_copy(out=selpe, in_=st_c[0:64, 1792:1984].rearrange("p (a b) -> p a b", a=2)),
             reads=["st_c"], writes=["selpe"])
        S.op('dve', lambda e: e.tensor_copy(out=invf, in_=st_c[0:96, 1984:1985]), reads=["st_c"], writes=["invf"])
        S.op('dve', lambda e: e.tensor_copy(out=ahalo, in_=st_c16.rearrange("p (a b) -> p a b", a=4)),
             reads=["st_c16"], writes=["ahalo"])
        posi = tb.take([NTOK], I32, parts=96)
        ang = tb.take([NTOK], F32, parts=96)
        tq = tb.take([NTOK], F32, parts=96)
        kq = tb.take([NTOK], F32, parts=96)
        S.dma(lambda e: e.dma_start(out=posi, in_=pos[0].partition_broadcast(96)), writes=["posi"])
        S.op('dve', lambda e: e.tensor_copy(out=ang, in_=posi), reads=["posi"], writes=["ang"])
        S.op('dve', lambda e: e.tensor_scalar(out=ang, in0=ang, scalar1=invf[:, 0:1], scalar2=None, op0=ALU.mult),
             reads=["ang", "invf"], writes=["ang"])
        for (tab, tres, shift, post) in ((Stab, "Stab", 0.0, 0.0), (Ctab, "Ctab", 0.25, math.pi / 2)):
            S.op('dve', lambda e, shift=shift: e.tensor_scalar(out=tq, in0=ang, scalar1=1.0 / TWO_PI, scalar2=shift,
                                                               op0=ALU.mult, op1=ALU.add),
                 reads=["ang"], writes=["tq"])
            S.op('dve', lambda e: e.tensor_scalar(out=kq, in0=tq, scalar1=MAGIC, scalar2=None, op0=ALU.add),
                 reads=["tq"], writes=["kq"])
            S.op('dve', lambda e: e.tensor_scalar(out=kq, in0=kq, scalar1=MAGIC, scalar2=None, op0=ALU.subtract),
                 reads=["kq"], writes=["kq"])
            S.op('dve', lambda e: e.scalar_tensor_tensor(out=tq, in0=kq, scalar=-C1, in1=ang, op0=ALU.mult, op1=ALU.add),
                 reads=["kq", "ang"], writes=["tq"])
            S.op('dve', lambda e, post=post: e.scalar_tensor_tensor(out=tq, in0=kq, scalar=-C2, in1=tq, op0=ALU.mult, op1=ALU.add),
                 reads=["kq", "tq"], writes=["tq"])
            if post != 0.0:
                S.op('dve', lambda e, post=post: e.tensor_scalar(out=tq, in0=tq, scalar1=post, scalar2=None, op0=ALU.add),
                     reads=["tq"], writes=["tq"])
            S.op('dve', lambda e: e.tensor_scalar(out=tq, in0=tq, scalar1=-PI_SAFE, scalar2=PI_SAFE, op0=ALU.max, op1=ALU.min),
                 reads=["tq"], writes=["tq"])
            S.op('act', lambda e, tab=tab: e.activation(out=tab, in_=tq, func=AF.Sin), reads=["tq"], writes=[tres])

        def load_vecs(l):
            S.dma(lambda e: e.dma_start(out=vec_sb, in_=vecs[l]), writes=["vec"])
            S.dma(lambda e: e.dma_start(out=gv_bc, in_=g_sgu_v[l].partition_broadcast(128)), writes=["gv_bc"])
            S.dma(lambda e: e.dma_start(out=brow32, in_=b_spatial[l:l + 1, :]), writes=["brow32"])

        VC = dict(gmix=0, gffn=8, gql=16, gkv=18, gq=19, gqp=20, gk=21, gkp=22, goa=23, gog=27, gop=31, psc=35)

        def vcol(name, i=0, P=128):
            c = VC[name] + i
            return vec_sb[0:P, c:c + 1]

        def issue_cc(l, pieces):
            for i in pieces:
                if i < 4:
                    rd = ["pay_kt%d_%d" % (h, i) for h in range(4)] + ["pay_v%d_%d" % (h, m) for h in range(4) for m in range(4 * i, 4 * i + 4)]
                else:
                    rd = ["pay_h%d" % m for m in range(NBLK)]
                S.special('pool', 'cc%d' % i, lambda e: e.collective_compute(
                    "AllGather", ALU.bypass, replica_groups=[[0, 1, 2, 3], [4, 5, 6, 7]],
                    ins=[pay[l][i].opt()], outs=[need_gath[l][i].opt()]), reads=rd, writes=["gath%d" % i])

        def stage_A(l, write_pay):
            tb = Bump(T_LO, TOT)
            stgs = [tb.take([INW], F32) for _ in range(4)]
            stg = stgs[0]
            for c in range(8):
                sg_, sgr = stgs[c % 4], ("stg" if c % 4 == 0 else "stg_%d" % (c % 4))
                S.dma(lambda e: e.dma_start(out=sg_, in_=w_in[l, c * 128:(c + 1) * 128, :]), writes=[sgr])
                S.op('dve', lambda e: e.tensor_copy(out=w_in_sb[:, c, 0:416], in_=sg_[:, 0:416]),
                     reads=[sgr], writes=["w_in%d" % c])
                S.op('pool', lambda e: e.tensor_scalar(out=w_in_sb[:, c, 416:432], in0=sg_[:, 400:416], scalar1=-1.0,
                                                       scalar2=None, op0=ALU.mult),
                     reads=[sgr], writes=["w_in%d" % c])
                S.op('pool', lambda e: e.tensor_copy(out=w_in_sb[:, c, 432:448], in_=sg_[:, 384:400]),
                     reads=[sgr], writes=["w_in%d" % c])
                S.op('act', lambda e: e.copy(out=w_in_sb[:, c, 448:1216], in_=sg_[:, 416:1184]),
                     reads=[sgr], writes=["w_in%d" % c])
            stq = stg[:, 0:768].rearrange("p (a b) -> p a b", a=2)
            S.dma(lambda e: e.dma_start(out=stq, in_=w_q_up[l].rearrange("(c p) n -> p c n", p=128)), writes=["stg"])
            S.op('pool', lambda e: e.tensor_copy(out=wq_sb, in_=stq), reads=["stg"], writes=["wq"])
            S.op('pool', lambda e: e.memset(wqR_sb, 0.0), writes=["wqR"])
            stq4 = stg[:, 0:768].rearrange("p (a b) -> p a b", a=8)
            wqR4 = wqR_sb.rearrange("p c (h d) -> p (c h) d", h=4)
            S.op('pool', lambda e: e.tensor_scalar(out=wqR4[:, :, 64:80], in0=stq4[:, :, 80:96], scalar1=-1.0, scalar2=None,
                                                   op0=ALU.mult), reads=["stg"], writes=["wqR"])
            S.op('pool', lambda e: e.tensor_copy(out=wqR4[:, :, 80:96], in_=stq4[:, :, 64:80]), reads=["stg"], writes=["wqR"])
            S.dma(lambda e: e.dma_start(out=stg[:, 0:768], in_=w_kv_up[l]), writes=["stg"])
            S.op('pool', lambda e: e.tensor_copy(out=wkv_sb, in_=stg[:, 0:768]), reads=["stg"], writes=["wkv"])
            S.op('pool', lambda e: e.memset(wk1_sb, 0.0), writes=["wk1"])
            S.op('pool', lambda e: e.tensor_copy(out=wk1_sb[:, :, 0:64],
                                                 in_=stg[:, 0:768].rearrange("p (h d) -> p h d", h=4)[:, :, 0:64]),
                 reads=["stg"], writes=["wk1"])
            for h in range(4):
                sgh, sghr = stgs[h], ("stg" if h == 0 else "stg_%d" % h)
                S.dma(lambda e: e.dma_start(out=sgh[:, 0:128], in_=w_spatial[l, h]), writes=[sghr])
                pt, pr = ps_next()
                S.op('pe', lambda e: e.transpose(out=pt[:, 0:128], in_=sgh[:, 0:128], identity=ident32),
                     reads=[sghr, "ident32"], writes=[pr])
                S.op('dve', lambda e, pt=pt, h=h: e.tensor_tensor(out=WsT_sb[:, h, :], in0=pt[:, 0:128], in1=triu32, op=ALU.mult),
                     reads=[pr, "triu32"], writes=["WsT"])
            S.barrier()
            tb = Bump(T_LO, TOT)
            sqb = [tb.take([512], BF16) for _ in range(2)]
            hT = tb.take([8, 512], BF16)
            rst = [tb.take([512], F32) for _ in range(2)]
            sd = tb.take([16], F32)
            qlat_n = tb.take([2, 512], BF16)
            kvlat_n = tb.take([512], BF16)
            kpe_sb = tb.take([512], BF16, parts=64)
            uT = tb.take([4, 512], BF16, parts=64)
            gm = tb.take([4, 512], F32, parts=64)
            v_n = tb.take([4, 256], BF16)
            t1 = tb.take([512], F32, parts=96)
            t2 = tb.take([512], F32, parts=96)
            rxb = [tb.take([512], F32) for _ in range(2)]
            t2k = t2
            KTg = tb.take([4, 512], BF16, parts=96)
            Vb = tb.take([512], BF16)
            ssv = tb.take([12], F32)
            SQI = [0]
            RSI = [0]

            def sq_next():
                i = SQI[0]
                SQI[0] = (i + 1) % 2
                return sqb[i], "sqb%d" % i

            def rst_next():
                i = RSI[0]
                RSI[0] = (i + 1) % 2
                return rst[i], "rst%d" % i

            for g in range(NGRP):
                gs = slice(g * 512, (g + 1) * 512)
                def x_stats(g2):
                    gs2 = slice(g2 * 512, (g2 + 1) * 512)
                    pss, pssr = ps_next()
                    for c in range(8):
                        sq, sqr = sq_next()
                        S.op('act', lambda e: e.activation(out=sq, in_=xT[:, c, gs2], func=AF.Square),
                             reads=["x%d_%d" % (c, g2)], writes=[sqr])
                        S.op('pe', lambda e: e.matmul(pss[:, :], ones_bf, sq, start=(c == 0), stop=(c == 7)),
                             reads=[sqr, "ones_bf"], writes=[pssr])
                    rstd_from(pss[:, :], pssr, 1024.0, 128, sd, "sd", rxb[g2 % 2], "rxb%d" % (g2 % 2), 512)

                if g == 0:
                    x_stats(0)
                rx, rxr = rxb[g % 2], "rxb%d" % (g % 2)
                for c in range(8):
                    S.op('dve', lambda e: e.scalar_tensor_tensor(out=hT[:, c, :], in0=xT[:, c, gs], scalar=vcol('gmix', c),
                                                                 in1=rx, op0=ALU.mult, op1=ALU.mult),
                         reads=["x%d_%d" % (c, g), rxr, "vec"], writes=["hT%d" % c])
                hres = ["hT%d" % c for c in range(8)]
                wres = ["w_in%d" % c for c in range(8)]

                def proj_ii(col0, M, pt, pr):
                    for c in range(8):
                        S.op('pe', lambda e, c=c: e.matmul(pt[0:M, :], w_in_sb[:, c, col0:col0 + M], hT[:, c, :],
                                                           start=(c == 0), stop=(c == 7)),
                             reads=hres + wres, writes=[pr])

                pq = [ps_next(), ps_next()]
                for mt in range(2):
                    proj_ii(mt * 128, 128, pq[mt][0], pq[mt][1])
                pss, pssr = ps_next()
                for mt in range(2):
                    sq, sqr = sq_next()
                    S.op('act', lambda e, sq=sq, mt=mt: e.activation(out=sq, in_=pq[mt][0][:, :], func=AF.Square),
                         reads=[pq[mt][1]], writes=[sqr])
                    S.op('pe', lambda e, sq=sq, mt=mt, pss=pss: e.matmul(pss[:, :], ones_bf, sq, start=(mt == 0), stop=(mt == 1)),
                         reads=[sqr, "ones_bf"], writes=[pssr])
                rq, rqr = rst_next()
                rstd_from(pss[:, :], pssr, 256.0, 128, sd, "sd", rq, rqr, 512)
                for mt in range(2):
                    S.op('dve', lambda e, mt=mt, rq=rq: e.scalar_tensor_tensor(out=qlat_n[:, mt, :], in0=pq[mt][0][:, :],
                                                                               scalar=vcol('gql', mt), in1=rq,
                                                                               op0=ALU.mult, op1=ALU.mult),
                         reads=[pq[mt][1], rqr, "vec"], writes=["qlat_n"])
                pk, pkr = ps_next()
                proj_ii(256, 128, pk, pkr)
                sq, sqr = sq_next()
                S.op('act', lambda e, sq=sq, pk=pk: e.activation(out=sq, in_=pk[:, :], func=AF.Square), reads=[pkr], writes=[sqr])
                pss, pssr = ps_next()
                S.op('pe', lambda e, sq=sq, pss=pss: e.matmul(pss[:, :], ones_bf, sq, start=True, stop=True),
                     reads=[sqr, "ones_bf"], writes=[pssr])
                rk, rkr = rst_next()
                rstd_from(pss[:, :], pssr, 128.0, 128, sd, "sd", rk, rkr, 512)
                S.op('dve', lambda e, pk=pk, rk=rk: e.scalar_tensor_tensor(out=kvlat_n, in0=pk[:, :], scalar=vcol('gkv'), in1=rk,
                                                                           op0=ALU.mult, op1=ALU.mult),
                     reads=[pkr, rkr, "vec"], writes=["kvlat_n"])
                pp, ppr = ps_next()
                proj_ii(384, 64, pp, ppr)
                S.op('act', lambda e, pp=pp: e.copy(out=kpe_sb, in_=pp[0:64, :]), reads=[ppr], writes=["kpe"])
                for h in range(4):
                    pu, pur = ps_next()
                    proj_ii(448 + h * 64, 64, pu, pur)
                    S.op('act', lambda e, pu=pu, h=h: e.copy(out=uT[:, h, :], in_=pu[0:64, :]), reads=[pur], writes=["uT"])
                if g + 1 < NGRP:
                    x_stats(g + 1)
                pvs = []
                for b in range(4):
                    m = 4 * g + b
                    pv, pvr = ps_next()
                    pvs.append((pv, pvr))
                    for c in range(8):
                        S.op('pe', lambda e: e.matmul(pv[:, :], hT[:, c, b * 128:(b + 1) * 128], w_in_sb[:, c, 704:1216],
                                                      start=(c == 0), stop=(c == 7)),
                             reads=hres + wres, writes=[pvr])
                    sq, sqr = sq_next()
                    S.op('act', lambda e: e.activation(out=sq[:, 0:256], in_=pv[:, 0:256], func=AF.Square, accum_out=ssv[:, b:b + 1]),
                         reads=[pvr], writes=[sqr, "ssv"])
                    S.op('act', lambda e: e.copy(out=pin[:, m, :], in_=pv[:, 256:512]), reads=[pvr], writes=["pin%d" % m])
                S.op('act', lambda e: e.activation(out=ssv[:, 4:8], in_=ssv[:, 0:4], func=AF.Ln, bias=epsb[:, 0:1], scale=1.0 / 256),
                     reads=["ssv", "epsb"], writes=["ssv1"])
                S.op('act', lambda e: e.activation(out=ssv[:, 8:12], in_=ssv[:, 4:8], func=AF.Exp, scale=-0.5), reads=["ssv1"], writes=["ssv2"])
                for b in range(4):
                    pv, pvr = pvs[b]
                    S.op('dve', lambda e: e.scalar_tensor_tensor(out=v_n[:, b, :], in0=pv[:, 0:256], scalar=ssv[:, 8 + b:9 + b], in1=gv_bc,
                                                                 op0=ALU.mult, op1=ALU.mult),
                         reads=[pvr, "ssv2", "gv_bc"], writes=["v_n%d" % b])
                for b in range(4):
                    pz, pzr = ps_next()
                    for h in range(4):
                        S.op('pe', lambda e: e.matmul(pz[0:64, h * 128:(h + 1) * 128], v_n[:, b, h * 64:(h + 1) * 64],
                                                      WsT_sb[:, h, :], start=True, stop=False),
                             reads=["v_n%d" % b, "WsT"], writes=[pzr])
                        S.op('pe', lambda e: e.matmul(pz[0:64, h * 128:(h + 1) * 128], ones32[0:1, 0:64],
                                                      brow32[0:1, h * 128:(h + 1) * 128], start=False, stop=True),
                             reads=["ones32", "brow32"], writes=[pzr])
                    S.op('dve', lambda e: e.tensor_tensor(out=gm[:, :, b * 128:(b + 1) * 128],
                                                          in0=pz[0:64, :].rearrange("p (h t) -> p h t", h=4),
                                                          in1=uT[:, :, b * 128:(b + 1) * 128], op=ALU.mult),
                         reads=[pzr, "uT"], writes=["gm"])
                pss, pssr = ps_next()
                for h in range(4):
                    sq, sqr = sq_next()
                    S.op('act', lambda e: e.activation(out=sq[0:64, :], in_=gm[:, h, :], func=AF.Square),
                         reads=["gm"], writes=[sqr])
                    S.op('pe', lambda e: e.matmul(pss[0:64, :], ones_bf[0:64, 0:64], sq[0:64, :], start=(h == 0), stop=(h == 3)),
                         reads=[sqr, "ones_bf"], writes=[pssr])
                rg, rgr = rst_next()
                rstd_from(pss[0:64, :], pssr, 256.0, 64, sd[0:64, :], "sd", rg[0:64, :], rgr, 512)
                for h in range(4):
                    S.op('dve', lambda e: e.scalar_tensor_tensor(out=gn[:, h, gs], in0=gm[:, h, :], scalar=vcol('gog', h, 64),
                                                                 in1=rg[0:64, :], op0=ALU.mult, op1=ALU.mult),
                         reads=["gm", rgr, "vec"], writes=["gn%d" % (4 * g + b) for b in range(4)])
                def head_norm_rope(praw, prawr, pR, pRr, gname, gpname, out_ap, ores, t2_pre=None):
                    sq, sqr = sq_next()
                    S.op('act', lambda e: e.activation(out=sq[0:96, :], in_=praw[0:96, :], func=AF.Square),
                         reads=[prawr], writes=[sqr])
                    pss, pssr = ps_next()
                    S.op('pe', lambda e: e.matmul(pss[0:96, :], ones_bf[0:96, 0:96], sq[0:96, :], start=True, stop=True),
                         reads=[sqr, "ones_bf"], writes=[pssr])
                    rr, rrr = rst_next()
                    rstd_from(pss[0:96, :], pssr, 96.0, 96, sd[0:96, :], "sd", rr[0:96, :], rrr, 512)
                    S.op('dve', lambda e: e.scalar_tensor_tensor(out=t1, in0=praw[0:96, :], scalar=vcol(gname, 0, 96),
                                                                 in1=Ctab[:, gs], op0=ALU.mult, op1=ALU.mult),
                         reads=[prawr, "Ctab", "vec"], writes=["t1"])
                    if t2_pre is None:
                        S.op('dve', lambda e: e.scalar_tensor_tensor(out=t2, in0=pR[0:96, :], scalar=vcol(gpname, 0, 96),
                                                                     in1=Stab[:, gs], op0=ALU.mult, op1=ALU.mult),
                             reads=[pRr, "Stab", "vec"], writes=["t2"])
                        t2u, t2r = t2, "t2"
                    else:
                        t2u, t2r = t2_pre
                    S.op('dve', lambda e: e.tensor_tensor(out=t1, in0=t1, in1=t2u, op=ALU.add), reads=["t1", t2r], writes=["t1"])
                    S.op('dve', lambda e: e.tensor_tensor(out=out_ap, in0=t1, in1=rr[0:96, :], op=ALU.mult),
                         reads=["t1", rrr], writes=[ores])

                for h in range(4):
                    pqa, pqar = ps_next()
                    pqb, pqbr = ps_next()
                    for c in range(2):
                        S.op('pe', lambda e, c=c, h=h, pqa=pqa: e.matmul(pqa[0:96, :], wq_sb[:, c, h * 96:(h + 1) * 96], qlat_n[:, c, :],
                                                                         start=(c == 0), stop=(c == 1)),
                             reads=["wq", "qlat_n"], writes=[pqar])
                    for c in range(2):
                        S.op('pe', lambda e, c=c, h=h, pqb=pqb: e.matmul(pqb[0:96, :], wqR_sb[:, c, h * 96:(h + 1) * 96], qlat_n[:, c, :],
                                                                         start=(c == 0), stop=(c == 1)),
                             reads=["wqR", "qlat_n"], writes=[pqbr])
                    head_norm_rope(pqa, pqar, pqb, pqbr, 'gq', 'gqp', qT[:, h, gs], "qT%d_%d" % (h, g))
                pkR, pkRr = ps_next()
                S.op('pe', lambda e: e.matmul(pkR[0:96, :], selpe[:, 1, :], kpe_sb, start=True, stop=True),
                     reads=["selpe", "kpe"], writes=[pkRr])
                S.op('dve', lambda e: e.scalar_tensor_tensor(out=t2k, in0=pkR[0:96, :], scalar=vcol('gkp', 0, 96),
                                                             in1=Stab[:, gs], op0=ALU.mult, op1=ALU.mult),
                     reads=[pkRr, "Stab", "vec"], writes=["t2"])
                for h in range(4):
                    pka, pkar = ps_next()
                    S.op('pe', lambda e, h=h, pka=pka: e.matmul(pka[0:96, :], wk1_sb[:, h, :], kvlat_n, start=True, stop=False),
                         reads=["wk1", "kvlat_n"], writes=[pkar])
                    S.op('pe', lambda e, h=h, pka=pka: e.matmul(pka[0:96, :], selpe[:, 0, :], kpe_sb, start=False, stop=True),
                         reads=["selpe", "kpe"], writes=[pkar])
                    head_norm_rope(pka, pkar, None, None, 'gk', 'gkp', KTg[:, h, :], "KTg", t2_pre=(t2k, "t2"))
                if write_pay:
                    for h in range(4):
                        kdst = pay[l][g][h * 56:h * 56 + 24, :].rearrange("r (c t) -> (r c) t", c=4)
                        S.dma(lambda e: e.dma_start(out=kdst, in_=KTg[:, h, :]),
                              reads=["KTg"], writes=["pay_kt%d_%d" % (h, g)])
                for b in range(4):
                    m = 4 * g + b
                    pv, pvr = ps_next()
                    S.op('pe', lambda e, pv=pv, b=b: e.matmul(
                        pv[:, :].rearrange("p (h d) -> p h d", h=4), kvlat_n[:, b * 128:(b + 1) * 128],
                        wkv_sb.rearrange("p (h d) -> p h d", h=4)[:, :, 64:192], start=True, stop=True),
                         reads=["kvlat_n", "wkv"], writes=[pvr])
                    S.op('act', lambda e, pv=pv: e.copy(out=Vb, in_=pv[:, :]), reads=[pvr], writes=["Vb"])
                    if write_pay:
                        for h in range(4):
                            dst = pay[l][g][h * 56 + 24 + b * 8:h * 56 + 32 + b * 8, :].rearrange("r (q d) -> (r q) d", q=16)
                            S.dma(lambda e: e.dma_start(out=dst, in_=Vb[:, h * 128:(h + 1) * 128]),
                                  reads=["Vb"], writes=["pay_v%d_%d" % (h, m)])
                        hd = pay[l][4].rearrange("(m r) (q d) -> m (r q) d", m=NBLK, q=8)[m]
                        S.dma(lambda e: e.dma_start(out=hd, in_=pin[112:128, m, :]),
                              reads=["pin%d" % m], writes=["pay_h%d" % m])
                if fused and write_pay:
                    issue_cc(l, [g])
            if fused and write_pay:
                issue_cc(l, [4])
            if dbg:
                S.dma(lambda e: e.dma_start(out=dbgt['qT'], in_=qT.rearrange("p h t -> p (h t)")),
                      reads=["qT%d_%d" % (h, g) for h in range(4) for g in range(4)], writes=["dbg_qT"])
                S.dma(lambda e: e.dma_start(out=dbgt['gn'], in_=gn.rearrange("p h t -> p (h t)")),
                      reads=["gn%d" % m for m in range(NBLK)], writes=["dbg_gn"])
            S.barrier()

        def stage_X(l):
            return

        def stage_B(l):
            gath = need_gath[l]
            tb = Bump(T_LO, TOT)
            KTh = tb.take([4 * NTOK], BF16, parts=96)
            Vh = tb.take([64, 128], BF16)
            pT = [tb.take([512], BF16) for _ in range(4)]
            rden = tb.take([512], F32)
            stgs_ = [tb.take([D], F32) for _ in range(2)]
            scale = 1.0 / math.sqrt(96.0)
            LA = 3
            SBK = (0, 1, 2, 7)
            phases = [(h, (0, 1, 2)) for h in range(4)] + [(h, (3,)) for h in range(4)]
            items = []
            unit = 0
            for pi, (h, qgs) in enumerate(phases):
                for g in qgs:
                    sl = [(r, mk) for r in range(4) for mk in range(4 * g + 4)]
                    for si, (r, mk) in enumerate(sl):
                        items.append(dict(h=h, g=g, r=r, mk=mk, first=(si == 0), last=(si == len(sl) - 1), hg=unit, ph=pi,
                                          lastg=(g == qgs[-1])))
                    unit += 1

            def load_kv(pi, r):
                h_, qgs_ = phases[pi]
                for g_ in range(qgs_[-1] + 1):
                    ksrc = gath[g_][r * 224 + h_ * 56:r * 224 + h_ * 56 + 24, :].rearrange("r (c t) -> (r c) t", c=4)
                    S.dma(lambda e: e.dma_start(out=KTh[:, r * NTOK + g_ * 512:r * NTOK + (g_ + 1) * 512], in_=ksrc),
                          reads=["gath%d" % g_], writes=["KTh%d_%d" % (r, g_)])
                    vsrc = gath[g_][r * 224 + h_ * 56 + 24:r * 224 + h_ * 56 + 56, :].rearrange("(m r2) (q d) -> (r2 q) m d", m=4, q=16)
                    S.dma(lambda e: e.dma_start(out=Vh[:, r * 16 + 4 * g_:r * 16 + 4 * g_ + 4, :], in_=vsrc),
                          reads=["gath%d" % g_], writes=["Vh%d_%d" % (r, g_)])

            for r in range(4):
                load_kv(0, r)
            for c in range(4):
                S.dma(lambda e, c=c: e.dma_start(out=stgs_[c % 2], in_=w_out[l, c * 128:(c + 1) * 128, :]), writes=["stgB%d" % (c % 2)])
                S.op('pool', lambda e, c=c: e.tensor_copy(out=wo_a[:, c, :], in_=stgs_[c % 2]), reads=["stgB%d" % (c % 2)], writes=["wo_a"])
            for (wsb, base, nm) in ((wo_g, 512, "wo_g"), (wo_p, 768, "wo_p")):
                for c in range(4):
                    S.dma(lambda e, c=c, base=base: e.dma_start(out=stgs_[c % 2][0:64, :], in_=w_out[l, base + c * 64:base + (c + 1) * 64, :]),
                          writes=["stgB%d" % (c % 2)])
                    S.op('pool', lambda e, c=c, wsb=wsb: e.tensor_copy(out=wsb[:, c, :], in_=stgs_[c % 2][0:64, :]), reads=["stgB%d" % (c % 2)], writes=[nm])
            S.dma(lambda e: e.dma_start(out=stgs_[0][0:64, 0:256].rearrange("p (g d) -> p g d", g=4),
                                        in_=w_pool[l].rearrange("g c d -> c g d")), writes=["stgB0"])
            S.op('pool', lambda e: e.tensor_copy(out=wpool_sb, in_=stgs_[0][0:64, 0:256].rearrange("p (g d) -> p g d", g=4)),
                 reads=["stgB0"], writes=["wpool"])
            NI = len(items)
            for idx in range(NI + LA):
                if idx < NI:
                    it_ = items[idx]
                    h, g, r, mk = it_['h'], it_['g'], it_['r'], it_['mk']
                    slot = r * 16 + mk
                    mlo = max(0, mk - 4 * g)
                    cs = slice(mlo * 128, 512)
                    pS, pSr = psb[SBK[idx % 4]], "ps%d" % SBK[idx % 4]
                    pt_, ptr = pT[idx % 4], "pT%d" % (idx % 4)
                    diag = mk >= 4 * g
                    S.op('pe', lambda e: e.matmul(pS[:, cs], KTh[:, slot * 128:(slot + 1) * 128],
                                                  qT[:, h, g * 512 + mlo * 128:(g + 1) * 512], start=True, stop=not diag),
                         reads=["KTh%d_%d" % (r, mk // 4), "qT%d_%d" % (h, g)], writes=[pSr])
                    if diag:
                        mb = mk - 4 * g
                        S.op('pe', lambda e: e.matmul(pS[:, mb * 128:(mb + 1) * 128], negI, notmask[:, r, :], start=False, stop=True),
                             reads=["negI", "notmask"], writes=[pSr])
                    S.op('act', lambda e: e.activation(out=pt_[:, cs], in_=pS[:, cs], func=AF.Exp, scale=scale),
                         reads=[pSr], writes=[ptr])
                if idx >= LA:
                    j_ = idx - LA
                    it_ = items[j_]
                    h, g, r, mk = it_['h'], it_['g'], it_['r'], it_['mk']
                    slot = r * 16 + mk
                    mlo = max(0, mk - 4 * g)
                    cs = slice(mlo * 128, 512)
                    pt_, ptr = pT[j_ % 4], "pT%d" % (j_ % 4)
                    po, por = psb[3 + (it_['hg'] % 2)], "ps%d" % (3 + (it_['hg'] % 2))
                    pd, pdr = psb[5 + (it_['hg'] % 2)], "ps%d" % (5 + (it_['hg'] % 2))
                    S.op('pe', lambda e: e.matmul(po[:, cs], Vh[:, slot, :], pt_[:, cs], start=it_['first'], stop=it_['last']),
                         reads=[ptr, "Vh%d_%d" % (r, mk // 4)], writes=[por])
                    S.op('pe', lambda e: e.matmul(pd[:, cs], ones_bf, pt_[:, cs], start=it_['first'], stop=it_['last']),
                         reads=[ptr, "ones_bf"], writes=[pdr])
                    if it_['lastg'] and mk == 4 * g + 3 and it_['ph'] + 1 < len(phases):
                        load_kv(it_['ph'] + 1, r)
                    if it_['last']:
                        S.op('dve', lambda e: e.reciprocal(out=rden, in_=pd[:, :]), reads=[pdr], writes=["rden"])
                        S.op('dve', lambda e: e.tensor_tensor(out=aT_ap(h)[:, g * 512:(g + 1) * 512], in0=po[:, :], in1=rden, op=ALU.mult),
                             reads=[por, "rden"], writes=[aT_res(h, g)])
            S.barrier()
            if dbg:
                for h in range(4):
                    S.dma(lambda e, h=h: e.dma_start(out=dbgt['aT'][:, h * NTOK:(h + 1) * NTOK], in_=aT_ap(h)), writes=["dbg_aT%d" % h])
                S.barrier()
            tb = Bump(T_LO, TOT)
            sqb = [tb.take([512], BF16) for _ in range(2)]
            rst = [tb.take([512], F32) for _ in range(3)]
            sd = tb.take([16], F32)
            an2 = [tb.take([4, 512], BF16) for _ in range(2)]
            halo2 = [tb.take([4, 256], BF16, parts=64) for _ in range(2)]
            pm2 = [tb.take([4, 128], BF16, parts=64) for _ in range(2)]
            py2 = [tb.take([4, 512], F32, parts=64) for _ in range(2)]
            pn2 = [tb.take([4, 512], BF16, parts=64) for _ in range(2)]
            for i_ in range(2):
                S.op('pool', lambda e: e.memset(halo2[i_], 0.0), writes=["halo%d_%d" % (i_, r) for r in range(4)])
            SQI = [0]
            RSI = [0]
            PMI = [0]

            def sq_next():
                i = SQI[0]
                SQI[0] = (i + 1) % 2
                return sqb[i], "sqb%d" % i

            def rst_next():
                i = RSI[0]
                RSI[0] = (i + 1) % 3
                return rst[i], "rst%d" % i

            def chain(g):
                par = g % 2
                an, halo, py, pn = an2[par], halo2[par], py2[par], pn2[par]
                anr, pyr, pnr = "an%d" % par, "py%d" % par, "pn%d" % par
                gs = slice(g * 512, (g + 1) * 512)
                pss, pssr = ps_next()
                for h in range(4):
                    sq, sqr = sq_next()
                    S.op('act', lambda e: e.activation(out=sq, in_=aT_ap(h)[:, gs], func=AF.Square),
                         reads=[aT_res(h, g)], writes=[sqr])
                    S.op('pe', lambda e: e.matmul(pss[:, :], ones_bf, sq, start=(h == 0), stop=(h == 3)),
                         reads=[sqr, "ones_bf"], writes=[pssr])
                yield
                ra, rar = rst_next()
                rstd_from(pss[:, :], pssr, 512.0, 128, sd, "sd", ra, rar, 512)
                for h in range(4):
                    S.op('dve', lambda e: e.scalar_tensor_tensor(out=an[:, h, :], in0=aT_ap(h)[:, gs], scalar=vcol('goa', h),
                                                                 in1=ra, op0=ALU.mult, op1=ALU.mult),
                         reads=[aT_res(h, g), rar, "vec"], writes=[anr])
                for r in range(4):
                    hsrc = gath[4][r * 32:(r + 1) * 32, :].rearrange("(m r2) (q d) -> (r2 q) m d", m=NBLK, q=8)
                    m0 = 4 * g if r < 3 else 4 * g - 1
                    b0_ = 0
                    if m0 < 0:
                        m0, b0_ = 0, 1
                    nb = 4 - b0_
                    S.dma(lambda e: e.dma_start(out=halo[16 * r:16 * r + 16, b0_:4, :], in_=hsrc[:, m0:m0 + nb, :]),
                          reads=["gath4"], writes=["halo%d_%d" % (par, r)])
                for b in range(4):
                    m = 4 * g + b
                    var = 0 if m == 0 else 1
                    ppm, ppmr = ps_next()
                    for gi in range(4):
                        S.op('pe', lambda e: e.matmul(ppm[0:64, gi * 128:(gi + 1) * 128], pin[:, m, gi * 64:(gi + 1) * 64],
                                                      apool[:, var, gi, :], start=True, stop=False),
                             reads=["pin%d" % m, "apool"], writes=[ppmr])
                        S.op('pe', lambda e: e.matmul(ppm[0:64, gi * 128:(gi + 1) * 128], halo[:, b, gi * 64:(gi + 1) * 64],
                                                      ahalo[:, gi, :], start=False, stop=True),
                             reads=["halo%d_%d" % (par, r) for r in range(4)] + ["ahalo"], writes=[ppmr])
                    yield
                    pmi = PMI[0] % 2
                    PMI[0] += 1
                    pm, pmr = pm2[pmi], "pm%d" % pmi
                    S.op('act', lambda e: e.copy(out=pm, in_=ppm[0:64, :].rearrange("p (g t) -> p g t", g=4)),
                         reads=[ppmr], writes=[pmr])
                    ppy, ppyr = ps_next()
                    for gi in range(4):
                        S.op('pe', lambda e: e.matmul(ppy[0:64, gi * 128:(gi + 1) * 128], wpool_sb[:, gi, :], pm[:, gi, :],
                                                      start=True, stop=True),
                             reads=["wpool", pmr], writes=[ppyr])
                    for gi in range(4):
                        S.op('dve', lambda e: e.tensor_scalar(out=py[:, gi, b * 128:(b + 1) * 128], in0=ppy[0:64, gi * 128:(gi + 1) * 128],
                                                              scalar1=vcol('psc', gi, 64), scalar2=None, op0=ALU.mult),
                             reads=[ppyr, "vec"], writes=[pyr])
                    yield
                pss, pssr = ps_next()
                for gi in range(4):
                    sq, sqr = sq_next()
                    S.op('act', lambda e: e.activation(out=sq[0:64, :], in_=py[:, gi, :], func=AF.Square),
                         reads=[pyr], writes=[sqr])
                    S.op('pe', lambda e: e.matmul(pss[0:64, :], ones_bf[0:64, 0:64], sq[0:64, :], start=(gi == 0), stop=(gi == 3)),
                         reads=[sqr, "ones_bf"], writes=[pssr])
                yield
                rp, rpr = rst_next()
                rstd_from(pss[0:64, :], pssr, 256.0, 64, sd[0:64, :], "sd", rp[0:64, :], rpr, 512)
                for gi in range(4):
                    S.op('dve', lambda e: e.scalar_tensor_tensor(out=pn[:, gi, :], in0=py[:, gi, :], scalar=vcol('gop', gi, 64),
                                                                 in1=rp[0:64, :], op0=ALU.mult, op1=ALU.mult),
                         reads=[pyr, rpr, "vec"], writes=[pnr])

            def wout(g):
                par = g % 2
                an, pn = an2[par], pn2[par]
                anr, pnr = "an%d" % par, "pn%d" % par
                gs = slice(g * 512, (g + 1) * 512)
                for dt_ in range(8):
                    pyo, pyor = ps_next()
                    ds = slice(dt_ * 128, (dt_ + 1) * 128)
                    for c in range(4):
                        S.op('pe', lambda e: e.matmul(pyo[:, :], wo_a[:, c, ds], an[:, c, :], start=(c == 0), stop=False),
                             reads=["wo_a", anr], writes=[pyor])
                    for c in range(4):
                        S.op('pe', lambda e: e.matmul(pyo[:, :], wo_g[:, c, ds], gn[:, c, gs], start=False, stop=False),
                             reads=["wo_g"] + ["gn%d" % (4 * g + b) for b in range(4)], writes=[pyor])
                    for c in range(4):
                        S.op('pe', lambda e: e.matmul(pyo[:, :], wo_p[:, c, ds], pn[:, c, :], start=False, stop=(c == 3)),
                             reads=["wo_p", pnr], writes=[pyor])
                    S.op('dve', lambda e: e.tensor_tensor(out=xT[:, dt_, gs], in0=pyo[:, :], in1=xT[:, dt_, gs], op=ALU.add),
                         reads=[pyor, "x%d_%d" % (dt_, g)], writes=["x%d_%d" % (dt_, g)])
                    yield

            for _ in chain(0):
                pass
            for g in range(NGRP):
                gens = [wout(g)]
                if g + 1 < NGRP:
                    gens.append(chain(g + 1))
                while gens:
                    for gen in list(gens):
                        try:
                            next(gen)
                        except StopIteration:
                            gens.remove(gen)
            S.barrier()

        def stage_C(l, stream_out=False):
            tb = Bump(W1_LO, TOT)
            hF = tb.take([8, NTOK], BF16)
            aF = tb.take([6, NTOK], BF16)
            wd = tb.take([6, D], BF16)
            wg = [tb.take([8, 256], BF16) for _ in range(2)]
            wu = [tb.take([8, 256], BF16) for _ in range(2)]
            stgA = [tb.take([8, 256], F32) for _ in range(2)]
            stgB = tb.take([D], F32)
            sqb = [tb.take([512], BF16) for _ in range(2)]
            rx = tb.take([512], F32)
            rx2 = tb.take([512], F32)
            sd = tb.take([16], F32)
            sg = [tb.take([512], F32) for _ in range(2)]
            pairs = []
            ht0 = 0
            for ch, nt in enumerate(FF_CHUNKS):
                j = 0
                while j < nt:
                    npair = min(2, nt - j)
                    pairs.append((ch, j, ht0 + j, npair))
                    j += npair
                ht0 += nt
            SA = [0]

            def load_pair(pi):
                ch, j, ht, npair = pairs[pi]
                ncol = npair * 128
                b_ = pi % 2
                for (wsrc, wdst, wr) in ((w_gate, wg[b_], "wg%d" % b_), (w_up, wu[b_], "wu%d" % b_)):
                    si = SA[0] % 2
                    SA[0] += 1
                    S.dma(lambda e: e.dma_start(out=stgA[si][:, :, 0:ncol],
                                                in_=wsrc[l, :, ht * 128:ht * 128 + ncol].rearrange("(c p) n -> p c n", p=128)),
                          writes=["stgA%d" % si])
                    S.op('act', lambda e: e.copy(out=wdst[:, :, 0:ncol], in_=stgA[si][:, :, 0:ncol]),
                         reads=["stgA%d" % si], writes=[wr])

            def load_wd(ch):
                base = sum(FF_CHUNKS[:ch])
                for j in range(FF_CHUNKS[ch]):
                    ht = base + j
                    S.dma(lambda e: e.dma_start(out=stgB, in_=w_down[l, ht * 128:(ht + 1) * 128, :]), writes=["stgB"])
                    S.op('pool', lambda e: e.tensor_copy(out=wd[:, j, :], in_=stgB), reads=["stgB"], writes=["wd"])

            load_pair(0)
            load_pair(1)
            load_wd(0)
            def ffn_norm(g):
                gs = slice(g * 512, (g + 1) * 512)
                pss, pssr = ps_next()
                for c in range(8):
                    sq, sqr = sqb[c % 2], "sqb%d" % (c % 2)
                    S.op('act', lambda e: e.activation(out=sq, in_=xT[:, c, gs], func=AF.Square),
                         reads=["x%d_%d" % (c, g)], writes=[sqr])
                    S.op('pe', lambda e: e.matmul(pss[:, :], ones_bf, sq, start=(c == 0), stop=(c == 7)),
                         reads=[sqr, "ones_bf"], writes=[pssr])
                rxg, rxgr = (rx, "rx") if g % 2 == 0 else (rx2, "rx2")
                rstd_from(pss[:, :], pssr, 1024.0, 128, sd, "sd", rxg, rxgr, 512)
                for c in range(8):
                    S.op('dve', lambda e: e.scalar_tensor_tensor(out=hF[:, c, gs], in0=xT[:, c, gs], scalar=vcol('gffn', c),
                                                                 in1=rxg, op0=ALU.mult, op1=ALU.mult),
                         reads=["x%d_%d" % (c, g), rxgr, "vec"], writes=["hF%d" % g])
            SG = [0]
            for pi, (ch, j, ht, npair) in enumerate(pairs):
                b_ = pi % 2
                wgb, wub = wg[b_], wu[b_]
                wgr, wur = "wg%d" % b_, "wu%d" % b_
                for jj in range(npair):
                    for g in range(NGRP):
                        if pi == 0 and jj == 0:
                            if g == 0:
                                ffn_norm(0)
                                ffn_norm(1)
                            elif g + 1 < NGRP:
                                ffn_norm(g + 1)
                        gs = slice(g * 512, (g + 1) * 512)
                        pg, pgr = ps_next()
                        pu, pur = ps_next()
                        for c in range(8):
                            S.op('pe', lambda e: e.matmul(pg[:, :], wgb[:, c, jj * 128:(jj + 1) * 128], hF[:, c, gs],
                                                          start=(c == 0), stop=(c == 7)),
                                 reads=[wgr, "hF%d" % g], writes=[pgr])
                        for c in range(8):
                            S.op('pe', lambda e: e.matmul(pu[:, :], wub[:, c, jj * 128:(jj + 1) * 128], hF[:, c, gs],
                                                          start=(c == 0), stop=(c == 7)),
                                 reads=[wur, "hF%d" % g], writes=[pur])
                        sgi = SG[0] % 2
                        SG[0] += 1
                        S.op('act', lambda e: e.activation(out=sg[sgi], in_=pg[:, :], func=AF.Silu), reads=[pgr], writes=["sg%d" % sgi])
                        S.op('dve', lambda e: e.tensor_tensor(out=aF[:, j + jj, gs], in0=pu[:, :], in1=sg[sgi], op=ALU.mult),
                             reads=[pur, "sg%d" % sgi], writes=["aF%d" % g])
                if pi + 2 < len(pairs):
                    load_pair(pi + 2)
                last_in_chunk = (pi + 1 == len(pairs)) or (pairs[pi + 1][0] != ch)
                if last_in_chunk:
                    nt = FF_CHUNKS[ch]
                    for g in range(NGRP):
                        gs = slice(g * 512, (g + 1) * 512)
                        for dt_ in range(8):
                            pyo, pyor = ps_next()
                            for jd in range(nt):
                                S.op('pe', lambda e: e.matmul(pyo[:, :], wd[:, jd, dt_ * 128:(dt_ + 1) * 128], aF[:, jd, gs],
                                                              start=(jd == 0), stop=(jd == nt - 1)),
                                     reads=["wd", "aF%d" % g], writes=[pyor])
                            S.op('dve', lambda e: e.tensor_tensor(out=xT[:, dt_, gs], in0=pyo[:, :], in1=xT[:, dt_, gs], op=ALU.add),
                                 reads=[pyor, "x%d_%d" % (dt_, g)], writes=["x%d_%d" % (dt_, g)])
                            if stream_out and ch + 1 == len(FF_CHUNKS):
                                S.dma(lambda e: e.dma_start(out=xout[dt_ * 128:(dt_ + 1) * 128, gs], in_=xT[:, dt_, gs]),
                                      reads=["x%d_%d" % (dt_, g)], writes=["xout_%d_%d" % (dt_, g)])
                    if ch + 1 < len(FF_CHUNKS):
                        load_wd(ch + 1)
            S.barrier()

        first_A = True
        cur_l = None
        streamed = False
        for st in stages:
            if st == 'nopay':
                continue
            kind, l = st[0], int(st[1])
            if cur_l != l:
                load_vecs(l)
                cur_l = l
            if kind == 'A':
                wp = (l in pay) and not (first_A and 'nopay' in stages)
                stage_A(l, wp)
                first_A = False
            elif kind == 'X':
                stage_X(l)
            elif kind == 'B':
                stage_B(l)
            elif kind == 'C':
                is_last = (st == [s for s in stages if s != 'nopay'][-1])
                stage_C(l, stream_out=is_last)
                streamed = is_last
        if xout is not None and not streamed:
            S.dma(lambda e: e.dma_start(out=xout.rearrange("(c p) t -> p c t", p=128), in_=xT),
                  reads=["x%d_%d" % (c, g) for c in range(8) for g in range(NGRP)], writes=["xout"])
        if dbg and 'x' in dbgt and xout is None:
            S.dma(lambda e: e.dma_start(out=dbgt['x'].rearrange("(c p) t -> p c t", p=128), in_=xT),
                  reads=["x%d_%d" % (c, g) for c in range(8) for g in range(NGRP)], writes=["dbgx"])
        S.finish()

        with nc.Block() as block:
            @block.sync
            def _(eng):
                S.replay('sp', eng, sems)

            @block.scalar
            def _(eng):
                S.replay('act', eng, sems)

            @block.vector
            def _(eng):
                S.replay('dve', eng, sems)

            @block.gpsimd
            def _(eng):
                S.replay('pool', eng, sems)

            @block.tensor
            def _(eng):
                S.replay('pe', eng, sems)
    return nc


def _consts(j):
    cst = np.zeros((128, 2048), np.float32)
    cst[:, 0:128] = np.eye(128, dtype=np.float32)
    k = np.arange(128)[:, None]
    q = np.arange(128)[None, :]
    triu = (k <= q).astype(np.float32)
    cst[:, 128:256] = triu
    mask = np.zeros((128, 4, 128), np.float32)
    for r in range(4):
        if r < j:
            mask[:, r, :] = 1.0
        elif r == j:
            mask[:, r, :] = triu
    cst[:, 256:768] = mask.reshape(128, 512)
    apool = np.zeros((128, 2, 4, 128), np.float32)
    ahalo = np.zeros((64, 4, 128), np.float32)
    own = (j - 1) % 4
    for gi, w in enumerate((2, 4, 8, 16)):
        for t in range(128):
            for var in range(2):
                first = (var == 0 and j == 0)
                cntv = float(min(t + 1, w)) if first else float(w)
                for s_ in range(max(0, t - w + 1), t + 1):
                    apool[s_, var, gi, t] += 1.0 / cntv
                apool[t, var, gi, t] -= 1.0
            for sp in range(16):
                tt = sp - 16
                if tt > t - w:
                    ahalo[own * 16 + sp, gi, t] = 1.0 / float(w)
    cst[:, 768:1792] = apool.reshape(128, 1024)
    sel = np.zeros((64, 2, 96), np.float32)
    for i in range(32):
        sel[i, 0, 64 + i] = 1.0
        sel[32 + i, 1, 64 + i] = 1.0
    cst[0:64, 1792:1984] = sel.reshape(64, 192)
    half = 16
    inv_freq = (1.0 / (np.float32(10000.0) ** (np.arange(half, dtype=np.float32) / np.float32(half)))).astype(np.float32)
    invf = np.zeros(128, np.float32)
    invf[64:80] = inv_freq
    invf[80:96] = inv_freq
    cst[:, 1984] = invf
    return cst, ahalo.reshape(64, 512)


def _vecs(inp):
    v = np.zeros((2, 128, NV), np.float32)
    for l in range(2):
        v[l, :, 0:8] = inp["g_mix_norm"][l].reshape(8, 128).T
        v[l, :, 8:16] = inp["g_ffn_norm"][l].reshape(8, 128).T
        v[l, :, 16:18] = inp["g_q_lat"][l].reshape(2, 128).T
        v[l, :, 18] = inp["g_kv_lat"][l]
        for (name, c0) in (("g_q_head", 19), ("g_k_head", 21)):
            gq = inp[name][l]
            v[l, 0:96, c0] = gq
            v[l, 64:80, c0 + 1] = gq[80:96]
            v[l, 80:96, c0 + 1] = gq[64:80]
        v[l, :, 23:27] = inp["g_out_mla"][l].reshape(4, 128).T
        v[l, 0:64, 27:31] = inp["g_out_sgu"][l].reshape(4, 64).T
        v[l, 0:64, 31:35] = inp["g_out_pool"][l].reshape(4, 64).T
        v[l, 0:64, 35:39] = inp["pool_scale"][l].reshape(4, 64).T
    return v


def _tok_index(j):
    return (np.arange(NBLK)[:, None] * 4 + j) * 128 + np.arange(128)[None, :]


def _common_maps(inp):
    f = lambda a: np.ascontiguousarray(np.asarray(a, dtype=np.float32))
    com = {
        "vecs": _vecs(inp),
        "w_in": f(inp["w_in"]), "w_q_up": f(inp["w_q_up"]), "w_kv_up": f(inp["w_kv_up"]),
        "g_sgu_v": f(inp["g_sgu_v"]), "w_spatial": f(inp["w_spatial"]),
        "b_spatial": f(inp["b_spatial"]).reshape(2, 512), "w_pool": f(inp["w_pool"]),
        "w_out": f(inp["w_out"]), "w_gate": f(inp["w_gate"]), "w_up": f(inp["w_up"]), "w_down": f(inp["w_down"]),
    }
    return com


_NC_CACHE = {}


def _get_nc(key, stages, fused=False, dbg=False):
    if key not in _NC_CACHE:
        _NC_CACHE[key] = build(stages, fused=fused, dbg=dbg)
    return _NC_CACHE[key]


def _run(nc, maps):
    res = run_bass_kernel_spmd(nc, maps, core_ids=list(range(8)))
    return res.results


FUSED = True


def kernel(**inp):
    inp = {k: np.asarray(v) for k, v in inp.items()}
    x = inp["x"].astype(np.float32, copy=False)
    positions = inp["positions"].astype(np.int32, copy=False)
    com = _common_maps(inp)
    per = []
    for c in range(8):
        b, j = divmod(c, 4)
        idx = _tok_index(j).reshape(-1)
        cst, c16 = _consts(j)
        per.append({
            "xin": np.ascontiguousarray(x[b, idx, :].T),
            "pos": np.ascontiguousarray(positions[b, idx].reshape(1, NTOK)),
            "cst": cst, "cst16": c16,
        })

    def maps_for(extra):
        out = []
        for c in range(8):
            m = dict(com)
            m.update(per[c])
            m.update(extra[c])
            out.append(m)
        return out

    if FUSED:
        nc = _get_nc("fused", ["A0", "X0", "B0", "C0", "A1", "X1", "B1", "C1"], fused=True)
        res = _run(nc, maps_for([{} for _ in range(8)]))
        outs = [r["xout"] for r in res]
    else:
        def gather(res, l):
            out = []
            for c in range(8):
                b = c // 4
                out.append({"gath%d_%d" % (l, i): np.concatenate([np.asarray(res[4 * b + j]["pay%d_%d" % (l, i)]) for j in range(4)], axis=0)
                            for i in range(5)})
            return out
        nc1 = _get_nc("L1", ["A0"])
        r1 = _run(nc1, maps_for([{} for _ in range(8)]))
        nc2 = _get_nc("L2", ["nopay", "A0", "B0", "C0", "A1"])
        r2 = _run(nc2, maps_for(gather(r1, 0)))
        g1 = gather(r2, 1)
        for c in range(8):
            per[c]["xin"] = np.ascontiguousarray(np.asarray(r2[c]["xout"], dtype=np.float32))
        nc3 = _get_nc("L3", ["nopay", "A1", "B1", "C1"])
        r3 = _run(nc3, maps_for(g1))
        outs = [r["xout"] for r in r3]
    y = np.empty((2, 8192, D), np.float32)
    for c in range(8):
        b, j = divmod(c, 4)
        idx = _tok_index(j).reshape(-1)
        y[b, idx, :] = np.asarray(outs[c], dtype=np.float32).T
    return y
```

```python
import math
import numpy as np
import ml_dtypes
import concourse.bass as bass
import concourse.mybir as mybir
from concourse.bass_utils import run_bass_kernel_spmd

F32 = mybir.dt.float32
BF16 = mybir.dt.bfloat16
I32 = mybir.dt.int32
AF = mybir.ActivationFunctionType
ALU = mybir.AluOpType

D = 1024
NTOK = 2048
NBLK = 16
NGRP = 4
INW = 1184
FFH = 2816
NHT = 22
EPS = 1e-6
PAY_KT = 0
PAY_V = 384
PAY_H = 896
PAY_ROWS = 928
PIECE_ROWS = (224, 224, 224, 224, 32)
NV = 40
FF_CHUNKS = (6, 6, 5, 5)
import os as _os
_SKIP = _os.environ.get('KSKIP', '').split(',')

TWO_PI = 2.0 * math.pi
C1 = 6.28125
C2 = TWO_PI - C1
MAGIC = 12582912.0
PI_SAFE = 3.1415925


class _Rec:
    def __init__(self):
        self.call = None

    def __getattr__(self, name):
        def f(*a, **k):
            self.call = (name, a, k)
            return None
        return f


def _record(fn):
    r = _Rec()
    fn(r)
    assert r.call is not None
    return r.call


class Sched:
    ENG = ('pe', 'act', 'dve', 'pool', 'sp')

    def __init__(self, ndma=16):
        self.q = {e: [] for e in self.ENG}
        self.cnt = {e: 0 for e in self.ENG}
        self.known = {e: {} for e in self.ENG}
        self.w = {}
        self.r = {}
        self.ndma = ndma
        self.dma_val = [0] * ndma
        self.dma_next = 0
        self.extra_val = {}

    def _need(self, e, toks):
        for (s, v) in toks:
            if s == e and e == 'pe':
                continue
            if self.known[e].get(s, 0) < v:
                self.known[e][s] = v
                self.q[e].append(('wait', s, v))

    def _deps(self, reads, writes):
        toks = []
        for r_ in reads:
            if r_ in self.w:
                toks.append(self.w[r_])
        for w_ in writes:
            if w_ in self.w:
                toks.append(self.w[w_])
            toks.extend(self.r.get(w_, {}).items())
        return toks

    def _commit(self, tok, reads, writes):
        for w_ in writes:
            self.w[w_] = tok
            self.r[w_] = {}
        for r_ in reads:
            d = self.r.setdefault(r_, {})
            if d.get(tok[0], 0) < tok[1]:
                d[tok[0]] = tok[1]

    def op(self, e, fn, reads=(), writes=()):
        fn = _record(fn)
        self._need(e, self._deps(reads, writes))
        self.cnt[e] += 1
        tok = (e, self.cnt[e])
        self.q[e].append(('op', fn, e))
        self._commit(tok, reads, writes)

    def dma(self, fn, reads=(), writes=(), q='sp'):
        fn = _record(fn)
        k = self.dma_next
        self.dma_next = (k + 1) % self.ndma
        s = 'dma%d' % k
        toks = self._deps(reads, writes)
        if self.dma_val[k] > 0:
            toks.append((s, self.dma_val[k]))
        self._need(q, toks)
        self.dma_val[k] += 16
        tok = (s, self.dma_val[k])
        self.q[q].append(('dma', fn, s))
        self._commit(tok, reads, writes)

    def special(self, q, semkey, fn, reads=(), writes=()):
        fn = _record(fn)
        toks = self._deps(reads, writes)
        v = self.extra_val.get(semkey, 0)
        if v > 0:
            toks.append((semkey, v))
        self._need(q, toks)
        v += 1
        self.extra_val[semkey] = v
        self.q[q].append(('op', fn, semkey))
        self._commit((semkey, v), reads, writes)

    def barrier(self):
        allt = [(f, self.cnt[f]) for f in self.ENG if self.cnt[f] > 0]
        allt += [('dma%d' % k, v) for k, v in enumerate(self.dma_val) if v > 0]
        for e in self.ENG:
            self._need(e, [t for t in allt if t[0] != e])

    def finish(self):
        allt = [('dma%d' % k, v) for k, v in enumerate(self.dma_val) if v > 0]
        allt += [(f, self.cnt[f]) for f in self.ENG if self.cnt[f] > 0 and f != 'sp']
        allt += list(self.extra_val.items())
        self._need('sp', allt)

    def replay(self, e, eng, sems):
        for item in self.q[e]:
            if item[0] == 'wait':
                eng.wait_ge(sems[item[1]], item[2])
            elif item[0] == 'op':
                getattr(eng, item[1][0])(*item[1][1], **item[1][2]).then_inc(sems[item[2]], 1)
            else:
                getattr(eng, item[1][0])(*item[1][1], **item[1][2]).then_inc(sems[item[2]], 16)


def build(stages, fused=False, dbg=False):
    nc = bass.Bass("TRN2", target_bir_lowering=False)
    S = Sched()

    def din(name, shape, dt=F32):
        return nc.dram_tensor(name, list(shape), dt, kind="ExternalInput").ap()

    def dout(name, shape, dt=F32):
        return nc.dram_tensor(name, list(shape), dt, kind="ExternalOutput").ap()

    xin = din("xin", [D, NTOK])
    pos = din("pos", [1, NTOK], I32)
    cst = din("cst", [128, 2048])
    cst16 = din("cst16", [64, 512])
    vecs = din("vecs", [2, 128, NV])
    w_in = din("w_in", [2, D, INW])
    w_q_up = din("w_q_up", [2, 256, 384])
    w_kv_up = din("w_kv_up", [2, 128, 768])
    g_sgu_v = din("g_sgu_v", [2, 256])
    w_spatial = din("w_spatial", [2, 4, 128, 128])
    b_spatial = din("b_spatial", [2, 512])
    w_pool = din("w_pool", [2, 4, 64, 64])
    w_out = din("w_out", [2, D, D])
    w_gate = din("w_gate", [2, D, FFH])
    w_up = din("w_up", [2, D, FFH])
    w_down = din("w_down", [2, FFH, D])

    layers = sorted({int(s[1]) for s in stages if s[0] in 'ABCX'})
    need_gath = {}
    pay = {}
    for l in layers:
        if fused:
            pay[l] = [nc.dram_tensor("pay%d_%d" % (l, i), [PIECE_ROWS[i], 2048], BF16, kind="Internal").ap() for i in range(5)]
            need_gath[l] = [nc.dram_tensor("gath%d_%d" % (l, i), [4 * PIECE_ROWS[i], 2048], BF16, kind="Internal").ap() for i in range(5)]
        else:
            if ("A%d" % l) in stages and not (("B%d" % l) in stages):
                pay[l] = [dout("pay%d_%d" % (l, i), [PIECE_ROWS[i], 2048], BF16) for i in range(5)]
            if ("B%d" % l) in stages:
                need_gath[l] = [din("gath%d_%d" % (l, i), [4 * PIECE_ROWS[i], 2048], BF16) for i in range(5)]
    xout = dout("xout", [D, NTOK]) if any(s[0] == 'C' for s in stages) else None
    dbgt = {}
    if dbg:
        dbgt['qT'] = dout("dbg_qT", [96, 4 * NTOK], BF16)
        dbgt['gn'] = dout("dbg_gn", [64, 4 * NTOK], BF16)
        dbgt['aT'] = dout("dbg_aT", [128, 4 * NTOK], BF16)
        dbgt['x'] = dout("dbg_x", [D, NTOK])

    from contextlib import ExitStack
    with ExitStack() as es:
        arena_cols = 212000 // 4
        arena = es.enter_context(nc.sbuf_tensor("arena", [128, arena_cols], F32))
        psb = [es.enter_context(nc.psum_tensor("ps%d" % i, [128, 512], F32)) for i in range(8)]
        sems = {}
        for e in Sched.ENG:
            sems[e] = es.enter_context(nc.semaphore("sem_" + e))
        for k in range(S.ndma):
            sems['dma%d' % k] = es.enter_context(nc.semaphore("sem_dma%d" % k))
        for i in range(5):
            sems['cc%d' % i] = es.enter_context(nc.semaphore("sem_cc%d" % i))

        class Bump:
            def __init__(self, lo, hi):
                self.lo, self.hi, self.p = lo, hi, lo

            def take(self, shape, dt, parts=128):
                esz = 4 if dt in (F32, I32) else 2
                n = 1
                for s_ in shape:
                    n *= s_
                nbytes = (n * esz + 63) // 64 * 64
                off = self.p
                self.p += nbytes
                assert self.p <= self.hi, ("SBUF overflow", self.p, self.hi)
                assert (n * esz) % 4 == 0
                ap = arena[0:parts, off // 4:(off + n * esz) // 4]
                if dt != F32:
                    ap = ap.bitcast(dt)
                if len(shape) == 2:
                    ap = ap.rearrange("p (a b) -> p a b", a=shape[0])
                elif len(shape) == 3:
                    ap = ap.rearrange("p (a b c) -> p a b c", a=shape[0], b=shape[1])
                return ap

        TOT = arena_cols * 4
        pers = Bump(0, TOT)
        xT = pers.take([8, NTOK], F32)
        ones_bf = pers.take([128], BF16)
        ident32 = pers.take([128], F32)
        triu32 = pers.take([128], F32)
        maskT = pers.take([4, 128], BF16)
        apool = pers.take([2, 4, 128], BF16)
        ahalo = pers.take([4, 128], BF16, parts=64)
        selpe = pers.take([2, 96], BF16, parts=64)
        ones32 = pers.take([64], F32, parts=1)
        invf = pers.take([1], F32, parts=96)
        vec_sb = pers.take([NV], F32)
        gv_bc = pers.take([256], F32)
        brow32 = pers.take([512], F32, parts=1)
        epsb = pers.take([1], F32)
        negI = pers.take([128], BF16)
        notmask = pers.take([4, 128], BF16)
        Ctab = pers.take([NTOK], F32, parts=96)
        Stab = pers.take([NTOK], F32, parts=96)
        W1_LO = pers.p
        W1_SZ = 26624
        MIX_LO = W1_LO + W1_SZ
        MIX_SZ = 16384 + 4096 + 16384 + 8192
        T_LO = MIX_LO + MIX_SZ
        assert T_LO < TOT
        wA = Bump(W1_LO, W1_LO + W1_SZ)
        w_in_sb = wA.take([8, 1216], BF16)
        wq_sb = wA.take([2, 384], BF16)
        wqR_sb = wA.take([2, 384], BF16)
        wkv_sb = wA.take([768], BF16)
        wk1_sb = wA.take([4, 96], BF16)
        WsT_sb = wA.take([4, 128], BF16)
        wB = Bump(W1_LO, W1_LO + W1_SZ)
        wpool_sb = wB.take([4, 64], BF16, parts=64)
        wo_a = wB.take([4, D], BF16)
        wo_g = wB.take([4, D], BF16, parts=64)
        wo_p = wB.take([4, D], BF16, parts=64)
        mx = Bump(MIX_LO, MIX_LO + MIX_SZ)
        qT = mx.take([4, NTOK], BF16, parts=96)
        aT0 = mx.take([NTOK], BF16)
        gn = mx.take([4, NTOK], BF16, parts=64)
        pin = mx.take([NBLK, 256], BF16)
        qT_off = MIX_LO

        def aT_ap(h):
            if h == 0:
                return aT0
            off = qT_off + (h - 1) * NTOK * 2
            return arena[0:128, off // 4:(off + NTOK * 2) // 4].bitcast(BF16)

        def aT_res(h, g):
            return "aT%d_%d" % (h, g) if h == 0 else "qT%d_%d" % (h - 1, g)

        PSN = [0]

        def ps_next():
            i = PSN[0]
            PSN[0] = (i + 1) % 8
            return psb[i], "ps%d" % i

        inv_sqrt = {}

        def rstd_from(ps_ap, pres, n, P, sd_ap, sdres, out_ap, ores, cols):
            S.op('act', lambda e: e.activation(out=out_ap, in_=ps_ap, func=AF.Ln, bias=epsb[0:P, 0:1], scale=1.0 / n),
                 reads=[pres, "epsb"], writes=[ores])
            S.op('act', lambda e: e.activation(out=out_ap, in_=out_ap, func=AF.Exp, scale=-0.5), reads=[ores], writes=[ores])

        tb = Bump(MIX_LO, MIX_LO + MIX_SZ)
        st_c = tb.take([2048], F32)
        st_c16 = tb.take([512], F32, parts=64)
        S.dma(lambda e: e.dma_start(out=st_c, in_=cst), writes=["st_c"])
        S.dma(lambda e: e.dma_start(out=st_c16, in_=cst16), writes=["st_c16"])
        S.dma(lambda e: e.dma_start(out=xT, in_=xin.rearrange("(c p) t -> p c t", p=128)),
              writes=["x%d_%d" % (c, g) for c in range(8) for g in range(NGRP)])
        S.op('pool', lambda e: e.memset(ones_bf, 1.0), writes=["ones_bf"])
        S.op('pool', lambda e: e.memset(ones32, 1.0), writes=["ones32"])
        S.op('pool', lambda e: e.memset(epsb, float(EPS)), writes=["epsb"])
        S.op('dve', lambda e: e.tensor_scalar(out=negI, in0=st_c[:, 0:128], scalar1=-30000.0, scalar2=None, op0=ALU.mult),
             reads=["st_c"], writes=["negI"])
        S.op('dve', lambda e: e.tensor_scalar(out=notmask, in0=st_c[:, 256:768].rearrange("p (a b) -> p a b", a=4),
                                              scalar1=-1.0, scalar2=1.0, op0=ALU.mult, op1=ALU.add),
             reads=["st_c"], writes=["notmask"])
        S.op('dve', lambda e: e.tensor_copy(out=ident32, in_=st_c[:, 0:128]), reads=["st_c"], writes=["ident32"])
        S.op('dve', lambda e: e.tensor_copy(out=triu32, in_=st_c[:, 128:256]), reads=["st_c"], writes=["triu32"])
        S.op('dve', lambda e: e.tensor_copy(out=maskT, in_=st_c[:, 256:768].rearrange("p (a b) -> p a b", a=4)),
             reads=["st_c"], writes=["maskT"])
        S.op('dve', lambda e: e.tensor_copy(out=apool, in_=st_c[:, 768:1792].rearrange("p (a b c) -> p a b c", a=2, b=4)),
             reads=["st_c"], writes=["apool"])
        S.op('dve', lambda e: e.tensor_copy(out=selpe, in_=st_c[0:64, 1792:1984].rearrange("p (a b) -> p a b", a=2)),
             reads=["st_c"], writes=["selpe"])
        S.op('dve', lambda e: e.tensor_copy(out=invf, in_=st_c[0:96, 1984:1985]), reads=["st_c"], writes=["invf"])
        S.op('dve', lambda e: e.tensor_copy(out=ahalo, in_=st_c16.rearrange("p (a b) -> p a b", a=4)),
             reads=["st_c16"], writes=["ahalo"])
        posi = tb.take([NTOK], I32, parts=96)
        ang = tb.take([NTOK], F32, parts=96)
        tq = tb.take([NTOK], F32, parts=96)
        kq = tb.take([NTOK], F32, parts=96)
        S.dma(lambda e: e.dma_start(out=posi, in_=pos[0].partition_broadcast(96)), writes=["posi"])
        S.op('dve', lambda e: e.tensor_copy(out=ang, in_=posi), reads=["posi"], writes=["ang"])
        S.op('dve', lambda e: e.tensor_scalar(out=ang, in0=ang, scalar1=invf[:, 0:1], scalar2=None, op0=ALU.mult),
             reads=["ang", "invf"], writes=["ang"])
        for (tab, tres, shift, post) in ((Stab, "Stab", 0.0, 0.0), (Ctab, "Ctab", 0.25, math.pi / 2)):
            S.op('dve', lambda e, shift=shift: e.tensor_scalar(out=tq, in0=ang, scalar1=1.0 / TWO_PI, scalar2=shift,
                                                               op0=ALU.mult, op1=ALU.add),
                 reads=["ang"], writes=["tq"])
            S.op('dve', lambda e: e.tensor_scalar(out=kq, in0=tq, scalar1=MAGIC, scalar2=None, op0=ALU.add),
                 reads=["tq"], writes=["kq"])
            S.op('dve', lambda e: e.tensor_scalar(out=kq, in0=kq, scalar1=MAGIC, scalar2=None, op0=ALU.subtract),
                 reads=["kq"], writes=["kq"])
            S.op('dve', lambda e: e.scalar_tensor_tensor(out=tq, in0=kq, scalar=-C1, in1=ang, op0=ALU.mult, op1=ALU.add),
                 reads=["kq", "ang"], writes=["tq"])
            S.op('dve', lambda e, post=post: e.scalar_tensor_tensor(out=tq, in0=kq, scalar=-C2, in1=tq, op0=ALU.mult, op1=ALU.add),
                 reads=["kq", "tq"], writes=["tq"])
            if post != 0.0:
                S.op('dve', lambda e, post=post: e.tensor_scalar(out=tq, in0=tq, scalar1=post, scalar2=None, op0=ALU.add),
                     reads=["tq"], writes=["tq"])
            S.op('dve', lambda e: e.tensor_scalar(out=tq, in0=tq, scalar1=-PI_SAFE, scalar2=PI_SAFE, op0=ALU.max, op1=ALU.min),
                 reads=["tq"], writes=["tq"])
            S.op('act', lambda e, tab=tab: e.activation(out=tab, in_=tq, func=AF.Sin), reads=["tq"], writes=[tres])

        def load_vecs(l):
            S.dma(lambda e: e.dma_start(out=vec_sb, in_=vecs[l]), writes=["vec"])
            S.dma(lambda e: e.dma_start(out=gv_bc, in_=g_sgu_v[l].partition_broadcast(128)), writes=["gv_bc"])
            S.dma(lambda e: e.dma_start(out=brow32, in_=b_spatial[l:l + 1, :]), writes=["brow32"])

        VC = dict(gmix=0, gffn=8, gql=16, gkv=18, gq=19, gqp=20, gk=21, gkp=22, goa=23, gog=27, gop=31, psc=35)

        def vcol(name, i=0, P=128):
            c = VC[name] + i
            return vec_sb[0:P, c:c + 1]

        def issue_cc(l, pieces):
            for i in pieces:
                if i < 4:
                    rd = ["pay_kt%d_%d" % (h, i) for h in range(4)] + ["pay_v%d_%d" % (h, m) for h in range(4) for m in range(4 * i, 4 * i + 4)]
                else:
                    rd = ["pay_h%d" % m for m in range(NBLK)]
                S.special('pool', 'cc%d' % i, lambda e: e.collective_compute(
                    "AllGather", ALU.bypass, replica_groups=[[0, 1, 2, 3], [4, 5, 6, 7]],
                    ins=[pay[l][i].opt()], outs=[need_gath[l][i].opt()]), reads=rd, writes=["gath%d" % i])

        def stage_A(l, write_pay):
            tb = Bump(T_LO, TOT)
            stgs = [tb.take([INW], F32) for _ in range(4)]
            stg = stgs[0]
            for c in range(8):
                sg_, sgr = stgs[c % 4], ("stg" if c % 4 == 0 else "stg_%d" % (c % 4))
                S.dma(lambda e: e.dma_start(out=sg_, in_=w_in[l, c * 128:(c + 1) * 128, :]), writes=[sgr])
                S.op('dve', lambda e: e.tensor_copy(out=w_in_sb[:, c, 0:416], in_=sg_[:, 0:416]),
                     reads=[sgr], writes=["w_in%d" % c])
                S.op('pool', lambda e: e.tensor_scalar(out=w_in_sb[:, c, 416:432], in0=sg_[:, 400:416], scalar1=-1.0,
                                                       scalar2=None, op0=ALU.mult),
                     reads=[sgr], writes=["w_in%d" % c])
                S.op('pool', lambda e: e.tensor_copy(out=w_in_sb[:, c, 432:448], in_=sg_[:, 384:400]),
                     reads=[sgr], writes=["w_in%d" % c])
                S.op('act', lambda e: e.copy(out=w_in_sb[:, c, 448:1216], in_=sg_[:, 416:1184]),
                     reads=[sgr], writes=["w_in%d" % c])
            stq = stg[:, 0:768].rearrange("p (a b) -> p a b", a=2)
            S.dma(lambda e: e.dma_start(out=stq, in_=w_q_up[l].rearrange("(c p) n -> p c n", p=128)), writes=["stg"])
            S.op('pool', lambda e: e.tensor_copy(out=wq_sb, in_=stq), reads=["stg"], writes=["wq"])
            S.op('pool', lambda e: e.memset(wqR_sb, 0.0), writes=["wqR"])
            stq4 = stg[:, 0:768].rearrange("p (a b) -> p a b", a=8)
            wqR4 = wqR_sb.rearrange("p c (h d) -> p (c h) d", h=4)
            S.op('pool', lambda e: e.tensor_scalar(out=wqR4[:, :, 64:80], in0=stq4[:, :, 80:96], scalar1=-1.0, scalar2=None,
                                                   op0=ALU.mult), reads=["stg"], writes=["wqR"])
            S.op('pool', lambda e: e.tensor_copy(out=wqR4[:, :, 80:96], in_=stq4[:, :, 64:80]), reads=["stg"], writes=["wqR"])
            S.dma(lambda e: e.dma_start(out=stg[:, 0:768], in_=w_kv_up[l]), writes=["stg"])
            S.op('pool', lambda e: e.tensor_copy(out=wkv_sb, in_=stg[:, 0:768]), reads=["stg"], writes=["wkv"])
            S.op('pool', lambda e: e.memset(wk1_sb, 0.0), writes=["wk1"])
            S.op('pool', lambda e: e.tensor_copy(out=wk1_sb[:, :, 0:64],
                                                 in_=stg[:, 0:768].rearrange("p (h d) -> p h d", h=4)[:, :, 0:64]),
                 reads=["stg"], writes=["wk1"])
            for h in range(4):
                sgh, sghr = stgs[h], ("stg" if h == 0 else "stg_%d" % h)
                S.dma(lambda e: e.dma_start(out=sgh[:, 0:128], in_=w_spatial[l, h]), writes=[sghr])
                pt, pr = ps_next()
                S.op('pe', lambda e: e.transpose(out=pt[:, 0:128], in_=sgh[:, 0:128], identity=ident32),
                     reads=[sghr, "ident32"], writes=[pr])
                S.op('dve', lambda e, pt=pt, h=h: e.tensor_tensor(out=WsT_sb[:, h, :], in0=pt[:, 0:128], in1=triu32, op=ALU.mult),
                     reads=[pr, "triu32"], writes=["WsT"])
            S.barrier()
            tb = Bump(T_LO, TOT)
            sqb = [tb.take([512], BF16) for _ in range(2)]
            hT = tb.take([8, 512], BF16)
            rst = [tb.take([512], F32) for _ in range(2)]
            sd = tb.take([16], F32)
            qlat_n = tb.take([2, 512], BF16)
            kvlat_n = tb.take([512], BF16)
            kpe_sb = tb.take([512], BF16, parts=64)
            uT = tb.take([4, 512], BF16, parts=64)
            gm = tb.take([4, 512], F32, parts=64)
            v_n = tb.take([4, 256], BF16)
            t1 = tb.take([512], F32, parts=96)
            t2 = tb.take([512], F32, parts=96)
            rxb = [tb.take([512], F32) for _ in range(2)]
            t2k = t2
            KTg = tb.take([4, 512], BF16, parts=96)
            Vb = tb.take([512], BF16)
            ssv = tb.take([12], F32)
            SQI = [0]
            RSI = [0]

            def sq_next():
                i = SQI[0]
                SQI[0] = (i + 1) % 2
                return sqb[i], "sqb%d" % i

            def rst_next():
                i = RSI[0]
                RSI[0] = (i + 1) % 2
                return rst[i], "rst%d" % i

            for g in range(NGRP):
                gs = slice(g * 512, (g + 1) * 512)
                def x_stats(g2):
                    gs2 = slice(g2 * 512, (g2 + 1) * 512)
                    pss, pssr = ps_next()
                    for c in range(8):
                        sq, sqr = sq_next()
                        S.op('act', lambda e: e.activation(out=sq, in_=xT[:, c, gs2], func=AF.Square),
                             reads=["x%d_%d" % (c, g2)], writes=[sqr])
                        S.op('pe', lambda e: e.matmul(pss[:, :], ones_bf, sq, start=(c == 0), stop=(c == 7)),
                             reads=[sqr, "ones_bf"], writes=[pssr])
                    rstd_from(pss[:, :], pssr, 1024.0, 128, sd, "sd", rxb[g2 % 2], "rxb%d" % (g2 % 2), 512)

                if g == 0:
                    x_stats(0)
                rx, rxr = rxb[g % 2], "rxb%d" % (g % 2)
                for c in range(8):
                    S.op('dve', lambda e: e.scalar_tensor_tensor(out=hT[:, c, :], in0=xT[:, c, gs], scalar=vcol('gmix', c),
                                                                 in1=rx, op0=ALU.mult, op1=ALU.mult),
                         reads=["x%d_%d" % (c, g), rxr, "vec"], writes=["hT%d" % c])
                hres = ["hT%d" % c for c in range(8)]
                wres = ["w_in%d" % c for c in range(8)]

                def proj_ii(col0, M, pt, pr):
                    for c in range(8):
                        S.op('pe', lambda e, c=c: e.matmul(pt[0:M, :], w_in_sb[:, c, col0:col0 + M], hT[:, c, :],
                                                           start=(c == 0), stop=(c == 7)),
                             reads=hres + wres, writes=[pr])

                pq = [ps_next(), ps_next()]
                for mt in range(2):
                    proj_ii(mt * 128, 128, pq[mt][0], pq[mt][1])
                pss, pssr = ps_next()
                for mt in range(2):
                    sq, sqr = sq_next()
                    S.op('act', lambda e, sq=sq, mt=mt: e.activation(out=sq, in_=pq[mt][0][:, :], func=AF.Square),
                         reads=[pq[mt][1]], writes=[sqr])
                    S.op('pe', lambda e, sq=sq, mt=mt, pss=pss: e.matmul(pss[:, :], ones_bf, sq, start=(mt == 0), stop=(mt == 1)),
                         reads=[sqr, "ones_bf"], writes=[pssr])
                rq, rqr = rst_next()
                rstd_from(pss[:, :], pssr, 256.0, 128, sd, "sd", rq, rqr, 512)
                for mt in range(2):
                    S.op('dve', lambda e, mt=mt, rq=rq: e.scalar_tensor_tensor(out=qlat_n[:, mt, :], in0=pq[mt][0][:, :],
                                                                               scalar=vcol('gql', mt), in1=rq,
                                                                               op0=ALU.mult, op1=ALU.mult),
                         reads=[pq[mt][1], rqr, "vec"], writes=["qlat_n"])
                pk, pkr = ps_next()
                proj_ii(256, 128, pk, pkr)
                sq, sqr = sq_next()
                S.op('act', lambda e, sq=sq, pk=pk: e.activation(out=sq, in_=pk[:, :], func=AF.Square), reads=[pkr], writes=[sqr])
                pss, pssr = ps_next()
                S.op('pe', lambda e, sq=sq, pss=pss: e.matmul(pss[:, :], ones_bf, sq, start=True, stop=True),
                     reads=[sqr, "ones_bf"], writes=[pssr])
                rk, rkr = rst_next()
                rstd_from(pss[:, :], pssr, 128.0, 128, sd, "sd", rk, rkr, 512)
                S.op('dve', lambda e, pk=pk, rk=rk: e.scalar_tensor_tensor(out=kvlat_n, in0=pk[:, :], scalar=vcol('gkv'), in1=rk,
                                                                           op0=ALU.mult, op1=ALU.mult),
                     reads=[pkr, rkr, "vec"], writes=["kvlat_n"])
                pp, ppr = ps_next()
                proj_ii(384, 64, pp, ppr)
                S.op('act', lambda e, pp=pp: e.copy(out=kpe_sb, in_=pp[0:64, :]), reads=[ppr], writes=["kpe"])
                for h in range(4):
                    pu, pur = ps_next()
                    proj_ii(448 + h * 64, 64, pu, pur)
                    S.op('act', lambda e, pu=pu, h=h: e.copy(out=uT[:, h, :], in_=pu[0:64, :]), reads=[pur], writes=["uT"])
                pvs = []
                for b in range(4):
                    m = 4 * g + b
                    pv, pvr = ps_next()
                    pvs.append((pv, pvr))
                    for c in range(8):
                        S.op('pe', lambda e: e.matmul(pv[:, :], hT[:, c, b * 128:(b + 1) * 128], w_in_sb[:, c, 704:1216],
                                                      start=(c == 0), stop=(c == 7)),
                             reads=hres + wres, writes=[pvr])
                    sq, sqr = sq_next()
                    S.op('act', lambda e: e.activation(out=sq[:, 0:256], in_=pv[:, 0:256], func=AF.Square, accum_out=ssv[:, b:b + 1]),
                         reads=[pvr], writes=[sqr, "ssv"])
                    S.op('act', lambda e: e.copy(out=pin[:, m, :], in_=pv[:, 256:512]), reads=[pvr], writes=["pin%d" % m])
                if g + 1 < NGRP:
                    x_stats(g + 1)
                S.op('act', lambda e: e.activation(out=ssv[:, 4:8], in_=ssv[:, 0:4], func=AF.Ln, bias=epsb[:, 0:1], scale=1.0 / 256),
                     reads=["ssv", "epsb"], writes=["ssv1"])
                S.op('act', lambda e: e.activation(out=ssv[:, 8:12], in_=ssv[:, 4:8], func=AF.Exp, scale=-0.5), reads=["ssv1"], writes=["ssv2"])
                for b in range(4):
                    pv, pvr = pvs[b]
                    S.op('dve', lambda e: e.scalar_tensor_tensor(out=v_n[:, b, :], in0=pv[:, 0:256], scalar=ssv[:, 8 + b:9 + b], in1=gv_bc,
                                                                 op0=ALU.mult, op1=ALU.mult),
                         reads=[pvr, "ssv2", "gv_bc"], writes=["v_n%d" % b])
                for b in range(4):
                    pz, pzr = ps_next()
                    for h in range(4):
                        S.op('pe', lambda e: e.matmul(pz[0:64, h * 128:(h + 1) * 128], v_n[:, b, h * 64:(h + 1) * 64],
                                                      WsT_sb[:, h, :], start=True, stop=False),
                             reads=["v_n%d" % b, "WsT"], writes=[pzr])
                        S.op('pe', lambda e: e.matmul(pz[0:64, h * 128:(h + 1) * 128], ones32[0:1, 0:64],
                                                      brow32[0:1, h * 128:(h + 1) * 128], start=False, stop=True),
                             reads=["ones32", "brow32"], writes=[pzr])
                    S.op('dve', lambda e: e.tensor_tensor(out=gm[:, :, b * 128:(b + 1) * 128],
                                                          in0=pz[0:64, :].rearrange("p (h t) -> p h t", h=4),
                                                          in1=uT[:, :, b * 128:(b + 1) * 128], op=ALU.mult),
                         reads=[pzr, "uT"], writes=["gm"])
                pss, pssr = ps_next()
                for h in range(4):
                    sq, sqr = sq_next()
                    S.op('act', lambda e: e.activation(out=sq[0:64, :], in_=gm[:, h, :], func=AF.Square),
                         reads=["gm"], writes=[sqr])
                    S.op('pe', lambda e: e.matmul(pss[0:64, :], ones_bf[0:64, 0:64], sq[0:64, :], start=(h == 0), stop=(h == 3)),
                         reads=[sqr, "ones_bf"], writes=[pssr])
                rg, rgr = rst_next()
                rstd_from(pss[0:64, :], pssr, 256.0, 64, sd[0:64, :], "sd", rg[0:64, :], rgr, 512)
                for h in range(4):
                    S.op('dve', lambda e: e.scalar_tensor_tensor(out=gn[:, h, gs], in0=gm[:, h, :], scalar=vcol('gog', h, 64),
                                                                 in1=rg[0:64, :], op0=ALU.mult, op1=ALU.mult),
                         reads=["gm", rgr, "vec"], writes=["gn%d" % (4 * g + b) for b in range(4)])
                def head_norm_rope(praw, prawr, pR, pRr, gname, gpname, out_ap, ores, t2_pre=None):
                    sq, sqr = sq_next()
                    S.op('act', lambda e: e.activation(out=sq[0:96, :], in_=praw[0:96, :], func=AF.Square),
                         reads=[prawr], writes=[sqr])
                    pss, pssr = ps_next()
                    S.op('pe', lambda e: e.matmul(pss[0:96, :], ones_bf[0:96, 0:96], sq[0:96, :], start=True, stop=True),
                         reads=[sqr, "ones_bf"], writes=[pssr])
                    rr, rrr = rst_next()
                    rstd_from(pss[0:96, :], pssr, 96.0, 96, sd[0:96, :], "sd", rr[0:96, :], rrr, 512)
                    S.op('dve', lambda e: e.scalar_tensor_tensor(out=t1, in0=praw[0:96, :], scalar=vcol(gname, 0, 96),
                                                                 in1=Ctab[:, gs], op0=ALU.mult, op1=ALU.mult),
                         reads=[prawr, "Ctab", "vec"], writes=["t1"])
                    if t2_pre is None:
                        S.op('dve', lambda e: e.scalar_tensor_tensor(out=t2, in0=pR[0:96, :], scalar=vcol(gpname, 0, 96),
                                                                     in1=Stab[:, gs], op0=ALU.mult, op1=ALU.mult),
                             reads=[pRr, "Stab", "vec"], writes=["t2"])
                        t2u, t2r = t2, "t2"
                    else:
                        t2u, t2r = t2_pre
                    S.op('dve', lambda e: e.tensor_tensor(out=t1, in0=t1, in1=t2u, op=ALU.add), reads=["t1", t2r], writes=["t1"])
                    S.op('dve', lambda e: e.tensor_tensor(out=out_ap, in0=t1, in1=rr[0:96, :], op=ALU.mult),
                         reads=["t1", rrr], writes=[ores])

                for h in range(4):
                    pqa, pqar = ps_next()
                    pqb, pqbr = ps_next()
                    for c in range(2):
                        S.op('pe', lambda e, c=c, h=h, pqa=pqa: e.matmul(pqa[0:96, :], wq_sb[:, c, h * 96:(h + 1) * 96], qlat_n[:, c, :],
                                                                         start=(c == 0), stop=(c == 1)),
                             reads=["wq", "qlat_n"], writes=[pqar])
                    for c in range(2):
                        S.op('pe', lambda e, c=c, h=h, pqb=pqb: e.matmul(pqb[0:96, :], wqR_sb[:, c, h * 96:(h + 1) * 96], qlat_n[:, c, :],
                                                                         start=(c == 0), stop=(c == 1)),
                             reads=["wqR", "qlat_n"], writes=[pqbr])
                    head_norm_rope(pqa, pqar, pqb, pqbr, 'gq', 'gqp', qT[:, h, gs], "qT%d_%d" % (h, g))
                pkR, pkRr = ps_next()
                S.op('pe', lambda e: e.matmul(pkR[0:96, :], selpe[:, 1, :], kpe_sb, start=True, stop=True),
                     reads=["selpe", "kpe"], writes=[pkRr])
                S.op('dve', lambda e: e.scalar_tensor_tensor(out=t2k, in0=pkR[0:96, :], scalar=vcol('gkp', 0, 96),
                                                             in1=Stab[:, gs], op0=ALU.mult, op1=ALU.mult),
                     reads=[pkRr, "Stab", "vec"], writes=["t2"])
                for h in range(4):
                    pka, pkar = ps_next()
                    S.op('pe', lambda e, h=h, pka=pka: e.matmul(pka[0:96, :], wk1_sb[:, h, :], kvlat_n, start=True, stop=False),
                         reads=["wk1", "kvlat_n"], writes=[pkar])
                    S.op('pe', lambda e, h=h, pka=pka: e.matmul(pka[0:96, :], selpe[:, 0, :], kpe_sb, start=False, stop=True),
                         reads=["selpe", "kpe"], writes=[pkar])
                    head_norm_rope(pka, pkar, None, None, 'gk', 'gkp', KTg[:, h, :], "KTg", t2_pre=(t2k, "t2"))
                if write_pay:
                    for h in range(4):
                        kdst = pay[l][g][h * 56:h * 56 + 24, :].rearrange("r (c t) -> (r c) t", c=4)
                        S.dma(lambda e: e.dma_start(out=kdst, in_=KTg[:, h, :]),
                              reads=["KTg"], writes=["pay_kt%d_%d" % (h, g)])
                for b in range(4):
                    m = 4 * g + b
                    pv, pvr = ps_next()
                    S.op('pe', lambda e, pv=pv, b=b: e.matmul(
                        pv[:, :].rearrange("p (h d) -> p h d", h=4), kvlat_n[:, b * 128:(b + 1) * 128],
                        wkv_sb.rearrange("p (h d) -> p h d", h=4)[:, :, 64:192], start=True, stop=True),
                         reads=["kvlat_n", "wkv"], writes=[pvr])
                    S.op('act', lambda e, pv=pv: e.copy(out=Vb, in_=pv[:, :]), reads=[pvr], writes=["Vb"])
                    if write_pay:
                        for h in range(4):
                            dst = pay[l][g][h * 56 + 24 + b * 8:h * 56 + 32 + b * 8, :].rearrange("r (q d) -> (r q) d", q=16)
                            S.dma(lambda e: e.dma_start(out=dst, in_=Vb[:, h * 128:(h + 1) * 128]),
                                  reads=["Vb"], writes=["pay_v%d_%d" % (h, m)])
                        hd = pay[l][4].rearrange("(m r) (q d) -> m (r q) d", m=NBLK, q=8)[m]
                        S.dma(lambda e: e.dma_start(out=hd, in_=pin[112:128, m, :]),
                              reads=["pin%d" % m], writes=["pay_h%d" % m])
                if fused and write_pay:
                    issue_cc(l, [g])
            if fused and write_pay:
                issue_cc(l, [4])
            if dbg:
                S.dma(lambda e: e.dma_start(out=dbgt['qT'], in_=qT.rearrange("p h t -> p (h t)")),
                      reads=["qT%d_%d" % (h, g) for h in range(4) for g in range(4)], writes=["dbg_qT"])
                S.dma(lambda e: e.dma_start(out=dbgt['gn'], in_=gn.rearrange("p h t -> p (h t)")),
                      reads=["gn%d" % m for m in range(NBLK)], writes=["dbg_gn"])
            S.barrier()

        def stage_X(l):
            return

        def stage_B(l):
            gath = need_gath[l]
            tb = Bump(T_LO, TOT)
            KTh = tb.take([4 * NTOK], BF16, parts=96)
            Vh = tb.take([64, 128], BF16)
            pT = [tb.take([512], BF16) for _ in range(4)]
            rden = tb.take([512], F32)
            stgs_ = [tb.take([D], F32) for _ in range(2)]
            scale = 1.0 / math.sqrt(96.0)
            LA = 3
            SBK = (0, 1, 2, 7)
            phases = [(h, (0, 1, 2)) for h in range(4)] + [(h, (3,)) for h in range(4)]
            items = []
            unit = 0
            for pi, (h, qgs) in enumerate(phases):
                for g in qgs:
                    sl = [(r, mk) for r in range(4) for mk in range(4 * g + 4)]
                    for si, (r, mk) in enumerate(sl):
                        items.append(dict(h=h, g=g, r=r, mk=mk, first=(si == 0), last=(si == len(sl) - 1), hg=unit, ph=pi,
                                          lastg=(g == qgs[-1])))
                    unit += 1

            def load_kv(pi, r):
                h_, qgs_ = phases[pi]
                for g_ in range(qgs_[-1] + 1):
                    ksrc = gath[g_][r * 224 + h_ * 56:r * 224 + h_ * 56 + 24, :].rearrange("r (c t) -> (r c) t", c=4)
                    S.dma(lambda e: e.dma_start(out=KTh[:, r * NTOK + g_ * 512:r * NTOK + (g_ + 1) * 512], in_=ksrc),
                          reads=["gath%d" % g_], writes=["KTh%d_%d" % (r, g_)])
                    vsrc = gath[g_][r * 224 + h_ * 56 + 24:r * 224 + h_ * 56 + 56, :].rearrange("(m r2) (q d) -> (r2 q) m d", m=4, q=16)
                    S.dma(lambda e: e.dma_start(out=Vh[:, r * 16 + 4 * g_:r * 16 + 4 * g_ + 4, :], in_=vsrc),
                          reads=["gath%d" % g_], writes=["Vh%d_%d" % (r, g_)])

            for r in range(4):
                load_kv(0, r)
            for c in range(4):
                S.dma(lambda e, c=c: e.dma_start(out=stgs_[c % 2], in_=w_out[l, c * 128:(c + 1) * 128, :]), writes=["stgB%d" % (c % 2)])
                S.op('pool', lambda e, c=c: e.tensor_copy(out=wo_a[:, c, :], in_=stgs_[c % 2]), reads=["stgB%d" % (c % 2)], writes=["wo_a"])
            for (wsb, base, nm) in ((wo_g, 512, "wo_g"), (wo_p, 768, "wo_p")):
                for c in range(4):
                    S.dma(lambda e, c=c, base=base: e.dma_start(out=stgs_[c % 2][0:64, :], in_=w_out[l, base + c * 64:base + (c + 1) * 64, :]),
                          writes=["stgB%d" % (c % 2)])
                    S.op('pool', lambda e, c=c, wsb=wsb: e.tensor_copy(out=wsb[:, c, :], in_=stgs_[c % 2][0:64, :]), reads=["stgB%d" % (c % 2)], writes=[nm])
            S.dma(lambda e: e.dma_start(out=stgs_[0][0:64, 0:256].rearrange("p (g d) -> p g d", g=4),
                                        in_=w_pool[l].rearrange("g c d -> c g d")), writes=["stgB0"])
            S.op('pool', lambda e: e.tensor_copy(out=wpool_sb, in_=stgs_[0][0:64, 0:256].rearrange("p (g d) -> p g d", g=4)),
                 reads=["stgB0"], writes=["wpool"])
            NI = len(items)
            for idx in range(NI + LA):
                if idx < NI:
                    it_ = items[idx]
                    h, g, r, mk = it_['h'], it_['g'], it_['r'], it_['mk']
                    slot = r * 16 + mk
                    mlo = max(0, mk - 4 * g)
                    cs = slice(mlo * 128, 512)
                    pS, pSr = psb[SBK[idx % 4]], "ps%d" % SBK[idx % 4]
                    pt_, ptr = pT[idx % 4], "pT%d" % (idx % 4)
                    diag = mk >= 4 * g
                    S.op('pe', lambda e: e.matmul(pS[:, cs], KTh[:, slot * 128:(slot + 1) * 128],
                                                  qT[:, h, g * 512 + mlo * 128:(g + 1) * 512], start=True, stop=not diag),
                         reads=["KTh%d_%d" % (r, mk // 4), "qT%d_%d" % (h, g)], writes=[pSr])
                    if diag:
                        mb = mk - 4 * g
                        S.op('pe', lambda e: e.matmul(pS[:, mb * 128:(mb + 1) * 128], negI, notmask[:, r, :], start=False, stop=True),
                             reads=["negI", "notmask"], writes=[pSr])
                    S.op('act', lambda e: e.activation(out=pt_[:, cs], in_=pS[:, cs], func=AF.Exp, scale=scale),
                         reads=[pSr], writes=[ptr])
                if idx >= LA:
                    j_ = idx - LA
                    it_ = items[j_]
                    h, g, r, mk = it_['h'], it_['g'], it_['r'], it_['mk']
                    slot = r * 16 + mk
                    mlo = max(0, mk - 4 * g)
                    cs = slice(mlo * 128, 512)
                    pt_, ptr = pT[j_ % 4], "pT%d" % (j_ % 4)
                    po, por = psb[3 + (it_['hg'] % 2)], "ps%d" % (3 + (it_['hg'] % 2))
                    pd, pdr = psb[5 + (it_['hg'] % 2)], "ps%d" % (5 + (it_['hg'] % 2))
                    S.op('pe', lambda e: e.matmul(po[:, cs], Vh[:, slot, :], pt_[:, cs], start=it_['first'], stop=it_['last']),
                         reads=[ptr, "Vh%d_%d" % (r, mk // 4)], writes=[por])
                    S.op('pe', lambda e: e.matmul(pd[:, cs], ones_bf, pt_[:, cs], start=it_['first'], stop=it_['last']),
                         reads=[ptr, "ones_bf"], writes=[pdr])
                    if it_['lastg'] and mk == 4 * g + 3 and it_['ph'] + 1 < len(phases):
                        load_kv(it_['ph'] + 1, r)
                    if it_['last']:
                        S.op('dve', lambda e: e.reciprocal(out=rden, in_=pd[:, :]), reads=[pdr], writes=["rden"])
                        S.op('dve', lambda e: e.tensor_tensor(out=aT_ap(h)[:, g * 512:(g + 1) * 512], in0=po[:, :], in1=rden, op=ALU.mult),
                             reads=[por, "rden"], writes=[aT_res(h, g)])
            S.barrier()
            if dbg:
                for h in range(4):
                    S.dma(lambda e, h=h: e.dma_start(out=dbgt['aT'][:, h * NTOK:(h + 1) * NTOK], in_=aT_ap(h)), writes=["dbg_aT%d" % h])
                S.barrier()
            tb = Bump(T_LO, TOT)
            sqb = [tb.take([512], BF16) for _ in range(2)]
            rst = [tb.take([512], F32) for _ in range(3)]
            sd = tb.take([16], F32)
            an2 = [tb.take([4, 512], BF16) for _ in range(2)]
            halo2 = [tb.take([4, 256], BF16, parts=64) for _ in range(2)]
            pm2 = [tb.take([4, 128], BF16, parts=64) for _ in range(2)]
            py2 = [tb.take([4, 512], F32, parts=64) for _ in range(2)]
            pn2 = [tb.take([4, 512], BF16, parts=64) for _ in range(2)]
            for i_ in range(2):
                S.op('pool', lambda e: e.memset(halo2[i_], 0.0), writes=["halo%d_%d" % (i_, r) for r in range(4)])
            SQI = [0]
            RSI = [0]
            PMI = [0]

            def sq_next():
                i = SQI[0]
                SQI[0] = (i + 1) % 2
                return sqb[i], "sqb%d" % i

            def rst_next():
                i = RSI[0]
                RSI[0] = (i + 1) % 3
                return rst[i], "rst%d" % i

            def chain(g):
                par = g % 2
                an, halo, py, pn = an2[par], halo2[par], py2[par], pn2[par]
                anr, pyr, pnr = "an%d" % par, "py%d" % par, "pn%d" % par
                gs = slice(g * 512, (g + 1) * 512)
                pss, pssr = ps_next()
                for h in range(4):
                    sq, sqr = sq_next()
                    S.op('act', lambda e: e.activation(out=sq, in_=aT_ap(h)[:, gs], func=AF.Square),
                         reads=[aT_res(h, g)], writes=[sqr])
                    S.op('pe', lambda e: e.matmul(pss[:, :], ones_bf, sq, start=(h == 0), stop=(h == 3)),
                         reads=[sqr, "ones_bf"], writes=[pssr])
                yield
                ra, rar = rst_next()
                rstd_from(pss[:, :], pssr, 512.0, 128, sd, "sd", ra, rar, 512)
                for h in range(4):
                    S.op('dve', lambda e: e.scalar_tensor_tensor(out=an[:, h, :], in0=aT_ap(h)[:, gs], scalar=vcol('goa', h),
                                                                 in1=ra, op0=ALU.mult, op1=ALU.mult),
                         reads=[aT_res(h, g), rar, "vec"], writes=[anr])
                for r in range(4):
                    hsrc = gath[4][r * 32:(r + 1) * 32, :].rearrange("(m r2) (q d) -> (r2 q) m d", m=NBLK, q=8)
                    m0 = 4 * g if r < 3 else 4 * g - 1
                    b0_ = 0
                    if m0 < 0:
                        m0, b0_ = 0, 1
                    nb = 4 - b0_
                    S.dma(lambda e: e.dma_start(out=halo[16 * r:16 * r + 16, b0_:4, :], in_=hsrc[:, m0:m0 + nb, :]),
                          reads=["gath4"], writes=["halo%d_%d" % (par, r)])
                for b in range(4):
                    m = 4 * g + b
                    var = 0 if m == 0 else 1
                    ppm, ppmr = ps_next()
                    for gi in range(4):
                        S.op('pe', lambda e: e.matmul(ppm[0:64, gi * 128:(gi + 1) * 128], pin[:, m, gi * 64:(gi + 1) * 64],
                                                      apool[:, var, gi, :], start=True, stop=False),
                             reads=["pin%d" % m, "apool"], writes=[ppmr])
                        S.op('pe', lambda e: e.matmul(ppm[0:64, gi * 128:(gi + 1) * 128], halo[:, b, gi * 64:(gi + 1) * 64],
                                                      ahalo[:, gi, :], start=False, stop=True),
                             reads=["halo%d_%d" % (par, r) for r in range(4)] + ["ahalo"], writes=[ppmr])
                    yield
                    pmi = PMI[0] % 2
                    PMI[0] += 1
                    pm, pmr = pm2[pmi], "pm%d" % pmi
                    S.op('act', lambda e: e.copy(out=pm, in_=ppm[0:64, :].rearrange("p (g t) -> p g t", g=4)),
                         reads=[ppmr], writes=[pmr])
                    ppy, ppyr = ps_next()
                    for gi in range(4):
                        S.op('pe', lambda e: e.matmul(ppy[0:64, gi * 128:(gi + 1) * 128], wpool_sb[:, gi, :], pm[:, gi, :],
                                                      start=True, stop=True),
                             reads=["wpool", pmr], writes=[ppyr])
                    for gi in range(4):
                        S.op('dve', lambda e: e.tensor_scalar(out=py[:, gi, b * 128:(b + 1) * 128], in0=ppy[0:64, gi * 128:(gi + 1) * 128],
                                                              scalar1=vcol('psc', gi, 64), scalar2=None, op0=ALU.mult),
                             reads=[ppyr, "vec"], writes=[pyr])
                    yield
                pss, pssr = ps_next()
                for gi in range(4):
                    sq, sqr = sq_next()
                    S.op('act', lambda e: e.activation(out=sq[0:64, :], in_=py[:, gi, :], func=AF.Square),
                         reads=[pyr], writes=[sqr])
                    S.op('pe', lambda e: e.matmul(pss[0:64, :], ones_bf[0:64, 0:64], sq[0:64, :], start=(gi == 0), stop=(gi == 3)),
                         reads=[sqr, "ones_bf"], writes=[pssr])
                yield
                rp, rpr = rst_next()
                rstd_from(pss[0:64, :], pssr, 256.0, 64, sd[0:64, :], "sd", rp[0:64, :], rpr, 512)
                for gi in range(4):
                    S.op('dve', lambda e: e.scalar_tensor_tensor(out=pn[:, gi, :], in0=py[:, gi, :], scalar=vcol('gop', gi, 64),
                                                                 in1=rp[0:64, :], op0=ALU.mult, op1=ALU.mult),
                         reads=[pyr, rpr, "vec"], writes=[pnr])

            def wout(g):
                par = g % 2
                an, pn = an2[par], pn2[par]
                anr, pnr = "an%d" % par, "pn%d" % par
                gs = slice(g * 512, (g + 1) * 512)
                for dt_ in range(8):
                    pyo, pyor = ps_next()
                    ds = slice(dt_ * 128, (dt_ + 1) * 128)
                    for c in range(4):
                        S.op('pe', lambda e: e.matmul(pyo[:, :], wo_a[:, c, ds], an[:, c, :], start=(c == 0), stop=False),
                             reads=["wo_a", anr], writes=[pyor])
                    yield
                    for c in range(4):
                        S.op('pe', lambda e: e.matmul(pyo[:, :], wo_g[:, c, ds], gn[:, c, gs], start=False, stop=False),
                             reads=["wo_g"] + ["gn%d" % (4 * g + b) for b in range(4)], writes=[pyor])
                    for c in range(4):
                        S.op('pe', lambda e: e.matmul(pyo[:, :], wo_p[:, c, ds], pn[:, c, :], start=False, stop=(c == 3)),
                             reads=["wo_p", pnr], writes=[pyor])
                    S.op('dve', lambda e: e.tensor_tensor(out=xT[:, dt_, gs], in0=pyo[:, :], in1=xT[:, dt_, gs], op=ALU.add),
                         reads=[pyor, "x%d_%d" % (dt_, g)], writes=["x%d_%d" % (dt_, g)])
                    yield

            for _ in chain(0):
                pass
            for g in range(NGRP):
                gens = [wout(g)]
                if g + 1 < NGRP:
                    gens.append(chain(g + 1))
                while gens:
                    for gen in list(gens):
                        try:
                            next(gen)
                        except StopIteration:
                            gens.remove(gen)
            S.barrier()

        def stage_C(l, stream_out=False):
            tb = Bump(W1_LO, TOT)
            hF = tb.take([8, NTOK], BF16)
            aF = tb.take([6, NTOK], BF16)
            wd = tb.take([6, D], BF16)
            wg = [tb.take([8, 256], BF16) for _ in range(2)]
            wu = [tb.take([8, 256], BF16) for _ in range(2)]
            stgA = [tb.take([8, 256], F32) for _ in range(2)]
            stgB = tb.take([D], F32)
            sqb = [tb.take([512], BF16) for _ in range(2)]
            rx = tb.take([512], F32)
            rx2 = tb.take([512], F32)
            sd = tb.take([16], F32)
            sg = [tb.take([512], F32) for _ in range(2)]
            pairs = []
            ht0 = 0
            for ch, nt in enumerate(FF_CHUNKS):
                j = 0
                while j < nt:
                    npair = min(2, nt - j)
                    pairs.append((ch, j, ht0 + j, npair))
                    j += npair
                ht0 += nt
            SA = [0]

            def load_pair(pi):
                ch, j, ht, npair = pairs[pi]
                ncol = npair * 128
                b_ = pi % 2
                for (wsrc, wdst, wr) in ((w_gate, wg[b_], "wg%d" % b_), (w_up, wu[b_], "wu%d" % b_)):
                    si = SA[0] % 2
                    SA[0] += 1
                    S.dma(lambda e: e.dma_start(out=stgA[si][:, :, 0:ncol],
                                                in_=wsrc[l, :, ht * 128:ht * 128 + ncol].rearrange("(c p) n -> p c n", p=128)),
                          writes=["stgA%d" % si])
                    S.op('act', lambda e: e.copy(out=wdst[:, :, 0:ncol], in_=stgA[si][:, :, 0:ncol]),
                         reads=["stgA%d" % si], writes=[wr])

            def load_wd(ch):
                base = sum(FF_CHUNKS[:ch])
                for j in range(FF_CHUNKS[ch]):
                    ht = base + j
                    S.dma(lambda e: e.dma_start(out=stgB, in_=w_down[l, ht * 128:(ht + 1) * 128, :]), writes=["stgB"])
                    S.op('pool', lambda e: e.tensor_copy(out=wd[:, j, :], in_=stgB), reads=["stgB"], writes=["wd"])

            load_pair(0)
            load_pair(1)
            load_wd(0)
            def ffn_norm(g):
                gs = slice(g * 512, (g + 1) * 512)
                pss, pssr = ps_next()
                for c in range(8):
                    sq, sqr = sqb[c % 2], "sqb%d" % (c % 2)
                    S.op('act', lambda e: e.activation(out=sq, in_=xT[:, c, gs], func=AF.Square),
                         reads=["x%d_%d" % (c, g)], writes=[sqr])
                    S.op('pe', lambda e: e.matmul(pss[:, :], ones_bf, sq, start=(c == 0), stop=(c == 7)),
                         reads=[sqr, "ones_bf"], writes=[pssr])
                rxg, rxgr = (rx, "rx") if g % 2 == 0 else (rx2, "rx2")
                rstd_from(pss[:, :], pssr, 1024.0, 128, sd, "sd", rxg, rxgr, 512)
                for c in range(8):
                    S.op('dve', lambda e: e.scalar_tensor_tensor(out=hF[:, c, gs], in0=xT[:, c, gs], scalar=vcol('gffn', c),
                                                                 in1=rxg, op0=ALU.mult, op1=ALU.mult),
                         reads=["x%d_%d" % (c, g), rxgr, "vec"], writes=["hF%d" % g])
            SG = [0]
            for pi, (ch, j, ht, npair) in enumerate(pairs):
                b_ = pi % 2
                wgb, wub = wg[b_], wu[b_]
                wgr, wur = "wg%d" % b_, "wu%d" % b_
                for jj in range(npair):
                    for g in range(NGRP):
                        if pi == 0 and jj == 0:
                            if g == 0:
                                ffn_norm(0)
                                ffn_norm(1)
                            elif g + 1 < NGRP:
                                ffn_norm(g + 1)
                        gs = slice(g * 512, (g + 1) * 512)
                        pg, pgr = ps_next()
                        pu, pur = ps_next()
                        for c in range(8):
                            S.op('pe', lambda e: e.matmul(pg[:, :], wgb[:, c, jj * 128:(jj + 1) * 128], hF[:, c, gs],
                                                          start=(c == 0), stop=(c == 7)),
                                 reads=[wgr, "hF%d" % g], writes=[pgr])
                        for c in range(8):
                            S.op('pe', lambda e: e.matmul(pu[:, :], wub[:, c, jj * 128:(jj + 1) * 128], hF[:, c, gs],
                                                          start=(c == 0), stop=(c == 7)),
                                 reads=[wur, "hF%d" % g], writes=[pur])
                        sgi = SG[0] % 2
                        SG[0] += 1
                        S.op('act', lambda e: e.activation(out=sg[sgi], in_=pg[:, :], func=AF.Silu), reads=[pgr], writes=["sg%d" % sgi])
                        S.op('dve', lambda e: e.tensor_tensor(out=aF[:, j + jj, gs], in0=pu[:, :], in1=sg[sgi], op=ALU.mult),
                             reads=[pur, "sg%d" % sgi], writes=["aF%d" % g])
                if pi + 2 < len(pairs):
                    load_pair(pi + 2)
                last_in_chunk = (pi + 1 == len(pairs)) or (pairs[pi + 1][0] != ch)
                if last_in_chunk:
                    nt = FF_CHUNKS[ch]
                    for g in range(NGRP):
                        gs = slice(g * 512, (g + 1) * 512)
                        for dt_ in range(8):
                            pyo, pyor = ps_next()
                            for jd in range(nt):
                                S.op('pe', lambda e: e.matmul(pyo[:, :], wd[:, jd, dt_ * 128:(dt_ + 1) * 128], aF[:, jd, gs],
                                                              start=(jd == 0), stop=(jd == nt - 1)),
                                     reads=["wd", "aF%d" % g], writes=[pyor])
                            S.op('dve', lambda e: e.tensor_tensor(out=xT[:, dt_, gs], in0=pyo[:, :], in1=xT[:, dt_, gs], op=ALU.add),
                                 reads=[pyor, "x%d_%d" % (dt_, g)], writes=["x%d_%d" % (dt_, g)])
                            if stream_out and ch + 1 == len(FF_CHUNKS):
                                S.dma(lambda e: e.dma_start(out=xout[dt_ * 128:(dt_ + 1) * 128, gs], in_=xT[:, dt_, gs]),
                                      reads=["x%d_%d" % (dt_, g)], writes=["xout_%d_%d" % (dt_, g)])
                    if ch + 1 < len(FF_CHUNKS):
                        load_wd(ch + 1)
            S.barrier()

        first_A = True
        cur_l = None
        streamed = False
        for st in stages:
            if st == 'nopay':
                continue
            kind, l = st[0], int(st[1])
            if cur_l != l:
                load_vecs(l)
                cur_l = l
            if kind == 'A':
                wp = (l in pay) and not (first_A and 'nopay' in stages)
                stage_A(l, wp)
                first_A = False
            elif kind == 'X':
                stage_X(l)
            elif kind == 'B':
                stage_B(l)
            elif kind == 'C':
                is_last = (st == [s for s in stages if s != 'nopay'][-1])
                stage_C(l, stream_out=is_last)
                streamed = is_last
        if xout is not None and not streamed:
            S.dma(lambda e: e.dma_start(out=xout.rearrange("(c p) t -> p c t", p=128), in_=xT),
                  reads=["x%d_%d" % (c, g) for c in range(8) for g in range(NGRP)], writes=["xout"])
        if dbg and 'x' in dbgt and xout is None:
            S.dma(lambda e: e.dma_start(out=dbgt['x'].rearrange("(c p) t -> p c t", p=128), in_=xT),
                  reads=["x%d_%d" % (c, g) for c in range(8) for g in range(NGRP)], writes=["dbgx"])
        S.finish()

        with nc.Block() as block:
            @block.sync
            def _(eng):
                S.replay('sp', eng, sems)

            @block.scalar
            def _(eng):
                S.replay('act', eng, sems)

            @block.vector
            def _(eng):
                S.replay('dve', eng, sems)

            @block.gpsimd
            def _(eng):
                S.replay('pool', eng, sems)

            @block.tensor
            def _(eng):
                S.replay('pe', eng, sems)
    return nc


def _consts(j):
    cst = np.zeros((128, 2048), np.float32)
    cst[:, 0:128] = np.eye(128, dtype=np.float32)
    k = np.arange(128)[:, None]
    q = np.arange(128)[None, :]
    triu = (k <= q).astype(np.float32)
    cst[:, 128:256] = triu
    mask = np.zeros((128, 4, 128), np.float32)
    for r in range(4):
        if r < j:
            mask[:, r, :] = 1.0
        elif r == j:
            mask[:, r, :] = triu
    cst[:, 256:768] = mask.reshape(128, 512)
    apool = np.zeros((128, 2, 4, 128), np.float32)
    ahalo = np.zeros((64, 4, 128), np.float32)
    own = (j - 1) % 4
    for gi, w in enumerate((2, 4, 8, 16)):
        for t in range(128):
            for var in range(2):
                first = (var == 0 and j == 0)
                cntv = float(min(t + 1, w)) if first else float(w)
                for s_ in range(max(0, t - w + 1), t + 1):
                    apool[s_, var, gi, t] += 1.0 / cntv
                apool[t, var, gi, t] -= 1.0
            for sp in range(16):
                tt = sp - 16
                if tt > t - w:
                    ahalo[own * 16 + sp, gi, t] = 1.0 / float(w)
    cst[:, 768:1792] = apool.reshape(128, 1024)
    sel = np.zeros((64, 2, 96), np.float32)
    for i in range(32):
        sel[i, 0, 64 + i] = 1.0
        sel[32 + i, 1, 64 + i] = 1.0
    cst[0:64, 1792:1984] = sel.reshape(64, 192)
    half = 16
    inv_freq = (1.0 / (np.float32(10000.0) ** (np.arange(half, dtype=np.float32) / np.float32(half)))).astype(np.float32)
    invf = np.zeros(128, np.float32)
    invf[64:80] = inv_freq
    invf[80:96] = inv_freq
    cst[:, 1984] = invf
    return cst, ahalo.reshape(64, 512)


def _vecs(inp):
    v = np.zeros((2, 128, NV), np.float32)
    for l in range(2):
        v[l, :, 0:8] = inp["g_mix_norm"][l].reshape(8, 128).T
        v[l, :, 8:16] = inp["g_ffn_norm"][l].reshape(8, 128).T
        v[l, :, 16:18] = inp["g_q_lat"][l].reshape(2, 128).T
        v[l, :, 18] = inp["g_kv_lat"][l]
        for (name, c0) in (("g_q_head", 19), ("g_k_head", 21)):
            gq = inp[name][l]
            v[l, 0:96, c0] = gq
            v[l, 64:80, c0 + 1] = gq[80:96]
            v[l, 80:96, c0 + 1] = gq[64:80]
        v[l, :, 23:27] = inp["g_out_mla"][l].reshape(4, 128).T
        v[l, 0:64, 27:31] = inp["g_out_sgu"][l].reshape(4, 64).T
        v[l, 0:64, 31:35] = inp["g_out_pool"][l].reshape(4, 64).T
        v[l, 0:64, 35:39] = inp["pool_scale"][l].reshape(4, 64).T
    return v


def _tok_index(j):
    return (np.arange(NBLK)[:, None] * 4 + j) * 128 + np.arange(128)[None, :]


def _common_maps(inp):
    f = lambda a: np.ascontiguousarray(np.asarray(a, dtype=np.float32))
    com = {
        "vecs": _vecs(inp),
        "w_in": f(inp["w_in"]), "w_q_up": f(inp["w_q_up"]), "w_kv_up": f(inp["w_kv_up"]),
        "g_sgu_v": f(inp["g_sgu_v"]), "w_spatial": f(inp["w_spatial"]),
        "b_spatial": f(inp["b_spatial"]).reshape(2, 512), "w_pool": f(inp["w_pool"]),
        "w_out": f(inp["w_out"]), "w_gate": f(inp["w_gate"]), "w_up": f(inp["w_up"]), "w_down": f(inp["w_down"]),
    }
    return com


_NC_CACHE = {}


def _get_nc(key, stages, fused=False, dbg=False):
    if key not in _NC_CACHE:
        _NC_CACHE[key] = build(stages, fused=fused, dbg=dbg)
    return _NC_CACHE[key]


def _run(nc, maps):
    res = run_bass_kernel_spmd(nc, maps, core_ids=list(range(8)))
    return res.results


FUSED = True


def kernel(**inp):
    inp = {k: np.asarray(v) for k, v in inp.items()}
    x = inp["x"].astype(np.float32, copy=False)
    positions = inp["positions"].astype(np.int32, copy=False)
    com = _common_maps(inp)
    per = []
    for c in range(8):
        b, j = divmod(c, 4)
        idx = _tok_index(j).reshape(-1)
        cst, c16 = _consts(j)
        per.append({
            "xin": np.ascontiguousarray(x[b, idx, :].T),
            "pos": np.ascontiguousarray(positions[b, idx].reshape(1, NTOK)),
            "cst": cst, "cst16": c16,
        })

    def maps_for(extra):
        out = []
        for c in range(8):
            m = dict(com)
            m.update(per[c])
            m.update(extra[c])
            out.append(m)
        return out

    if FUSED:
        nc = _get_nc("fused", ["A0", "X0", "B0", "C0", "A1", "X1", "B1", "C1"], fused=True)
        res = _run(nc, maps_for([{} for _ in range(8)]))
        outs = [r["xout"] for r in res]
    else:
        def gather(res, l):
            out = []
            for c in range(8):
                b = c // 4
                out.append({"gath%d_%d" % (l, i): np.concatenate([np.asarray(res[4 * b + j]["pay%d_%d" % (l, i)]) for j in range(4)], axis=0)
                            for i in range(5)})
            return out
        nc1 = _get_nc("L1", ["A0"])
        r1 = _run(nc1, maps_for([{} for _ in range(8)]))
        nc2 = _get_nc("L2", ["nopay", "A0", "B0", "C0", "A1"])
        r2 = _run(nc2, maps_for(gather(r1, 0)))
        g1 = gather(r2, 1)
        for c in range(8):
            per[c]["xin"] = np.ascontiguousarray(np.asarray(r2[c]["xout"], dtype=np.float32))
        nc3 = _get_nc("L3", ["nopay", "A1", "B1", "C1"])
        r3 = _run(nc3, maps_for(g1))
        outs = [r["xout"] for r in r3]
    y = np.empty((2, 8192, D), np.float32)
    for c in range(8):
        b, j = divmod(c, 4)
        idx = _tok_index(j).reshape(-1)
        y[b, idx, :] = np.asarray(outs[c], dtype=np.float32).T
    return y
```

```python
import math
import numpy as np
import ml_dtypes
import concourse.bass as bass
import concourse.mybir as mybir
from concourse.bass_utils import run_bass_kernel_spmd

F32 = mybir.dt.float32
BF16 = mybir.dt.bfloat16
I32 = mybir.dt.int32
AF = mybir.ActivationFunctionType
ALU = mybir.AluOpType

D = 1024
NTOK = 2048
NBLK = 16
NGRP = 4
INW = 1184
FFH = 2816
NHT = 22
EPS = 1e-6
PAY_KT = 0
PAY_V = 384
PAY_H = 896
PAY_ROWS = 928
PIECE_ROWS = (224, 224, 224, 224, 32)
NV = 40
FF_CHUNKS = (6, 6, 5, 5)
import os as _os
_SKIP = _os.environ.get('KSKIP', '').split(',')

TWO_PI = 2.0 * math.pi
C1 = 6.28125
C2 = TWO_PI - C1
MAGIC = 12582912.0
PI_SAFE = 3.1415925


class _Rec:
    def __init__(self):
        self.call = None

    def __getattr__(self, name):
        def f(*a, **k):
            self.call = (name, a, k)
            return None
        return f


def _record(fn):
    r = _Rec()
    fn(r)
    assert r.call is not None
    return r.call


class Sched:
    ENG = ('pe', 'act', 'dve', 'pool', 'sp')

    def __init__(self, ndma=16):
        self.q = {e: [] for e in self.ENG}
        self.cnt = {e: 0 for e in self.ENG}
        self.known = {e: {} for e in self.ENG}
        self.w = {}
        self.r = {}
        self.ndma = ndma
        self.dma_val = [0] * ndma
        self.dma_next = 0
        self.extra_val = {}

    def _need(self, e, toks):
        for (s, v) in toks:
            if s == e and e == 'pe':
                continue
            if self.known[e].get(s, 0) < v:
                self.known[e][s] = v
                self.q[e].append(('wait', s, v))

    def _deps(self, reads, writes):
        toks = []
        for r_ in reads:
            if r_ in self.w:
                toks.append(self.w[r_])
        for w_ in writes:
            if w_ in self.w:
                toks.append(self.w[w_])
            toks.extend(self.r.get(w_, {}).items())
        return toks

    def _commit(self, tok, reads, writes):
        for w_ in writes:
            self.w[w_] = tok
            self.r[w_] = {}
        for r_ in reads:
            d = self.r.setdefault(r_, {})
            if d.get(tok[0], 0) < tok[1]:
                d[tok[0]] = tok[1]

    def op(self, e, fn, reads=(), writes=()):
        fn = _record(fn)
        self._need(e, self._deps(reads, writes))
        self.cnt[e] += 1
        tok = (e, self.cnt[e])
        self.q[e].append(('op', fn, e))
        self._commit(tok, reads, writes)

    def dma(self, fn, reads=(), writes=(), q='sp'):
        fn = _record(fn)
        k = self.dma_next
        self.dma_next = (k + 1) % self.ndma
        s = 'dma%d' % k
        toks = self._deps(reads, writes)
        if self.dma_val[k] > 0:
            toks.append((s, self.dma_val[k]))
        self._need(q, toks)
        self.dma_val[k] += 16
        tok = (s, self.dma_val[k])
        self.q[q].append(('dma', fn, s))
        self._commit(tok, reads, writes)

    def special(self, q, semkey, fn, reads=(), writes=()):
        fn = _record(fn)
        toks = self._deps(reads, writes)
        v = self.extra_val.get(semkey, 0)
        if v > 0:
            toks.append((semkey, v))
        self._need(q, toks)
        v += 1
        self.extra_val[semkey] = v
        self.q[q].append(('op', fn, semkey))
        self._commit((semkey, v), reads, writes)

    def barrier(self):
        allt = [(f, self.cnt[f]) for f in self.ENG if self.cnt[f] > 0]
        allt += [('dma%d' % k, v) for k, v in enumerate(self.dma_val) if v > 0]
        for e in self.ENG:
            self._need(e, [t for t in allt if t[0] != e])

    def finish(self):
        allt = [('dma%d' % k, v) for k, v in enumerate(self.dma_val) if v > 0]
        allt += [(f, self.cnt[f]) for f in self.ENG if self.cnt[f] > 0 and f != 'sp']
        allt += list(self.extra_val.items())
        self._need('sp', allt)

    def replay(self, e, eng, sems):
        for item in self.q[e]:
            if item[0] == 'wait':
                eng.wait_ge(sems[item[1]], item[2])
            elif item[0] == 'op':
                getattr(eng, item[1][0])(*item[1][1], **item[1][2]).then_inc(sems[item[2]], 1)
            else:
                getattr(eng, item[1][0])(*item[1][1], **item[1][2]).then_inc(sems[item[2]], 16)


def build(stages, fused=False, dbg=False):
    nc = bass.Bass("TRN2", target_bir_lowering=False)
    S = Sched()

    def din(name, shape, dt=F32):
        return nc.dram_tensor(name, list(shape), dt, kind="ExternalInput").ap()

    def dout(name, shape, dt=F32):
        return nc.dram_tensor(name, list(shape), dt, kind="ExternalOutput").ap()

    xin = din("xin", [D, NTOK])
    pos = din("pos", [1, NTOK], I32)
    cst = din("cst", [128, 2048])
    cst16 = din("cst16", [64, 512])
    vecs = din("vecs", [2, 128, NV])
    w_in = din("w_in", [2, D, INW])
    w_q_up = din("w_q_up", [2, 256, 384])
    w_kv_up = din("w_kv_up", [2, 128, 768])
    g_sgu_v = din("g_sgu_v", [2, 256])
    w_spatial = din("w_spatial", [2, 4, 128, 128])
    b_spatial = din("b_spatial", [2, 512])
    w_pool = din("w_pool", [2, 4, 64, 64])
    w_out = din("w_out", [2, D, D])
    w_gate = din("w_gate", [2, D, FFH])
    w_up = din("w_up", [2, D, FFH])
    w_down = din("w_down", [2, FFH, D])

    layers = sorted({int(s[1]) for s in stages if s[0] in 'ABCX'})
    need_gath = {}
    pay = {}
    for l in layers:
        if fused:
            pay[l] = [nc.dram_tensor("pay%d_%d" % (l, i), [PIECE_ROWS[i], 2048], BF16, kind="Internal").ap() for i in range(5)]
            need_gath[l] = [nc.dram_tensor("gath%d_%d" % (l, i), [4 * PIECE_ROWS[i], 2048], BF16, kind="Internal").ap() for i in range(5)]
        else:
            if ("A%d" % l) in stages and not (("B%d" % l) in stages):
                pay[l] = [dout("pay%d_%d" % (l, i), [PIECE_ROWS[i], 2048], BF16) for i in range(5)]
            if ("B%d" % l) in stages:
                need_gath[l] = [din("gath%d_%d" % (l, i), [4 * PIECE_ROWS[i], 2048], BF16) for i in range(5)]
    xout = dout("xout", [D, NTOK]) if any(s[0] == 'C' for s in stages) else None
    dbgt = {}
    if dbg:
        dbgt['qT'] = dout("dbg_qT", [96, 4 * NTOK], BF16)
        dbgt['gn'] = dout("dbg_gn", [64, 4 * NTOK], BF16)
        dbgt['aT'] = dout("dbg_aT", [128, 4 * NTOK], BF16)
        dbgt['x'] = dout("dbg_x", [D, NTOK])

    from contextlib import ExitStack
    with ExitStack() as es:
        arena_cols = 212000 // 4
        arena = es.enter_context(nc.sbuf_tensor("arena", [128, arena_cols], F32))
        psb = [es.enter_context(nc.psum_tensor("ps%d" % i, [128, 512], F32)) for i in range(8)]
        sems = {}
        for e in Sched.ENG:
            sems[e] = es.enter_context(nc.semaphore("sem_" + e))
        for k in range(S.ndma):
            sems['dma%d' % k] = es.enter_context(nc.semaphore("sem_dma%d" % k))
        for i in range(5):
            sems['cc%d' % i] = es.enter_context(nc.semaphore("sem_cc%d" % i))

        class Bump:
            def __init__(self, lo, hi):
                self.lo, self.hi, self.p = lo, hi, lo

            def take(self, shape, dt, parts=128):
                esz = 4 if dt in (F32, I32) else 2
                n = 1
                for s_ in shape:
                    n *= s_
                nbytes = (n * esz + 63) // 64 * 64
                off = self.p
                self.p += nbytes
                assert self.p <= self.hi, ("SBUF overflow", self.p, self.hi)
                assert (n * esz) % 4 == 0
                ap = arena[0:parts, off // 4:(off + n * esz) // 4]
                if dt != F32:
                    ap = ap.bitcast(dt)
                if len(shape) == 2:
                    ap = ap.rearrange("p (a b) -> p a b", a=shape[0])
                elif len(shape) == 3:
                    ap = ap.rearrange("p (a b c) -> p a b c", a=shape[0], b=shape[1])
                return ap

        TOT = arena_cols * 4
        pers = Bump(0, TOT)
        xT = pers.take([8, NTOK], F32)
        ones_bf = pers.take([128], BF16)
        ident32 = pers.take([128], F32)
        triu32 = pers.take([128], F32)
        maskT = pers.take([4, 128], BF16)
        apool = pers.take([2, 4, 128], BF16)
        ahalo = pers.take([4, 128], BF16, parts=64)
        selpe = pers.take([2, 96], BF16, parts=64)
        ones32 = pers.take([64], F32, parts=1)
        invf = pers.take([1], F32, parts=96)
        vec_sb = pers.take([NV], F32)
        gv_bc = pers.take([256], F32)
        brow32 = pers.take([512], F32, parts=1)
        epsb = pers.take([1], F32)
        negI = pers.take([128], BF16)
        notmask = pers.take([4, 128], BF16)
        Ctab = pers.take([NTOK], F32, parts=96)
        Stab = pers.take([NTOK], F32, parts=96)
        W1_LO = pers.p
        W1_SZ = 26624
        MIX_LO = W1_LO + W1_SZ
        MIX_SZ = 16384 + 4096 + 16384 + 8192
        T_LO = MIX_LO + MIX_SZ
        assert T_LO < TOT
        wA = Bump(W1_LO, W1_LO + W1_SZ)
        w_in_sb = wA.take([8, 1216], BF16)
        wq_sb = wA.take([2, 384], BF16)
        wqR_sb = wA.take([2, 384], BF16)
        wkv_sb = wA.take([768], BF16)
        wk1_sb = wA.take([4, 96], BF16)
        WsT_sb = wA.take([4, 128], BF16)
        wB = Bump(W1_LO, W1_LO + W1_SZ)
        wpool_sb = wB.take([4, 64], BF16, parts=64)
        wo_a = wB.take([4, D], BF16)
        wo_g = wB.take([4, D], BF16, parts=64)
        wo_p = wB.take([4, D], BF16, parts=64)
        mx = Bump(MIX_LO, MIX_LO + MIX_SZ)
        qT = mx.take([4, NTOK], BF16, parts=96)
        aT0 = mx.take([NTOK], BF16)
        gn = mx.take([4, NTOK], BF16, parts=64)
        pin = mx.take([NBLK, 256], BF16)
        qT_off = MIX_LO

        def aT_ap(h):
            if h == 0:
                return aT0
            off = qT_off + (h - 1) * NTOK * 2
            return arena[0:128, off // 4:(off + NTOK * 2) // 4].bitcast(BF16)

        def aT_res(h, g):
            return "aT%d_%d" % (h, g) if h == 0 else "qT%d_%d" % (h - 1, g)

        PSN = [0]

        def ps_next():
            i = PSN[0]
            PSN[0] = (i + 1) % 8
            return psb[i], "ps%d" % i

        inv_sqrt = {}

        def rstd_from(ps_ap, pres, n, P, sd_ap, sdres, out_ap, ores, cols):
            S.op('act', lambda e: e.activation(out=out_ap, in_=ps_ap, func=AF.Ln, bias=epsb[0:P, 0:1], scale=1.0 / n),
                 reads=[pres, "epsb"], writes=[ores])
            S.op('act', lambda e: e.activation(out=out_ap, in_=out_ap, func=AF.Exp, scale=-0.5), reads=[ores], writes=[ores])

        tb = Bump(MIX_LO, MIX_LO + MIX_SZ)
        st_c = tb.take([2048], F32)
        st_c16 = tb.take([512], F32, parts=64)
        S.dma(lambda e: e.dma_start(out=st_c, in_=cst), writes=["st_c"])
        S.dma(lambda e: e.dma_start(out=st_c16, in_=cst16), writes=["st_c16"])
        S.dma(lambda e: e.dma_start(out=xT, in_=xin.rearrange("(c p) t -> p c t", p=128)),
              writes=["x%d_%d" % (c, g) for c in range(8) for g in range(NGRP)])
        S.op('pool', lambda e: e.memset(ones_bf, 1.0), writes=["ones_bf"])
        S.op('pool', lambda e: e.memset(ones32, 1.0), writes=["ones32"])
        S.op('pool', lambda e: e.memset(epsb, float(EPS)), writes=["epsb"])
        S.op('dve', lambda e: e.tensor_scalar(out=negI, in0=st_c[:, 0:128], scalar1=-30000.0, scalar2=None, op0=ALU.mult),
             reads=["st_c"], writes=["negI"])
        S.op('dve', lambda e: e.tensor_scalar(out=notmask, in0=st_c[:, 256:768].rearrange("p (a b) -> p a b", a=4),
                                              scalar1=-1.0, scalar2=1.0, op0=ALU.mult, op1=ALU.add),
             reads=["st_c"], writes=["notmask"])
        S.op('dve', lambda e: e.tensor_copy(out=ident32, in_=st_c[:, 0:128]), reads=["st_c"], writes=["ident32"])
        S.op('dve', lambda e: e.tensor_copy(out=triu32, in_=st_c[:, 128:256]), reads=["st_c"], writes=["triu32"])
        S.op('dve', lambda e: e.tensor_copy(out=maskT, in_=st_c[:, 256:768].rearrange("p (a b) -> p a b", a=4)),
             reads=["st_c"], writes=["maskT"])
        S.op('dve', lambda e: e.tensor_copy(out=apool, in_=st_c[:, 768:1792].rearrange("p (a b c) -> p a b c", a=2, b=4)),
             reads=["st_c"], writes=["apool"])
        S.op('dve', lambda e: e.tensor_copy(out=selpe, in_=st_c[0:64, 1792:1984].rearrange("p (a b) -> p a b", a=2)),
             reads=["st_c"], writes=["selpe"])
        S.op('dve', lambda e: e.tensor_copy(out=invf, in_=st_c[0:96, 1984:1985]), reads=["st_c"], writes=["invf"])
        S.op('dve', lambda e: e.tensor_copy(out=ahalo, in_=st_c16.rearrange("p (a b) -> p a b", a=4)),
             reads=["st_c16"], writes=["ahalo"])
        posi = tb.take([NTOK], I32, parts=96)
        ang = tb.take([NTOK], F32, parts=96)
        tq = tb.take([NTOK], F32, parts=96)
        kq = tb.take([NTOK], F32, parts=96)
        S.dma(lambda e: e.dma_start(out=posi, in_=pos[0].partition_broadcast(96)), writes=["posi"])
        S.op('dve', lambda e: e.tensor_copy(out=ang, in_=posi), reads=["posi"], writes=["ang"])
        S.op('dve', lambda e: e.tensor_scalar(out=ang, in0=ang, scalar1=invf[:, 0:1], scalar2=None, op0=ALU.mult),
             reads=["ang", "invf"], writes=["ang"])
        for (tab, tres, shift, post) in ((Stab, "Stab", 0.0, 0.0), (Ctab, "Ctab", 0.25, math.pi / 2)):
            S.op('dve', lambda e, shift=shift: e.tensor_scalar(out=tq, in0=ang, scalar1=1.0 / TWO_PI, scalar2=shift,
                                                               op0=ALU.mult, op1=ALU.add),
                 reads=["ang"], writes=["tq"])
            S.op('dve', lambda e: e.tensor_scalar(out=kq, in0=tq, scalar1=MAGIC, scalar2=None, op0=ALU.add),
                 reads=["tq"], writes=["kq"])
            S.op('dve', lambda e: e.tensor_scalar(out=kq, in0=kq, scalar1=MAGIC, scalar2=None, op0=ALU.subtract),
                 reads=["kq"], writes=["kq"])
            S.op('dve', lambda e: e.scalar_tensor_tensor(out=tq, in0=kq, scalar=-C1, in1=ang, op0=ALU.mult, op1=ALU.add),
                 reads=["kq", "ang"], writes=["tq"])
            S.op('dve', lambda e, post=post: e.scalar_tensor_tensor(out=tq, in0=kq, scalar=-C2, in1=tq, op0=ALU.mult, op1=ALU.add),
                 reads=["kq", "tq"], writes=["tq"])
            if post != 0.0:
                S.op('dve', lambda e, post=post: e.tensor_scalar(out=tq, in0=tq, scalar1=post, scalar2=None, op0=ALU.add),
                     reads=["tq"], writes=["tq"])
            S.op('dve', lambda e: e.tensor_scalar(out=tq, in0=tq, scalar1=-PI_SAFE, scalar2=PI_SAFE, op0=ALU.max, op1=ALU.min),
                 reads=["tq"], writes=["tq"])
            S.op('act', lambda e, tab=tab: e.activation(out=tab, in_=tq, func=AF.Sin), reads=["tq"], writes=[tres])

        def load_vecs(l):
            S.dma(lambda e: e.dma_start(out=vec_sb, in_=vecs[l]), writes=["vec"])
            S.dma(lambda e: e.dma_start(out=gv_bc, in_=g_sgu_v[l].partition_broadcast(128)), writes=["gv_bc"])
            S.dma(lambda e: e.dma_start(out=brow32, in_=b_spatial[l:l + 1, :]), writes=["brow32"])

        VC = dict(gmix=0, gffn=8, gql=16, gkv=18, gq=19, gqp=20, gk=21, gkp=22, goa=23, gog=27, gop=31, psc=35)

        def vcol(name, i=0, P=128):
            c = VC[name] + i
            return vec_sb[0:P, c:c + 1]

        def issue_cc(l, pieces):
            for i in pieces:
                if i < 4:
                    rd = ["pay_kt%d_%d" % (h, i) for h in range(4)] + ["pay_v%d_%d" % (h, m) for h in range(4) for m in range(4 * i, 4 * i + 4)]
                else:
                    rd = ["pay_h%d" % m for m in range(NBLK)]
                S.special('pool', 'cc%d' % i, lambda e: e.collective_compute(
                    "AllGather", ALU.bypass, replica_groups=[[0, 1, 2, 3], [4, 5, 6, 7]],
                    ins=[pay[l][i].opt()], outs=[need_gath[l][i].opt()]), reads=rd, writes=["gath%d" % i])

        def stage_A(l, write_pay):
            tb = Bump(T_LO, TOT)
            stgs = [tb.take([INW], F32) for _ in range(4)]
            stg = stgs[0]
            for c in range(8):
                sg_, sgr = stgs[c % 4], ("stg" if c % 4 == 0 else "stg_%d" % (c % 4))
                S.dma(lambda e: e.dma_start(out=sg_, in_=w_in[l, c * 128:(c + 1) * 128, :]), writes=[sgr])
                S.op('dve', lambda e: e.tensor_copy(out=w_in_sb[:, c, 0:416], in_=sg_[:, 0:416]),
                     reads=[sgr], writes=["w_in%d" % c])
                S.op('pool', lambda e: e.tensor_scalar(out=w_in_sb[:, c, 416:432], in0=sg_[:, 400:416], scalar1=-1.0,
                                                       scalar2=None, op0=ALU.mult),
                     reads=[sgr], writes=["w_in%d" % c])
                S.op('pool', lambda e: e.tensor_copy(out=w_in_sb[:, c, 432:448], in_=sg_[:, 384:400]),
                     reads=[sgr], writes=["w_in%d" % c])
                S.op('act', lambda e: e.copy(out=w_in_sb[:, c, 448:1216], in_=sg_[:, 416:1184]),
                     reads=[sgr], writes=["w_in%d" % c])
            stq = stg[:, 0:768].rearrange("p (a b) -> p a b", a=2)
            S.dma(lambda e: e.dma_start(out=stq, in_=w_q_up[l].rearrange("(c p) n -> p c n", p=128)), writes=["stg"])
            S.op('pool', lambda e: e.tensor_copy(out=wq_sb, in_=stq), reads=["stg"], writes=["wq"])
            S.op('pool', lambda e: e.memset(wqR_sb, 0.0), writes=["wqR"])
            stq4 = stg[:, 0:768].rearrange("p (a b) -> p a b", a=8)
            wqR4 = wqR_sb.rearrange("p c (h d) -> p (c h) d", h=4)
            S.op('pool', lambda e: e.tensor_scalar(out=wqR4[:, :, 64:80], in0=stq4[:, :, 80:96], scalar1=-1.0, scalar2=None,
                                                   op0=ALU.mult), reads=["stg"], writes=["wqR"])
            S.op('pool', lambda e: e.tensor_copy(out=wqR4[:, :, 80:96], in_=stq4[:, :, 64:80]), reads=["stg"], writes=["wqR"])
            S.dma(lambda e: e.dma_start(out=stg[:, 0:768], in_=w_kv_up[l]), writes=["stg"])
            S.op('pool', lambda e: e.tensor_copy(out=wkv_sb, in_=stg[:, 0:768]), reads=["stg"], writes=["wkv"])
            S.op('pool', lambda e: e.memset(wk1_sb, 0.0), writes=["wk1"])
            S.op('pool', lambda e: e.tensor_copy(out=wk1_sb[:, :, 0:64],
                                                 in_=stg[:, 0:768].rearrange("p (h d) -> p h d", h=4)[:, :, 0:64]),
                 reads=["stg"], writes=["wk1"])
            for h in range(4):
                sgh, sghr = stgs[h], ("stg" if h == 0 else "stg_%d" % h)
                S.dma(lambda e: e.dma_start(out=sgh[:, 0:128], in_=w_spatial[l, h]), writes=[sghr])
                pt, pr = ps_next()
                S.op('pe', lambda e: e.transpose(out=pt[:, 0:128], in_=sgh[:, 0:128], identity=ident32),
                     reads=[sghr, "ident32"], writes=[pr])
                S.op('dve', lambda e, pt=pt, h=h: e.tensor_tensor(out=WsT_sb[:, h, :], in0=pt[:, 0:128], in1=triu32, op=ALU.mult),
                     reads=[pr, "triu32"], writes=["WsT"])
            S.barrier()
            tb = Bump(T_LO, TOT)
            sqb = [tb.take([512], BF16) for _ in range(2)]
            hT = tb.take([8, 512], BF16)
            rst = [tb.take([512], F32) for _ in range(2)]
            sd = tb.take([16], F32)
            qlat_n = tb.take([2, 512], BF16)
            kvlat_n = tb.take([512], BF16)
            kpe_sb = tb.take([512], BF16, parts=64)
            uT = tb.take([4, 512], BF16, parts=64)
            gm = tb.take([4, 512], F32, parts=64)
            v_n = tb.take([4, 256], BF16)
            t1 = tb.take([512], F32, parts=96)
            t2 = tb.take([512], F32, parts=96)
            rxb = [tb.take([512], F32) for _ in range(2)]
            t2k = t2
            KTg = tb.take([4, 512], BF16, parts=96)
            Vb = tb.take([512], BF16)
            ssv = tb.take([12], F32)
            SQI = [0]
            RSI = [0]

            def sq_next():
                i = SQI[0]
                SQI[0] = (i + 1) % 2
                return sqb[i], "sqb%d" % i

            def rst_next():
                i = RSI[0]
                RSI[0] = (i + 1) % 2
                return rst[i], "rst%d" % i

            for g in range(NGRP):
                gs = slice(g * 512, (g + 1) * 512)
                def x_stats(g2):
                    gs2 = slice(g2 * 512, (g2 + 1) * 512)
                    pss, pssr = ps_next()
                    for c in range(8):
                        sq, sqr = sq_next()
                        S.op('act', lambda e: e.activation(out=sq, in_=xT[:, c, gs2], func=AF.Square),
                             reads=["x%d_%d" % (c, g2)], writes=[sqr])
                        S.op('pe', lambda e: e.matmul(pss[:, :], ones_bf, sq, start=(c == 0), stop=(c == 7)),
                             reads=[sqr, "ones_bf"], writes=[pssr])
                    rstd_from(pss[:, :], pssr, 1024.0, 128, sd, "sd", rxb[g2 % 2], "rxb%d" % (g2 % 2), 512)

                if g == 0:
                    x_stats(0)
                rx, rxr = rxb[g % 2], "rxb%d" % (g % 2)
                for c in range(8):
                    S.op('dve', lambda e: e.scalar_tensor_tensor(out=hT[:, c, :], in0=xT[:, c, gs], scalar=vcol('gmix', c),
                                                                 in1=rx, op0=ALU.mult, op1=ALU.mult),
                         reads=["x%d_%d" % (c, g), rxr, "vec"], writes=["hT%d" % c])
                hres = ["hT%d" % c for c in range(8)]
                wres = ["w_in%d" % c for c in range(8)]

                def proj_ii(col0, M, pt, pr):
                    for c in range(8):
                        S.op('pe', lambda e, c=c: e.matmul(pt[0:M, :], w_in_sb[:, c, col0:col0 + M], hT[:, c, :],
                                                           start=(c == 0), stop=(c == 7)),
                             reads=hres + wres, writes=[pr])

                pq = [ps_next(), ps_next()]
                for mt in range(2):
                    proj_ii(mt * 128, 128, pq[mt][0], pq[mt][1])
                pss, pssr = ps_next()
                for mt in range(2):
                    sq, sqr = sq_next()
                    S.op('act', lambda e, sq=sq, mt=mt: e.activation(out=sq, in_=pq[mt][0][:, :], func=AF.Square),
                         reads=[pq[mt][1]], writes=[sqr])
                    S.op('pe', lambda e, sq=sq, mt=mt, pss=pss: e.matmul(pss[:, :], ones_bf, sq, start=(mt == 0), stop=(mt == 1)),
                         reads=[sqr, "ones_bf"], writes=[pssr])
                rq, rqr = rst_next()
                rstd_from(pss[:, :], pssr, 256.0, 128, sd, "sd", rq, rqr, 512)
                for mt in range(2):
                    S.op('dve', lambda e, mt=mt, rq=rq: e.scalar_tensor_tensor(out=qlat_n[:, mt, :], in0=pq[mt][0][:, :],
                                                                               scalar=vcol('gql', mt), in1=rq,
                                                                               op0=ALU.mult, op1=ALU.mult),
                         reads=[pq[mt][1], rqr, "vec"], writes=["qlat_n"])
                pk, pkr = ps_next()
                proj_ii(256, 128, pk, pkr)
                sq, sqr = sq_next()
                S.op('act', lambda e, sq=sq, pk=pk: e.activation(out=sq, in_=pk[:, :], func=AF.Square), reads=[pkr], writes=[sqr])
                pss, pssr = ps_next()
                S.op('pe', lambda e, sq=sq, pss=pss: e.matmul(pss[:, :], ones_bf, sq, start=True, stop=True),
                     reads=[sqr, "ones_bf"], writes=[pssr])
                rk, rkr = rst_next()
                rstd_from(pss[:, :], pssr, 128.0, 128, sd, "sd", rk, rkr, 512)
                S.op('dve', lambda e, pk=pk, rk=rk: e.scalar_tensor_tensor(out=kvlat_n, in0=pk[:, :], scalar=vcol('gkv'), in1=rk,
                                                                           op0=ALU.mult, op1=ALU.mult),
                     reads=[pkr, rkr, "vec"], writes=["kvlat_n"])
                pp, ppr = ps_next()
                proj_ii(384, 64, pp, ppr)
                S.op('act', lambda e, pp=pp: e.copy(out=kpe_sb, in_=pp[0:64, :]), reads=[ppr], writes=["kpe"])
                for h in range(4):
                    pu, pur = ps_next()
                    proj_ii(448 + h * 64, 64, pu, pur)
                    S.op('act', lambda e, pu=pu, h=h: e.copy(out=uT[:, h, :], in_=pu[0:64, :]), reads=[pur], writes=["uT"])
                pvs = []
                for b in range(4):
                    m = 4 * g + b
                    pv, pvr = ps_next()
                    pvs.append((pv, pvr))
                    for c in range(8):
                        S.op('pe', lambda e: e.matmul(pv[:, :], hT[:, c, b * 128:(b + 1) * 128], w_in_sb[:, c, 704:1216],
                                                      start=(c == 0), stop=(c == 7)),
                             reads=hres + wres, writes=[pvr])
                    sq, sqr = sq_next()
                    S.op('act', lambda e: e.activation(out=sq[:, 0:256], in_=pv[:, 0:256], func=AF.Square, accum_out=ssv[:, b:b + 1]),
                         reads=[pvr], writes=[sqr, "ssv"])
                    S.op('act', lambda e: e.copy(out=pin[:, m, :], in_=pv[:, 256:512]), reads=[pvr], writes=["pin%d" % m])
                S.op('act', lambda e: e.activation(out=ssv[:, 4:8], in_=ssv[:, 0:4], func=AF.Ln, bias=epsb[:, 0:1], scale=1.0 / 256),
                     reads=["ssv", "epsb"], writes=["ssv1"])
                S.op('act', lambda e: e.activation(out=ssv[:, 8:12], in_=ssv[:, 4:8], func=AF.Exp, scale=-0.5), reads=["ssv1"], writes=["ssv2"])
                for b in range(4):
                    pv, pvr = pvs[b]
                    S.op('dve', lambda e: e.scalar_tensor_tensor(out=v_n[:, b, :], in0=pv[:, 0:256], scalar=ssv[:, 8 + b:9 + b], in1=gv_bc,
                                                                 op0=ALU.mult, op1=ALU.mult),
                         reads=[pvr, "ssv2", "gv_bc"], writes=["v_n%d" % b])
                for b in range(4):
                    pz, pzr = ps_next()
                    for h in range(4):
                        S.op('pe', lambda e: e.matmul(pz[0:64, h * 128:(h + 1) * 128], v_n[:, b, h * 64:(h + 1) * 64],
                                                      WsT_sb[:, h, :], start=True, stop=False),
                             reads=["v_n%d" % b, "WsT"], writes=[pzr])
                        S.op('pe', lambda e: e.matmul(pz[0:64, h * 128:(h + 1) * 128], ones32[0:1, 0:64],
                                                      brow32[0:1, h * 128:(h + 1) * 128], start=False, stop=True),
                             reads=["ones32", "brow32"], writes=[pzr])
                    S.op('dve', lambda e: e.tensor_tensor(out=gm[:, :, b * 128:(b + 1) * 128],
                                                          in0=pz[0:64, :].rearrange("p (h t) -> p h t", h=4),
                                                          in1=uT[:, :, b * 128:(b + 1) * 128], op=ALU.mult),
                         reads=[pzr, "uT"], writes=["gm"])
                if g + 1 < NGRP:
                    x_stats(g + 1)
                pss, pssr = ps_next()
                for h in range(4):
                    sq, sqr = sq_next()
                    S.op('act', lambda e: e.activation(out=sq[0:64, :], in_=gm[:, h, :], func=AF.Square),
                         reads=["gm"], writes=[sqr])
                    S.op('pe', lambda e: e.matmul(pss[0:64, :], ones_bf[0:64, 0:64], sq[0:64, :], start=(h == 0), stop=(h == 3)),
                         reads=[sqr, "ones_bf"], writes=[pssr])
                rg, rgr = rst_next()
                rstd_from(pss[0:64, :], pssr, 256.0, 64, sd[0:64, :], "sd", rg[0:64, :], rgr, 512)
                for h in range(4):
                    S.op('dve', lambda e: e.scalar_tensor_tensor(out=gn[:, h, gs], in0=gm[:, h, :], scalar=vcol('gog', h, 64),
                                                                 in1=rg[0:64, :], op0=ALU.mult, op1=ALU.mult),
                         reads=["gm", rgr, "vec"], writes=["gn%d" % (4 * g + b) for b in range(4)])
                def head_norm_rope(praw, prawr, pR, pRr, gname, gpname, out_ap, ores, t2_pre=None):
                    sq, sqr = sq_next()
                    S.op('act', lambda e: e.activation(out=sq[0:96, :], in_=praw[0:96, :], func=AF.Square),
                         reads=[prawr], writes=[sqr])
                    pss, pssr = ps_next()
                    S.op('pe', lambda e: e.matmul(pss[0:96, :], ones_bf[0:96, 0:96], sq[0:96, :], start=True, stop=True),
                         reads=[sqr, "ones_bf"], writes=[pssr])
                    rr, rrr = rst_next()
                    rstd_from(pss[0:96, :], pssr, 96.0, 96, sd[0:96, :], "sd", rr[0:96, :], rrr, 512)
                    S.op('dve', lambda e: e.scalar_tensor_tensor(out=t1, in0=praw[0:96, :], scalar=vcol(gname, 0, 96),
                                                                 in1=Ctab[:, gs], op0=ALU.mult, op1=ALU.mult),
                         reads=[prawr, "Ctab", "vec"], writes=["t1"])
                    if t2_pre is None:
                        S.op('dve', lambda e: e.scalar_tensor_tensor(out=t2, in0=pR[0:96, :], scalar=vcol(gpname, 0, 96),
                                                                     in1=Stab[:, gs], op0=ALU.mult, op1=ALU.mult),
                             reads=[pRr, "Stab", "vec"], writes=["t2"])
                        t2u, t2r = t2, "t2"
                    else:
                        t2u, t2r = t2_pre
                    S.op('dve', lambda e: e.tensor_tensor(out=t1, in0=t1, in1=t2u, op=ALU.add), reads=["t1", t2r], writes=["t1"])
                    S.op('dve', lambda e: e.tensor_tensor(out=out_ap, in0=t1, in1=rr[0:96, :], op=ALU.mult),
                         reads=["t1", rrr], writes=[ores])

                for h in range(4):
                    pqa, pqar = ps_next()
                    pqb, pqbr = ps_next()
                    for c in range(2):
                        S.op('pe', lambda e, c=c, h=h, pqa=pqa: e.matmul(pqa[0:96, :], wq_sb[:, c, h * 96:(h + 1) * 96], qlat_n[:, c, :],
                                                                         start=(c == 0), stop=(c == 1)),
                             reads=["wq", "qlat_n"], writes=[pqar])
                    for c in range(2):
                        S.op('pe', lambda e, c=c, h=h, pqb=pqb: e.matmul(pqb[0:96, :], wqR_sb[:, c, h * 96:(h + 1) * 96], qlat_n[:, c, :],
                                                                         start=(c == 0), stop=(c == 1)),
                             reads=["wqR", "qlat_n"], writes=[pqbr])
                    head_norm_rope(pqa, pqar, pqb, pqbr, 'gq', 'gqp', qT[:, h, gs], "qT%d_%d" % (h, g))
                pkR, pkRr = ps_next()
                S.op('pe', lambda e: e.matmul(pkR[0:96, :], selpe[:, 1, :], kpe_sb, start=True, stop=True),
                     reads=["selpe", "kpe"], writes=[pkRr])
                S.op('dve', lambda e: e.scalar_tensor_tensor(out=t2k, in0=pkR[0:96, :], scalar=vcol('gkp', 0, 96),
                                                             in1=Stab[:, gs], op0=ALU.mult, op1=ALU.mult),
                     reads=[pkRr, "Stab", "vec"], writes=["t2"])
                for h in range(4):
                    pka, pkar = ps_next()
                    S.op('pe', lambda e, h=h, pka=pka: e.matmul(pka[0:96, :], wk1_sb[:, h, :], kvlat_n, start=True, stop=False),
                         reads=["wk1", "kvlat_n"], writes=[pkar])
                    S.op('pe', lambda e, h=h, pka=pka: e.matmul(pka[0:96, :], selpe[:, 0, :], kpe_sb, start=False, stop=True),
                         reads=["selpe", "kpe"], writes=[pkar])
                    head_norm_rope(pka, pkar, None, None, 'gk', 'gkp', KTg[:, h, :], "KTg", t2_pre=(t2k, "t2"))
                if write_pay:
                    for h in range(4):
                        kdst = pay[l][g][h * 56:h * 56 + 24, :].rearrange("r (c t) -> (r c) t", c=4)
                        S.dma(lambda e: e.dma_start(out=kdst, in_=KTg[:, h, :]),
                              reads=["KTg"], writes=["pay_kt%d_%d" % (h, g)])
                for b in range(4):
                    m = 4 * g + b
                    pv, pvr = ps_next()
                    S.op('pe', lambda e, pv=pv, b=b: e.matmul(
                        pv[:, :].rearrange("p (h d) -> p h d", h=4), kvlat_n[:, b * 128:(b + 1) * 128],
                        wkv_sb.rearrange("p (h d) -> p h d", h=4)[:, :, 64:192], start=True, stop=True),
                         reads=["kvlat_n", "wkv"], writes=[pvr])
                    S.op('act', lambda e, pv=pv: e.copy(out=Vb, in_=pv[:, :]), reads=[pvr], writes=["Vb"])
                    if write_pay:
                        for h in range(4):
                            dst = pay[l][g][h * 56 + 24 + b * 8:h * 56 + 32 + b * 8, :].rearrange("r (q d) -> (r q) d", q=16)
                            S.dma(lambda e: e.dma_start(out=dst, in_=Vb[:, h * 128:(h + 1) * 128]),
                                  reads=["Vb"], writes=["pay_v%d_%d" % (h, m)])
                        hd = pay[l][4].rearrange("(m r) (q d) -> m (r q) d", m=NBLK, q=8)[m]
                        S.dma(lambda e: e.dma_start(out=hd, in_=pin[112:128, m, :]),
                              reads=["pin%d" % m], writes=["pay_h%d" % m])
                if fused and write_pay:
                    issue_cc(l, [g])
            if fused and write_pay:
                issue_cc(l, [4])
            if dbg:
                S.dma(lambda e: e.dma_start(out=dbgt['qT'], in_=qT.rearrange("p h t -> p (h t)")),
                      reads=["qT%d_%d" % (h, g) for h in range(4) for g in range(4)], writes=["dbg_qT"])
                S.dma(lambda e: e.dma_start(out=dbgt['gn'], in_=gn.rearrange("p h t -> p (h t)")),
                      reads=["gn%d" % m for m in range(NBLK)], writes=["dbg_gn"])
            S.barrier()

        def stage_X(l):
            return

        def stage_B(l):
            gath = need_gath[l]
            tb = Bump(T_LO, TOT)
            KTh = tb.take([4 * NTOK], BF16, parts=96)
            Vh = tb.take([64, 128], BF16)
            pT = [tb.take([512], BF16) for _ in range(4)]
            rden = tb.take([512], F32)
            stgs_ = [tb.take([D], F32) for _ in range(2)]
            scale = 1.0 / math.sqrt(96.0)
            LA = 3
            SBK = (0, 1, 2, 7)
            phases = [(h, (0, 1, 2)) for h in range(4)] + [(h, (3,)) for h in range(4)]
            items = []
            unit = 0
            for pi, (h, qgs) in enumerate(phases):
                for g in qgs:
                    sl = [(r, mk) for r in range(4) for mk in range(4 * g + 4)]
                    for si, (r, mk) in enumerate(sl):
                        items.append(dict(h=h, g=g, r=r, mk=mk, first=(si == 0), last=(si == len(sl) - 1), hg=unit, ph=pi,
                                          lastg=(g == qgs[-1])))
                    unit += 1

            def load_kv(pi, r):
                h_, qgs_ = phases[pi]
                for g_ in range(qgs_[-1] + 1):
                    ksrc = gath[g_][r * 224 + h_ * 56:r * 224 + h_ * 56 + 24, :].rearrange("r (c t) -> (r c) t", c=4)
                    S.dma(lambda e: e.dma_start(out=KTh[:, r * NTOK + g_ * 512:r * NTOK + (g_ + 1) * 512], in_=ksrc),
                          reads=["gath%d" % g_], writes=["KTh%d_%d" % (r, g_)])
                    vsrc = gath[g_][r * 224 + h_ * 56 + 24:r * 224 + h_ * 56 + 56, :].rearrange("(m r2) (q d) -> (r2 q) m d", m=4, q=16)
                    S.dma(lambda e: e.dma_start(out=Vh[:, r * 16 + 4 * g_:r * 16 + 4 * g_ + 4, :], in_=vsrc),
                          reads=["gath%d" % g_], writes=["Vh%d_%d" % (r, g_)])

            for r in range(4):
                load_kv(0, r)
            for c in range(4):
                S.dma(lambda e, c=c: e.dma_start(out=stgs_[c % 2], in_=w_out[l, c * 128:(c + 1) * 128, :]), writes=["stgB%d" % (c % 2)])
                S.op('pool', lambda e, c=c: e.tensor_copy(out=wo_a[:, c, :], in_=stgs_[c % 2]), reads=["stgB%d" % (c % 2)], writes=["wo_a"])
            for (wsb, base, nm) in ((wo_g, 512, "wo_g"), (wo_p, 768, "wo_p")):
                for c in range(4):
                    S.dma(lambda e, c=c, base=base: e.dma_start(out=stgs_[c % 2][0:64, :], in_=w_out[l, base + c * 64:base + (c + 1) * 64, :]),
                          writes=["stgB%d" % (c % 2)])
                    S.op('pool', lambda e, c=c, wsb=wsb: e.tensor_copy(out=wsb[:, c, :], in_=stgs_[c % 2][0:64, :]), reads=["stgB%d" % (c % 2)], writes=[nm])
            S.dma(lambda e: e.dma_start(out=stgs_[0][0:64, 0:256].rearrange("p (g d) -> p g d", g=4),
                                        in_=w_pool[l].rearrange("g c d -> c g d")), writes=["stgB0"])
            S.op('pool', lambda e: e.tensor_copy(out=wpool_sb, in_=stgs_[0][0:64, 0:256].rearrange("p (g d) -> p g d", g=4)),
                 reads=["stgB0"], writes=["wpool"])
            NI = len(items)
            for idx in range(NI + LA):
                if idx < NI:
                    it_ = items[idx]
                    h, g, r, mk = it_['h'], it_['g'], it_['r'], it_['mk']
                    slot = r * 16 + mk
                    mlo = max(0, mk - 4 * g)
                    cs = slice(mlo * 128, 512)
                    pS, pSr = psb[SBK[idx % 4]], "ps%d" % SBK[idx % 4]
                    pt_, ptr = pT[idx % 4], "pT%d" % (idx % 4)
                    diag = mk >= 4 * g
                    S.op('pe', lambda e: e.matmul(pS[:, cs], KTh[:, slot * 128:(slot + 1) * 128],
                                                  qT[:, h, g * 512 + mlo * 128:(g + 1) * 512], start=True, stop=not diag),
                         reads=["KTh%d_%d" % (r, mk // 4), "qT%d_%d" % (h, g)], writes=[pSr])
                    if diag:
                        mb = mk - 4 * g
                        S.op('pe', lambda e: e.matmul(pS[:, mb * 128:(mb + 1) * 128], negI, notmask[:, r, :], start=False, stop=True),
                             reads=["negI", "notmask"], writes=[pSr])
                    S.op('act', lambda e: e.activation(out=pt_[:, cs], in_=pS[:, cs], func=AF.Exp, scale=scale),
                         reads=[pSr], writes=[ptr])
                if idx >= LA:
                    j_ = idx - LA
                    it_ = items[j_]
                    h, g, r, mk = it_['h'], it_['g'], it_['r'], it_['mk']
                    slot = r * 16 + mk
                    mlo = max(0, mk - 4 * g)
                    cs = slice(mlo * 128, 512)
                    pt_, ptr = pT[j_ % 4], "pT%d" % (j_ % 4)
                    po, por = psb[3 + (it_['hg'] % 2)], "ps%d" % (3 + (it_['hg'] % 2))
                    pd, pdr = psb[5 + (it_['hg'] % 2)], "ps%d" % (5 + (it_['hg'] % 2))
                    S.op('pe', lambda e: e.matmul(po[:, cs], Vh[:, slot, :], pt_[:, cs], start=it_['first'], stop=it_['last']),
                         reads=[ptr, "Vh%d_%d" % (r, mk // 4)], writes=[por])
                    S.op('pe', lambda e: e.matmul(pd[:, cs], ones_bf, pt_[:, cs], start=it_['first'], stop=it_['last']),
                         reads=[ptr, "ones_bf"], writes=[pdr])
                    if it_['lastg'] and mk == 4 * g + 3 and it_['ph'] + 1 < len(phases):
                        load_kv(it_['ph'] + 1, r)
                    if it_['last']:
                        S.op('dve', lambda e: e.reciprocal(out=rden, in_=pd[:, :]), reads=[pdr], writes=["rden"])
                        S.op('dve', lambda e: e.tensor_tensor(out=aT_ap(h)[:, g * 512:(g + 1) * 512], in0=po[:, :], in1=rden, op=ALU.mult),
                             reads=[por, "rden"], writes=[aT_res(h, g)])
            S.barrier()
            if dbg:
                for h in range(4):
                    S.dma(lambda e, h=h: e.dma_start(out=dbgt['aT'][:, h * NTOK:(h + 1) * NTOK], in_=aT_ap(h)), writes=["dbg_aT%d" % h])
                S.barrier()
            tb = Bump(T_LO, TOT)
            sqb = [tb.take([512], BF16) for _ in range(2)]
            rst = [tb.take([512], F32) for _ in range(3)]
            sd = tb.take([16], F32)
            an2 = [tb.take([4, 512], BF16) for _ in range(2)]
            halo2 = [tb.take([4, 256], BF16, parts=64) for _ in range(2)]
            pm2 = [tb.take([4, 128], BF16, parts=64) for _ in range(2)]
            py2 = [tb.take([4, 512], F32, parts=64) for _ in range(2)]
            pn2 = [tb.take([4, 512], BF16, parts=64) for _ in range(2)]
            for i_ in range(2):
                S.op('pool', lambda e: e.memset(halo2[i_], 0.0), writes=["halo%d_%d" % (i_, r) for r in range(4)])
            SQI = [0]
            RSI = [0]
            PMI = [0]

            def sq_next():
                i = SQI[0]
                SQI[0] = (i + 1) % 2
                return sqb[i], "sqb%d" % i

            def rst_next():
                i = RSI[0]
                RSI[0] = (i + 1) % 3
                return rst[i], "rst%d" % i

            def chain(g):
                par = g % 2
                an, halo, py, pn = an2[par], halo2[par], py2[par], pn2[par]
                anr, pyr, pnr = "an%d" % par, "py%d" % par, "pn%d" % par
                gs = slice(g * 512, (g + 1) * 512)
                pss, pssr = ps_next()
                for h in range(4):
                    sq, sqr = sq_next()
                    S.op('act', lambda e: e.activation(out=sq, in_=aT_ap(h)[:, gs], func=AF.Square),
                         reads=[aT_res(h, g)], writes=[sqr])
                    S.op('pe', lambda e: e.matmul(pss[:, :], ones_bf, sq, start=(h == 0), stop=(h == 3)),
                         reads=[sqr, "ones_bf"], writes=[pssr])
                yield
                ra, rar = rst_next()
                rstd_from(pss[:, :], pssr, 512.0, 128, sd, "sd", ra, rar, 512)
                for h in range(4):
                    S.op('dve', lambda e: e.scalar_tensor_tensor(out=an[:, h, :], in0=aT_ap(h)[:, gs], scalar=vcol('goa', h),
                                                                 in1=ra, op0=ALU.mult, op1=ALU.mult),
                         reads=[aT_res(h, g), rar, "vec"], writes=[anr])
                for r in range(4):
                    hsrc = gath[4][r * 32:(r + 1) * 32, :].rearrange("(m r2) (q d) -> (r2 q) m d", m=NBLK, q=8)
                    m0 = 4 * g if r < 3 else 4 * g - 1
                    b0_ = 0
                    if m0 < 0:
                        m0, b0_ = 0, 1
                    nb = 4 - b0_
                    S.dma(lambda e: e.dma_start(out=halo[16 * r:16 * r + 16, b0_:4, :], in_=hsrc[:, m0:m0 + nb, :]),
                          reads=["gath4"], writes=["halo%d_%d" % (par, r)])
                for b in range(4):
                    m = 4 * g + b
                    var = 0 if m == 0 else 1
                    ppm, ppmr = ps_next()
                    for gi in range(4):
                        S.op('pe', lambda e: e.matmul(ppm[0:64, gi * 128:(gi + 1) * 128], pin[:, m, gi * 64:(gi + 1) * 64],
                                                      apool[:, var, gi, :], start=True, stop=False),
                             reads=["pin%d" % m, "apool"], writes=[ppmr])
                        S.op('pe', lambda e: e.matmul(ppm[0:64, gi * 128:(gi + 1) * 128], halo[:, b, gi * 64:(gi + 1) * 64],
                                                      ahalo[:, gi, :], start=False, stop=True),
                             reads=["halo%d_%d" % (par, r) for r in range(4)] + ["ahalo"], writes=[ppmr])
                    yield
                    pmi = PMI[0] % 2
                    PMI[0] += 1
                    pm, pmr = pm2[pmi], "pm%d" % pmi
                    S.op('act', lambda e: e.copy(out=pm, in_=ppm[0:64, :].rearrange("p (g t) -> p g t", g=4)),
                         reads=[ppmr], writes=[pmr])
                    ppy, ppyr = ps_next()
                    for gi in range(4):
                        S.op('pe', lambda e: e.matmul(ppy[0:64, gi * 128:(gi + 1) * 128], wpool_sb[:, gi, :], pm[:, gi, :],
                                                      start=True, stop=True),
                             reads=["wpool", pmr], writes=[ppyr])
                    for gi in range(4):
                        S.op('dve', lambda e: e.tensor_scalar(out=py[:, gi, b * 128:(b + 1) * 128], in0=ppy[0:64, gi * 128:(gi + 1) * 128],
                                                              scalar1=vcol('psc', gi, 64), scalar2=None, op0=ALU.mult),
                             reads=[ppyr, "vec"], writes=[pyr])
                    yield
                pss, pssr = ps_next()
                for gi in range(4):
                    sq, sqr = sq_next()
                    S.op('act', lambda e: e.activation(out=sq[0:64, :], in_=py[:, gi, :], func=AF.Square),
                         reads=[pyr], writes=[sqr])
                    S.op('pe', lambda e: e.matmul(pss[0:64, :], ones_bf[0:64, 0:64], sq[0:64, :], start=(gi == 0), stop=(gi == 3)),
                         reads=[sqr, "ones_bf"], writes=[pssr])
                yield
                rp, rpr = rst_next()
                rstd_from(pss[0:64, :], pssr, 256.0, 64, sd[0:64, :], "sd", rp[0:64, :], rpr, 512)
                for gi in range(4):
                    S.op('dve', lambda e: e.scalar_tensor_tensor(out=pn[:, gi, :], in0=py[:, gi, :], scalar=vcol('gop', gi, 64),
                                                                 in1=rp[0:64, :], op0=ALU.mult, op1=ALU.mult),
                         reads=[pyr, rpr, "vec"], writes=[pnr])

            def wout(g):
                par = g % 2
                an, pn = an2[par], pn2[par]
                anr, pnr = "an%d" % par, "pn%d" % par
                gs = slice(g * 512, (g + 1) * 512)
                for dt_ in range(8):
                    pyo, pyor = ps_next()
                    ds = slice(dt_ * 128, (dt_ + 1) * 128)
                    for c in range(4):
                        S.op('pe', lambda e: e.matmul(pyo[:, :], wo_a[:, c, ds], an[:, c, :], start=(c == 0), stop=False),
                             reads=["wo_a", anr], writes=[pyor])
                    for c in range(4):
                        S.op('pe', lambda e: e.matmul(pyo[:, :], wo_g[:, c, ds], gn[:, c, gs], start=False, stop=False),
                             reads=["wo_g"] + ["gn%d" % (4 * g + b) for b in range(4)], writes=[pyor])
                    for c in range(4):
                        S.op('pe', lambda e: e.matmul(pyo[:, :], wo_p[:, c, ds], pn[:, c, :], start=False, stop=(c == 3)),
                             reads=["wo_p", pnr], writes=[pyor])
                    S.op('dve', lambda e: e.tensor_tensor(out=xT[:, dt_, gs], in0=pyo[:, :], in1=xT[:, dt_, gs], op=ALU.add),
                         reads=[pyor, "x%d_%d" % (dt_, g)], writes=["x%d_%d" % (dt_, g)])
                    yield

            for _ in chain(0):
                pass
            for g in range(NGRP):
                gens = [wout(g)]
                if g + 1 < NGRP:
                    gens.append(chain(g + 1))
                while gens:
                    for gen in list(gens):
                        try:
                            next(gen)
                        except StopIteration:
                            gens.remove(gen)
            S.barrier()

        def stage_C(l, stream_out=False):
            tb = Bump(W1_LO, TOT)
            hF = tb.take([8, NTOK], BF16)
            aF = tb.take([6, NTOK], BF16)
            wd = tb.take([6, D], BF16)
            wg = [tb.take([8, 256], BF16) for _ in range(2)]
            wu = [tb.take([8, 256], BF16) for _ in range(2)]
            stgA = [tb.take([8, 256], F32) for _ in range(2)]
            stgB = tb.take([D], F32)
            sqb = [tb.take([512], BF16) for _ in range(2)]
            rx = tb.take([512], F32)
            rx2 = tb.take([512], F32)
            sd = tb.take([16], F32)
            sg = [tb.take([512], F32) for _ in range(2)]
            pairs = []
            ht0 = 0
            for ch, nt in enumerate(FF_CHUNKS):
                j = 0
                while j < nt:
                    npair = min(2, nt - j)
                    pairs.append((ch, j, ht0 + j, npair))
                    j += npair
                ht0 += nt
            SA = [0]

            def load_pair(pi):
                ch, j, ht, npair = pairs[pi]
                ncol = npair * 128
                b_ = pi % 2
                for (wsrc, wdst, wr) in ((w_gate, wg[b_], "wg%d" % b_), (w_up, wu[b_], "wu%d" % b_)):
                    si = SA[0] % 2
                    SA[0] += 1
                    S.dma(lambda e: e.dma_start(out=stgA[si][:, :, 0:ncol],
                                                in_=wsrc[l, :, ht * 128:ht * 128 + ncol].rearrange("(c p) n -> p c n", p=128)),
                          writes=["stgA%d" % si])
                    S.op('act', lambda e: e.copy(out=wdst[:, :, 0:ncol], in_=stgA[si][:, :, 0:ncol]),
                         reads=["stgA%d" % si], writes=[wr])

            def load_wd(ch):
                base = sum(FF_CHUNKS[:ch])
                for j in range(FF_CHUNKS[ch]):
                    ht = base + j
                    S.dma(lambda e: e.dma_start(out=stgB, in_=w_down[l, ht * 128:(ht + 1) * 128, :]), writes=["stgB"])
                    S.op('pool', lambda e: e.tensor_copy(out=wd[:, j, :], in_=stgB), reads=["stgB"], writes=["wd"])

            load_pair(0)
            load_pair(1)
            load_wd(0)
            def ffn_norm(g):
                gs = slice(g * 512, (g + 1) * 512)
                pss, pssr = ps_next()
                for c in range(8):
                    sq, sqr = sqb[c % 2], "sqb%d" % (c % 2)
                    S.op('act', lambda e: e.activation(out=sq, in_=xT[:, c, gs], func=AF.Square),
                         reads=["x%d_%d" % (c, g)], writes=[sqr])
                    S.op('pe', lambda e: e.matmul(pss[:, :], ones_bf, sq, start=(c == 0), stop=(c == 7)),
                         reads=[sqr, "ones_bf"], writes=[pssr])
                rxg, rxgr = (rx, "rx") if g % 2 == 0 else (rx2, "rx2")
                rstd_from(pss[:, :], pssr, 1024.0, 128, sd, "sd", rxg, rxgr, 512)
                for c in range(8):
                    S.op('dve', lambda e: e.scalar_tensor_tensor(out=hF[:, c, gs], in0=xT[:, c, gs], scalar=vcol('gffn', c),
                                                                 in1=rxg, op0=ALU.mult, op1=ALU.mult),
                         reads=["x%d_%d" % (c, g), rxgr, "vec"], writes=["hF%d" % g])
            SG = [0]
            for pi, (ch, j, ht, npair) in enumerate(pairs):
                b_ = pi % 2
                wgb, wub = wg[b_], wu[b_]
                wgr, wur = "wg%d" % b_, "wu%d" % b_
                for jj in range(npair):
                    for g in range(NGRP):
                        if pi == 0 and jj == 0:
                            if g == 0:
                                ffn_norm(0)
                                ffn_norm(1)
                            elif g + 1 < NGRP:
                                ffn_norm(g + 1)
                        gs = slice(g * 512, (g + 1) * 512)
                        pg, pgr = ps_next()
                        pu, pur = ps_next()
                        for c in range(8):
                            S.op('pe', lambda e: e.matmul(pg[:, :], wgb[:, c, jj * 128:(jj + 1) * 128], hF[:, c, gs],
                                                          start=(c == 0), stop=(c == 7)),
                                 reads=[wgr, "hF%d" % g], writes=[pgr])
                        for c in range(8):
                            S.op('pe', lambda e: e.matmul(pu[:, :], wub[:, c, jj * 128:(jj + 1) * 128], hF[:, c, gs],
                                                          start=(c == 0), stop=(c == 7)),
                                 reads=[wur, "hF%d" % g], writes=[pur])
                        sgi = SG[0] % 2
                        SG[0] += 1
                        S.op('act', lambda e: e.activation(out=sg[sgi], in_=pg[:, :], func=AF.Silu), reads=[pgr], writes=["sg%d" % sgi])
                        S.op('dve', lambda e: e.tensor_tensor(out=aF[:, j + jj, gs], in0=pu[:, :], in1=sg[sgi], op=ALU.mult),
                             reads=[pur, "sg%d" % sgi], writes=["aF%d" % g])
                if pi + 2 < len(pairs):
                    load_pair(pi + 2)
                last_in_chunk = (pi + 1 == len(pairs)) or (pairs[pi + 1][0] != ch)
                if last_in_chunk:
                    nt = FF_CHUNKS[ch]
                    for g in range(NGRP):
                        gs = slice(g * 512, (g + 1) * 512)
                        for dt_ in range(8):
                            pyo, pyor = ps_next()
                            for jd in range(nt):
                                S.op('pe', lambda e: e.matmul(pyo[:, :], wd[:, jd, dt_ * 128:(dt_ + 1) * 128], aF[:, jd, gs],
                                                              start=(jd == 0), stop=(jd == nt - 1)),
                                     reads=["wd", "aF%d" % g], writes=[pyor])
                            S.op('dve', lambda e: e.tensor_tensor(out=xT[:, dt_, gs], in0=pyo[:, :], in1=xT[:, dt_, gs], op=ALU.add),
                                 reads=[pyor, "x%d_%d" % (dt_, g)], writes=["x%d_%d" % (dt_, g)])
                            if stream_out and ch + 1 == len(FF_CHUNKS):
                                S.dma(lambda e: e.dma_start(out=xout[dt_ * 128:(dt_ + 1) * 128, gs], in_=xT[:, dt_, gs]),
                                      reads=["x%d_%d" % (dt_, g)], writes=["xout_%d_%d" % (dt_, g)])
                    if ch + 1 < len(FF_CHUNKS):
                        load_wd(ch + 1)
            S.barrier()

        first_A = True
        cur_l = None
        streamed = False
        for st in stages:
            if st == 'nopay':
                continue
            kind, l = st[0], int(st[1])
            if cur_l != l:
                load_vecs(l)
                cur_l = l
            if kind == 'A':
                wp = (l in pay) and not (first_A and 'nopay' in stages)
                stage_A(l, wp)
                first_A = False
            elif kind == 'X':
                stage_X(l)
            elif kind == 'B':
                stage_B(l)
            elif kind == 'C':
                is_last = (st == [s for s in stages if s != 'nopay'][-1])
                stage_C(l, stream_out=is_last)
                streamed = is_last
        if xout is not None and not streamed:
            S.dma(lambda e: e.dma_start(out=xout.rearrange("(c p) t -> p c t", p=128), in_=xT),
                  reads=["x%d_%d" % (c, g) for c in range(8) for g in range(NGRP)], writes=["xout"])
        if dbg and 'x' in dbgt and xout is None:
            S.dma(lambda e: e.dma_start(out=dbgt['x'].rearrange("(c p) t -> p c t", p=128), in_=xT),
                  reads=["x%d_%d" % (c, g) for c in range(8) for g in range(NGRP)], writes=["dbgx"])
        S.finish()

        with nc.Block() as block:
            @block.sync
            def _(eng):
                S.replay('sp', eng, sems)

            @block.scalar
            def _(eng):
                S.replay('act', eng, sems)

            @block.vector
            def _(eng):
                S.replay('dve', eng, sems)

            @block.gpsimd
            def _(eng):
                S.replay('pool', eng, sems)

            @block.tensor
            def _(eng):
                S.replay('pe', eng, sems)
    return nc


def _consts(j):
    cst = np.zeros((128, 2048), np.float32)
    cst[:, 0:128] = np.eye(128, dtype=np.float32)
    k = np.arange(128)[:, None]
    q = np.arange(128)[None, :]
    triu = (k <= q).astype(np.float32)
    cst[:, 128:256] = triu
    mask = np.zeros((128, 4, 128), np.float32)
    for r in range(4):
        if r < j:
            mask[:, r, :] = 1.0
        elif r == j:
            mask[:, r, :] = triu
    cst[:, 256:768] = mask.reshape(128, 512)
    apool = np.zeros((128, 2, 4, 128), np.float32)
    ahalo = np.zeros((64, 4, 128), np.float32)
    own = (j - 1) % 4
    for gi, w in enumerate((2, 4, 8, 16)):
        for t in range(128):
            for var in range(2):
                first = (var == 0 and j == 0)
                cntv = float(min(t + 1, w)) if first else float(w)
                for s_ in range(max(0, t - w + 1), t + 1):
                    apool[s_, var, gi, t] += 1.0 / cntv
                apool[t, var, gi, t] -= 1.0
            for sp in range(16):
                tt = sp - 16
                if tt > t - w:
                    ahalo[own * 16 + sp, gi, t] = 1.0 / float(w)
    cst[:, 768:1792] = apool.reshape(128, 1024)
    sel = np.zeros((64, 2, 96), np.float32)
    for i in range(32):
        sel[i, 0, 64 + i] = 1.0
        sel[32 + i, 1, 64 + i] = 1.0
    cst[0:64, 1792:1984] = sel.reshape(64, 192)
    half = 16
    inv_freq = (1.0 / (np.float32(10000.0) ** (np.arange(half, dtype=np.float32) / np.float32(half)))).astype(np.float32)
    invf = np.zeros(128, np.float32)
    invf[64:80] = inv_freq
    invf[80:96] = inv_freq
    cst[:, 1984] = invf
    return cst, ahalo.reshape(64, 512)


def _vecs(inp):
    v = np.zeros((2, 128, NV), np.float32)
    for l in range(2):
        v[l, :, 0:8] = inp["g_mix_norm"][l].reshape(8, 128).T
        v[l, :, 8:16] = inp["g_ffn_norm"][l].reshape(8, 128).T
        v[l, :, 16:18] = inp["g_q_lat"][l].reshape(2, 128).T
        v[l, :, 18] = inp["g_kv_lat"][l]
        for (name, c0) in (("g_q_head", 19), ("g_k_head", 21)):
            gq = inp[name][l]
            v[l, 0:96, c0] = gq
            v[l, 64:80, c0 + 1] = gq[80:96]
            v[l, 80:96, c0 + 1] = gq[64:80]
        v[l, :, 23:27] = inp["g_out_mla"][l].reshape(4, 128).T
        v[l, 0:64, 27:31] = inp["g_out_sgu"][l].reshape(4, 64).T
        v[l, 0:64, 31:35] = inp["g_out_pool"][l].reshape(4, 64).T
        v[l, 0:64, 35:39] = inp["pool_scale"][l].reshape(4, 64).T
    return v


def _tok_index(j):
    return (np.arange(NBLK)[:, None] * 4 + j) * 128 + np.arange(128)[None, :]


def _common_maps(inp):
    f = lambda a: np.ascontiguousarray(np.asarray(a, dtype=np.float32))
    com = {
        "vecs": _vecs(inp),
        "w_in": f(inp["w_in"]), "w_q_up": f(inp["w_q_up"]), "w_kv_up": f(inp["w_kv_up"]),
        "g_sgu_v": f(inp["g_sgu_v"]), "w_spatial": f(inp["w_spatial"]),
        "b_spatial": f(inp["b_spatial"]).reshape(2, 512), "w_pool": f(inp["w_pool"]),
        "w_out": f(inp["w_out"]), "w_gate": f(inp["w_gate"]), "w_up": f(inp["w_up"]), "w_down": f(inp["w_down"]),
    }
    return com


_NC_CACHE = {}


def _get_nc(key, stages, fused=False, dbg=False):
    if key not in _NC_CACHE:
        _NC_CACHE[key] = build(stages, fused=fused, dbg=dbg)
    return _NC_CACHE[key]


def _run(nc, maps):
    res = run_bass_kernel_spmd(nc, maps, core_ids=list(range(8)))
    return res.results


FUSED = True


def kernel(**inp):
    inp = {k: np.asarray(v) for k, v in inp.items()}
    x = inp["x"].astype(np.float32, copy=False)
    positions = inp["positions"].astype(np.int32, copy=False)
    com = _common_maps(inp)
    per = []
    for c in range(8):
        b, j = divmod(c, 4)
        idx = _tok_index(j).reshape(-1)
        cst, c16 = _consts(j)
        per.append({
            "xin": np.ascontiguousarray(x[b, idx, :].T),
            "pos": np.ascontiguousarray(positions[b, idx].reshape(1, NTOK)),
            "cst": cst, "cst16": c16,
        })

    def maps_for(extra):
        out = []
        for c in range(8):
            m = dict(com)
            m.update(per[c])
            m.update(extra[c])
            out.append(m)
        return out

    if FUSED:
        nc = _get_nc("fused", ["A0", "X0", "B0", "C0", "A1", "X1", "B1", "C1"], fused=True)
        res = _run(nc, maps_for([{} for _ in range(8)]))
        outs = [r["xout"] for r in res]
    else:
        def gather(res, l):
            out = []
            for c in range(8):
                b = c // 4
                out.append({"gath%d_%d" % (l, i): np.concatenate([np.asarray(res[4 * b + j]["pay%d_%d" % (l, i)]) for j in range(4)], axis=0)
                            for i in range(5)})
            return out
        nc1 = _get_nc("L1", ["A0"])
        r1 = _run(nc1, maps_for([{} for _ in range(8)]))
        nc2 = _get_nc("L2", ["nopay", "A0", "B0", "C0", "A1"])
        r2 = _run(nc2, maps_for(gather(r1, 0)))
        g1 = gather(r2, 1)
        for c in range(8):
            per[c]["xin"] = np.ascontiguousarray(np.asarray(r2[c]["xout"], dtype=np.float32))
        nc3 = _get_nc("L3", ["nopay", "A1", "B1", "C1"])
        r3 = _run(nc3, maps_for(g1))
        outs = [r["xout"] for r in r3]
    y = np.empty((2, 8192, D), np.float32)
    for c in range(8):
        b, j = divmod(c, 4)
        idx = _tok_index(j).reshape(-1)
        y[b, idx, :] = np.asarray(outs[c], dtype=np.float32).T
    return y
```
